# Optimizing a Trainium2 kernel written in Bass

```python
import jax, jax.numpy as jnp
from jax import lax
import numpy as np

D_MODEL = 4096
BATCH = 4
SEQ = 4096
DEPTH = 1
DEC_BATCH = 2
DEC_SEQ = 8192
PAST_LEN = 128

HEAD_DIM = 128
MIX_W = D_MODEL
RET_HEADS = (MIX_W // 2) // HEAD_DIM
ATT_HEADS = (MIX_W // 2) // HEAD_DIM
ATT_KV_HEADS = ATT_HEADS // 4
RET_W = RET_HEADS * HEAD_DIM
ATT_Q_W = ATT_HEADS * HEAD_DIM
ATT_KV_W = ATT_KV_HEADS * HEAD_DIM
IN_W = 4 * RET_W + ATT_Q_W + 2 * ATT_KV_W
RET_CHUNK = 128
WINDOW = 128
BLOCK = 128
N_BUCKETS = 32
MAX_DISTANCE = 128
MEM_TOKENS = 256
MEM_HEADS = 4
MEM_W = MEM_HEADS * HEAD_DIM
D_FF = 4 * D_MODEL
ROPE_BASE = 10000.0
RMS_EPS = 1e-6
NEG_INF = -1e30

kernel_name = "hymba_retention_swa_t5bias_encoder"


def rmsnorm(x, g):
    xf = x.astype(jnp.float32)
    var = jnp.mean(xf * xf, axis=-1, keepdims=True)
    return (xf * lax.rsqrt(var + RMS_EPS) * g.astype(jnp.float32)).astype(x.dtype)


def head_rmsnorm(y):
    yf = y.astype(jnp.float32)
    var = jnp.mean(yf * yf, axis=-1, keepdims=True)
    return (yf * lax.rsqrt(var + RMS_EPS)).astype(y.dtype)


def rope(x, pos):
    half = x.shape[-1] // 2
    inv = ROPE_BASE ** (-jnp.arange(half, dtype=jnp.float32) / half)
    ang = pos[:, None] * inv[None, :]
    cos = jnp.cos(ang)[None, :, None, :]
    sin = jnp.sin(ang)[None, :, None, :]
    xf = x.astype(jnp.float32)
    x1, x2 = xf[..., :half], xf[..., half:]
    return jnp.concatenate([x1 * cos - x2 * sin, x1 * sin + x2 * cos], axis=-1).astype(x.dtype)


def chunk_retention(q, k, v, log_g, strict):
    b, s, h, d = q.shape
    dt = q.dtype
    nc = s // RET_CHUNK
    qc = q.reshape(b, nc, RET_CHUNK, h, d)
    kc = k.reshape(b, nc, RET_CHUNK, h, d)
    vc = v.reshape(b, nc, RET_CHUNK, h, d)
    pos = np.arange(RET_CHUNK)
    diff = pos[:, None] - pos[None, :]
    mask = (diff > 0) if strict else (diff >= 0)
    decay = jnp.where(mask[None], jnp.exp(log_g[:, None, None] * np.maximum(diff, 0).astype(np.float32)), 0.0).astype(dt)
    att = jnp.einsum('bcihd,bcjhd->bchij', qc, kc) * decay[None, None]
    y_inner = jnp.einsum('bchij,bcjhd->bcihd', att, vc)
    posf = pos.astype(np.float32)
    zeta = jnp.exp(log_g[:, None] * (RET_CHUNK - 1 - posf)[None, :]).astype(dt)
    xi = jnp.exp(log_g[:, None] * (posf + 1.0)[None, :]).astype(dt)
    chunk_decay = jnp.exp(log_g * RET_CHUNK).astype(dt)
    kv = jnp.einsum('bcjhd,hj,bcjhe->bchde', kc, zeta, vc)

    def step(state, kv_c):
        return state * chunk_decay[None, :, None, None] + kv_c, state

    _, r_prev = lax.scan(step, jnp.zeros((b, h, d, d), dt), jnp.moveaxis(kv, 1, 0))
    r_prev = jnp.moveaxis(r_prev, 0, 1)
    y_cross = jnp.einsum('bcihd,hi,bchde->bcihe', qc, xi, r_prev)
    return (y_inner + y_cross).reshape(b, s, h, d)


def t5_buckets(rel):
    nb = N_BUCKETS // 2
    max_exact = nb // 2
    base = np.where(rel > 0, nb, 0)
    n = np.abs(rel)
    large = max_exact + (np.log(np.maximum(n, 1) / max_exact) / np.log(MAX_DISTANCE / max_exact) * (nb - max_exact)).astype(np.int32)
    large = np.minimum(large, nb - 1)
    return (base + np.where(n < max_exact, n, large)).astype(np.int32)


def windowed_gqa(q, k, v, sink, rel_bias):
    b, s, H, d = q.shape
    KV = k.shape[2]
    G = H // KV
    nb = s // BLOCK
    qb = q.reshape(b, nb, BLOCK, KV, G, d)
    pad = ((0, 0), (BLOCK, BLOCK), (0, 0), (0, 0))
    kp = jnp.pad(k, pad).reshape(b, nb + 2, BLOCK, KV, d)
    vp = jnp.pad(v, pad).reshape(b, nb + 2, BLOCK, KV, d)
    kband = jnp.concatenate([kp[:, :-2], kp[:, 1:-1], kp[:, 2:]], axis=2)
    vband = jnp.concatenate([vp[:, :-2], vp[:, 1:-1], vp[:, 2:]], axis=2)
    qi = np.arange(BLOCK)[:, None]
    kj = np.arange(3 * BLOCK)[None, :] - BLOCK
    rel = kj - qi
    in_window = np.abs(rel) <= WINDOW
    bias = rel_bias.astype(jnp.float32)[t5_buckets(rel)]
    bias = jnp.transpose(bias, (2, 0, 1)).reshape(KV, G, BLOCK, 3 * BLOCK)
    key_pos = jnp.arange(nb)[:, None] * BLOCK + kj
    valid = (key_pos >= 0) & (key_pos < s)
    mask = jnp.asarray(in_window)[None] & valid[:, None, :]
    scores = jnp.einsum('bnqkgd,bnjkd->bnkgqj', qb, kband).astype(jnp.float32) * (HEAD_DIM ** -0.5)
    scores = jnp.where(mask[None, :, None, None], scores + bias[None, None], NEG_INF)
    sink_f = sink.astype(jnp.float32).reshape(KV, G)[None, None, :, :, None]
    m = jnp.maximum(jnp.max(scores, axis=-1), sink_f)
    p = jnp.exp(scores - m[..., None])
    denom = jnp.sum(p, axis=-1) + jnp.exp(sink_f - m)
    p = (p / denom[..., None]).astype(v.dtype)
    out = jnp.einsum('bnkgqj,bnjkd->bnqkgd', p, vband)
    return out.reshape(b, s, H * d)


def parallel_mixer(h, w_in, dec_f, dec_b, sink, rel_bias, w_out):
    b, s, _ = h.shape
    proj = h @ w_in
    splits = [RET_W, 2 * RET_W, 3 * RET_W, 4 * RET_W, 4 * RET_W + ATT_Q_W, 4 * RET_W + ATT_Q_W + ATT_KV_W]
    q_r, k_r, v_r, g_r, q_a, k_a, v_a = jnp.split(proj, splits, axis=-1)
    pos = jnp.arange(s, dtype=jnp.float32)
    q_r = rope(q_r.reshape(b, s, RET_HEADS, HEAD_DIM), pos)
    k_r = rope(k_r.reshape(b, s, RET_HEADS, HEAD_DIM), pos) * (HEAD_DIM ** -0.5)
    v_r = v_r.reshape(b, s, RET_HEADS, HEAD_DIM)
    log_gf = -jnp.exp(dec_f.astype(jnp.float32))
    log_gb = -jnp.exp(dec_b.astype(jnp.float32))
    y_f = chunk_retention(q_r, k_r, v_r, log_gf, strict=False)
    y_b = jnp.flip(chunk_retention(jnp.flip(q_r, 1), jnp.flip(k_r, 1), jnp.flip(v_r, 1), log_gb, strict=True), 1)
    y_ret = head_rmsnorm(y_f + y_b) * jax.nn.silu(g_r.reshape(b, s, RET_HEADS, HEAD_DIM))
    y_ret = y_ret.reshape(b, s, RET_W)
    y_att = windowed_gqa(q_a.reshape(b, s, ATT_HEADS, HEAD_DIM),
                         k_a.reshape(b, s, ATT_KV_HEADS, HEAD_DIM),
                         v_a.reshape(b, s, ATT_KV_HEADS, HEAD_DIM), sink, rel_bias)
    return jnp.concatenate([y_ret, y_att], axis=-1) @ w_out


def memory_cross_attention(h, m, w_cq, w_ckv, w_co):
    b, s, _ = h.shape
    M = m.shape[1]
    q = (h @ w_cq).reshape(b, s, MEM_HEADS, HEAD_DIM)
    k, v = jnp.split(m @ w_ckv, 2, axis=-1)
    k = k.reshape(b, M, MEM_HEADS, HEAD_DIM)
    v = v.reshape(b, M, MEM_HEADS, HEAD_DIM)
    sc = jnp.einsum('bshd,bmhd->bhsm', q, k).astype(jnp.float32) * (HEAD_DIM ** -0.5)
    p = jax.nn.softmax(sc, axis=-1).astype(v.dtype)
    o = jnp.einsum('bhsm,bmhd->bshd', p, v).reshape(b, s, MEM_W)
    return o @ w_co


def squared_relu_mlp(h, w1, w2):
    a = jax.nn.relu(h @ w1)
    return (a * a) @ w2


def encode(x, mem, norm_mix, w_in, ret_decay_f, ret_decay_b, attn_sink, rel_bias, w_out,
           norm_cross, norm_mem, w_cq, w_ckv, w_co, norm_mlp, w_mlp_in, w_mlp_out, norm_final):
    for l in range(DEPTH):
        x = x + parallel_mixer(rmsnorm(x, norm_mix[l]), w_in[l], ret_decay_f[l], ret_decay_b[l],
                               attn_sink[l], rel_bias, w_out[l])
        x = x + memory_cross_attention(rmsnorm(x, norm_cross[l]), rmsnorm(mem, norm_mem[l]),
                                       w_cq[l], w_ckv[l], w_co[l])
        x = x + squared_relu_mlp(rmsnorm(x, norm_mlp[l]), w_mlp_in[l], w_mlp_out[l])
    return rmsnorm(x, norm_final)


def setup_inputs(seed: int = 0) -> dict:
    key = jax.random.key(seed)
    ks = jax.random.split(key, 20)
    f32 = jnp.float32

    def nrm(k, shape, fan_in):
        return jax.random.normal(k, shape, f32) * (fan_in ** -0.5)

    def gain(k, shape):
        return 1.0 + 0.01 * jax.random.normal(k, shape, f32)

    base_decay = jnp.log(-jnp.log(1.0 - 2.0 ** (-5.0 - jnp.arange(RET_HEADS, dtype=f32))))
    return {
        "x_prompt": jax.random.normal(ks[0], (BATCH, SEQ, D_MODEL), f32),
        "x_sample": jax.random.normal(ks[1], (DEC_BATCH, DEC_SEQ, D_MODEL), f32),
        "mem_prompt": jax.random.normal(ks[2], (BATCH, MEM_TOKENS, D_MODEL), f32),
        "mem_sample": jax.random.normal(ks[3], (DEC_BATCH, MEM_TOKENS, D_MODEL), f32),
        "norm_mix": gain(ks[4], (DEPTH, D_MODEL)),
        "w_in": nrm(ks[5], (DEPTH, D_MODEL, IN_W), D_MODEL),
        "ret_decay_f": base_decay[None] + 0.01 * jax.random.normal(ks[6], (DEPTH, RET_HEADS), f32),
        "ret_decay_b": base_decay[None] + 0.01 * jax.random.normal(ks[7], (DEPTH, RET_HEADS), f32),
        "attn_sink": 0.5 * jax.random.normal(ks[8], (DEPTH, ATT_HEADS), f32),
        "rel_bias": 0.5 * jax.random.normal(ks[9], (N_BUCKETS, ATT_HEADS), f32),
        "w_out": nrm(ks[10], (DEPTH, MIX_W, D_MODEL), MIX_W),
        "norm_cross": gain(ks[11], (DEPTH, D_MODEL)),
        "norm_mem": gain(ks[12], (DEPTH, D_MODEL)),
        "w_cq": nrm(ks[13], (DEPTH, D_MODEL, MEM_W), D_MODEL),
        "w_ckv": nrm(ks[14], (DEPTH, D_MODEL, 2 * MEM_W), D_MODEL),
        "w_co": nrm(ks[15], (DEPTH, MEM_W, D_MODEL), MEM_W),
        "norm_mlp": gain(ks[16], (DEPTH, D_MODEL)),
        "w_mlp_in": nrm(ks[17], (DEPTH, D_MODEL, D_FF), D_MODEL),
        "w_mlp_out": nrm(ks[18], (DEPTH, D_FF, D_MODEL), D_FF),
        "norm_final": gain(ks[19], (D_MODEL,)),
    }


def reference(x_prompt, x_sample, mem_prompt, mem_sample, norm_mix, w_in, ret_decay_f, ret_decay_b,
              attn_sink, rel_bias, w_out, norm_cross, norm_mem, w_cq, w_ckv, w_co, norm_mlp,
              w_mlp_in, w_mlp_out, norm_final):
    y_prompt = encode(x_prompt, mem_prompt, norm_mix, w_in, ret_decay_f, ret_decay_b, attn_sink, rel_bias,
                      w_out, norm_cross, norm_mem, w_cq, w_ckv, w_co, norm_mlp, w_mlp_in, w_mlp_out, norm_final)
    y_sample = encode(x_sample, mem_sample, norm_mix, w_in, ret_decay_f, ret_decay_b, attn_sink, rel_bias,
                      w_out, norm_cross, norm_mem, w_cq, w_ckv, w_co, norm_mlp, w_mlp_in, w_mlp_out, norm_final)
    return (y_prompt, y_sample)
```

```python
import numpy as np
import concourse.bass as bass
import concourse.mybir as mybir
from concourse.bass_utils import run_bass_kernel_spmd

F32 = mybir.dt.float32
BF16 = mybir.dt.bfloat16
ALU = mybir.AluOpType
AF = mybir.ActivationFunctionType
AX = mybir.AxisListType
NEG = -1e30
import os as _os
POOLC = _os.environ.get("POOLC", "dve")
EPS = 1e-6


def _c(name, *args, **kwargs):
    return lambda e: getattr(e, name)(*args, **kwargs)


class Op:
    __slots__ = ("eng", "fn", "dma", "deps", "sig", "signaled", "idx", "pre")

    def __init__(self, eng, fn, dma):
        self.eng = eng
        self.fn = fn
        self.dma = dma
        self.deps = {}
        self.sig = None
        self.signaled = False
        self.pre = None


class Sched:
    COMPUTE = ("pe", "act", "dve", "pool")
    NSLOT = {"sp": 10, "act": 4, "pool": 40}

    def __init__(self, nc, same_eng_sync=True):
        self.nc = nc
        self.ops = []
        self.last_w = {}
        self.readers = {}
        self.same_eng_sync = same_eng_sync
        self.bar_deps = None
        self.bar_seen = {}
        self.last_on = {}
        self.last_dma = {}
        self.ndma = {"sp": 0, "act": 0, "pool": 0}

    def add(self, eng, fn, r=(), w=(), dma=False):
        op = Op(eng, fn, dma)
        op.idx = len(self.ops)
        deps = {}
        w = list(w) + [t for t in r if t.startswith("ps") and t not in w]
        for t in r:
            lw = self.last_w.get(t)
            if lw is not None:
                deps[lw] = "raw"
        for t in w:
            lw = self.last_w.get(t)
            if lw is not None:
                deps[lw] = "waw"
            for rd in self.readers.get(t, ()):
                if rd not in deps:
                    deps[rd] = "war"
        key = (eng, dma)
        if self.bar_deps is not None and not self.bar_seen.get(key):
            self.bar_seen[key] = True
            for d in self.bar_deps:
                if d not in deps:
                    deps[d] = "bar"
        for d, kind in deps.items():
            if d is op:
                continue
            if not d.dma and not dma and d.eng == eng:
                if eng == "pe":
                    continue
                if kind == "war" or not self.same_eng_sync:
                    continue
            gk = ("dma", d.eng, d.sig[2]) if d.dma else ("c", d.eng)
            best = op.deps.get(gk)
            if best is None or d.idx > best.idx:
                op.deps[gk] = d
        for d in op.deps.values():
            d.signaled = True
        for t in r:
            self.readers.setdefault(t, []).append(op)
        for t in w:
            self.last_w[t] = op
            self.readers[t] = []
        if dma:
            k = self.ndma[eng]
            self.ndma[eng] = k + 1
            ns = self.NSLOT[eng]
            slot = k % ns
            op.sig = ("dma", eng, slot, 16 * (k // ns + 1))
            op.signaled = True
            op.pre = self.last_dma.get((eng, slot))
            self.last_dma[(eng, slot)] = op
        else:
            self.last_on[eng] = op
        self.ops.append(op)
        return op

    def barrier(self):
        deps = list(self.last_on.values()) + list(self.last_dma.values())
        for d in deps:
            d.signaled = True
        self.bar_deps = deps
        self.bar_seen = {}
        self.last_w = {}
        self.readers = {}

    def emit(self, final_wait_eng="sp"):
        nc = self.nc
        for o in self.last_on.values():
            o.signaled = True
        cnt = {e: 0 for e in self.COMPUTE}
        for op in self.ops:
            if not op.dma and op.signaled:
                cnt[op.eng] += 1
                op.sig = ("c", op.eng, 0, cnt[op.eng])
        sems = {}
        for op in self.ops:
            if op.sig is not None and op.sig[:3] not in sems:
                sems[op.sig[:3]] = nc.alloc_semaphore(name="s_%s_%s_%d" % op.sig[:3])
        streams = {e: [] for e in ("pe", "act", "dve", "pool", "sp")}
        for op in self.ops:
            streams[op.eng].append(op)
        finals = list(self.last_dma.values()) + list(self.last_on.values())
        nwaits = [0]

        def run(ename, e):
            waited = {}

            def wait(sig):
                k = sig[:3]
                if waited.get(k, 0) >= sig[3]:
                    return
                waited[k] = sig[3]
                e.wait_ge(sems[k], sig[3])
                nwaits[0] += 1

            for op in streams[ename]:
                for d in op.deps.values():
                    wait(d.sig)
                if op.dma and op.pre is not None:
                    wait(op.pre.sig)
                ins = op.fn(e)
                if op.signaled:
                    ins.then_inc(sems[op.sig[:3]], 16 if op.dma else 1)
            if ename == final_wait_eng:
                for d in finals:
                    wait(d.sig)

        with nc.Block() as block:
            block.tensor(lambda e: run("pe", e))
            block.scalar(lambda e: run("act", e))
            block.vector(lambda e: run("dve", e))
            block.gpsimd(lambda e: run("pool", e))
            block.sync(lambda e: run("sp", e))
        self.stats = dict(nops=len(self.ops), nwaits=nwaits[0], nsems=len(sems),
                          per_eng={k: len(v) for k, v in streams.items()})


class Cfg:
    def __init__(self, D=4096, N=4096, DFF=16384, T=512, F=512):
        self.D, self.N, self.DFF, self.T, self.F = D, N, DFF, T, F
        self.KC = D // 128
        self.HR = (D // 2) // 128
        self.HA = self.HR
        self.KV = self.HA // 4
        self.RW = self.HR * 128
        self.AQ = self.HA * 128
        self.AKV = self.KV * 128
        self.INW = 4 * self.RW + self.AQ + 2 * self.AKV
        self.MEM = 256
        self.MH = 4
        self.MW = 512
        self.NCH = N // 128


def t5_buckets(rel):
    nb = 16
    max_exact = 8
    base = np.where(rel > 0, nb, 0)
    n = np.abs(rel)
    large = max_exact + (np.log(np.maximum(n, 1) / max_exact) / np.log(128 / max_exact) * (nb - max_exact)).astype(np.int32)
    large = np.minimum(large, nb - 1)
    return (base + np.where(n < max_exact, n, large)).astype(np.int32)


def static_tables():
    st = {}
    st["ident"] = np.eye(128, dtype=np.float32)
    rs = np.zeros((128, 128), np.float32)
    for dp in range(128):
        rs[(dp + 64) % 128, dp] = 1.0
    st["rswap"] = rs
    j = np.arange(128)[:, None].astype(np.float32)
    i = np.arange(128)[None, :].astype(np.float32)
    af = np.where(i >= j, i - j, 1e30).astype(np.float32)
    ab = np.where(j > i, j - i, 1e30).astype(np.float32)
    st["adec"] = np.stack([af, ab], 1)
    jj = np.arange(128).astype(np.float32)
    st["zexp"] = np.stack([127.0 - jj, jj], 1).astype(np.float32)
    ii = np.arange(128).astype(np.float32)
    xi = np.stack([ii + 1.0, 128.0 - ii], 0)
    st["xiexp"] = np.ascontiguousarray(np.broadcast_to(xi[None], (128, 2, 128))).astype(np.float32)
    qi = np.arange(128)[:, None]
    kj = np.arange(384)[None, :] - 128
    rel = kj - qi
    bk = t5_buckets(rel)
    inw = np.abs(rel) <= 128
    eb = np.zeros((33, 128, 384), np.float32)
    for b in range(32):
        eb[b] = ((bk == b) & inw)
    eb[32] = ~inw
    st["ebuck"] = eb
    return st


def build_nc(cfg, debug_outs=(), stop=99):
    c = cfg
    D, N, KC, T, HR, HA, KV, DFF, F = c.D, c.N, c.KC, c.T, c.HR, c.HA, c.KV, c.DFF, c.F
    NCH = c.NCH
    nc = bass.Bass("TRN2", target_bir_lowering=False)

    def din(name, shape, dt=F32):
        return nc.dram_tensor(name, list(shape), dt, kind="ExternalInput").ap()

    def dscr(name, shape, dt=BF16):
        kind = "ExternalOutput" if name in debug_outs else "Internal"
        return nc.dram_tensor(name, list(shape), dt, kind=kind).ap()

    x_own = din("x_own", [N, D])
    x_oth = din("x_oth", [N, D])
    x_halo = din("x_halo", [256, D])
    mem = din("mem", [256, D])
    w_in = din("w_in", [D, c.INW])
    w_out = din("w_out", [D, D])
    w_cq = din("w_cq", [D, 512])
    w_ckv = din("w_ckv", [D, 1024])
    w_co = din("w_co", [512, D])
    w1 = din("w1", [D, DFF])
    w2 = din("w2", [DFF, D])
    gcols_d = din("gcols", [128, 4, KC])
    gfin_d = din("gfin", [128, D])
    cs_own = din("cs_own", [2, 128, N])
    cs_oth = din("cs_oth", [2, 128, N])
    dec_d = din("dec", [128, 2 * HR])
    sink_d = din("sink", [128, HA])
    relb_d = din("relb", [128, 32 * HA])
    flags_d = din("flags", [128, 4])
    ident_d = din("ident", [128, 128])
    rswap_d = din("rswap", [128, 128])
    adec_d = din("adec", [128, 2, 128])
    zexp_d = din("zexp", [128, 2])
    xiexp_d = din("xiexp", [128, 2, 128])
    ebuck_d = din("ebuck", [33, 128, 384])
    y_out = nc.dram_tensor("y", [N, D], F32, kind="ExternalOutput").ap()

    w_in_b = dscr("w_in_b", [D, c.INW])
    w_out_b = dscr("w_out_b", [D, D])
    w_cq_b = dscr("w_cq_b", [D, 512])
    w_ckv_b = dscr("w_ckv_b", [D, 1024])
    w_co_b = dscr("w_co_b", [512, D])
    w1_b = dscr("w1_b", [D, DFF])
    w2_b = dscr("w2_b", [DFF, D])
    qrT = dscr("qrT", [HR, 128, N])
    krT = dscr("krT", [HR, 128, 2 * N])
    vr = dscr("vr", [2 * N, c.RW])
    gT = dscr("gT", [HR, 128, N])
    qaT = dscr("qaT", [HA, 128, N])
    kaT = dscr("kaT", [KV, 128, N + 256])
    va = dscr("va", [N + 256, c.AKV])
    kmT = dscr("kmT", [4, 128, 256])
    vm = dscr("vm", [256, 512])
    ymixT = dscr("ymixT", [2 * HR, 128, N])

    S = Sched(nc)
    A = S.add

    base0 = nc.sbuf_base
    top = nc.sbuf_top
    cur = [(base0 + 63) // 64 * 64]

    def sb(name, shape, dt):
        nbytes = int(np.prod(shape[1:])) * (4 if dt == F32 else 2)
        off = cur[0]
        cur[0] = (off + nbytes + 63) // 64 * 64
        assert cur[0] <= top, ("SBUF overflow", name, cur[0], top)
        return nc.alloc_sbuf_tensor_at(name, list(shape), dt, offset=off).ap()

    ident = sb("ident", [128, 128], BF16)
    rswap = sb("rswap", [128, 128], BF16)
    ones = sb("ones", [128, 128], BF16)
    gcols = sb("gcols", [128, 4, KC], F32)
    flags = sb("flags", [128, 4], F32)
    lg = sb("lg", [128, 2 * HR], F32)
    cdec = sb("cdec", [128, 2 * HR], F32)
    zfb = sb("zfb", [128, 2, HR], F32)
    sink = sb("sink", [128, HA], F32)
    stat = sb("stat", [128, 8], F32)
    tmpf = sb("tmpf", [128, 128], F32)
    persist_end = cur[0]

    psall = nc.alloc_psum_tensor("psall", [128, 4096], F32).ap()
    psum = [psall[:, i * 512:(i + 1) * 512] for i in range(8)]

    def psb(i):
        return psum[i].bitcast(BF16)

    def cast_w(src, dst, name, nsplit):
        rows = src.shape[0]
        rp = rows // nsplit
        for i in range(nsplit):
            A("pool", _c("dma_start", out=dst[i * rp:(i + 1) * rp, :], in_=src[i * rp:(i + 1) * rp, :]),
              w=[name], dma=True)

    cast_w(w_ckv, w_ckv_b, "w_ckv_b", 1)
    cast_w(w_in, w_in_b, "w_in_b", 4)

    def load(eng, dst, src, wtok, rtok=()):
        return A(eng, _c("dma_start", out=dst, in_=src), r=list(rtok), w=list(wtok), dma=True)

    p0 = cur[0]
    identf = sb("identf", [128, 128], F32)
    rswapf = sb("rswapf", [128, 128], F32)
    adec = sb("adec", [128, 2, 128], F32)
    zexp = sb("zexp", [128, 2], F32)
    xiexp = sb("xiexp", [128, 2, 128], F32)
    decs = sb("decs", [128, 2 * HR], F32)
    load("sp", identf, ident_d, ["identf"])
    load("sp", rswapf, rswap_d, ["rswapf"])
    load("sp", gcols, gcols_d, ["gcols"])
    load("sp", flags, flags_d, ["flags"])
    load("sp", decs, dec_d, ["decs"])
    load("sp", sink, sink_d, ["sink"])
    load("sp", adec, adec_d, ["adec"])
    load("sp", zexp, zexp_d, ["zexp"])
    load("sp", xiexp, xiexp_d, ["xiexp"])
    A("dve", _c("tensor_copy", ident, identf), r=["identf"], w=["ident"])
    A("dve", _c("tensor_copy", rswap, rswapf), r=["rswapf"], w=["rswap"])
    A("dve", _c("memset", ones, 1.0), w=["ones"])
    A("act", _c("activation", out=lg, in_=decs, func=AF.Exp), r=["decs"], w=["lg"])
    A("dve", _c("tensor_scalar", out=lg, in0=lg, scalar1=-1.0, scalar2=None, op0=ALU.mult), r=["lg"], w=["lg"])
    A("act", _c("activation", out=cdec, in_=lg, func=AF.Exp, scale=128.0), r=["lg"], w=["cdec"])
    for d_ in range(2):
        A("dve", _c("tensor_scalar", out=zfb[:, d_, :], in0=lg[:, d_ * HR:(d_ + 1) * HR], scalar1=zexp[:, d_:d_ + 1],
                                                 scalar2=None, op0=ALU.mult), r=["lg", "zexp"], w=["zfb"])
    A("act", _c("activation", out=zfb, in_=zfb, func=AF.Exp), r=["zfb"], w=["zfb"])
    S.barrier()
    tables_end = cur[0]
    if stop <= 0:
        S.emit()
        return nc, S

    cur[0] = tables_end
    WP = 512
    hT = sb("hT", [128, KC, T], BF16)
    xbuf = [sb("xbuf%d" % i, [128, D], F32) for i in range(2)]
    hnb = [sb("hn%d" % i, [128, D], BF16) for i in range(2)]
    hcnt = [0]
    junk = sb("junk", [128, D], BF16)
    NWB = 2
    wbuf = [sb("wbuf%d" % i, [128, KC, WP], BF16) for i in range(NWB)]
    cst = sb("cst", [128, 2, T], F32)
    stage = [sb("stage%d" % i, [128, 4, T], BF16) for i in range(2)]
    qsb = [sb("qsb%d" % i, [128, T], BF16) for i in range(2)]
    t1 = [sb("t1_%d" % i, [128, T], F32) for i in range(2)]
    t2 = [sb("t2_%d" % i, [128, T], F32) for i in range(2)]
    cnt = {"x": 0, "w": 0, "stage": 0, "acc": 0, "q": 0, "tp": 0}

    def rms_stats(xt, xtok, dcol):
        A("act", _c("activation", out=junk, in_=xt, func=AF.Square, accum_out=stat[:, dcol:dcol + 1]),
          r=[xtok], w=["junk", "stat%d" % dcol])
        A("dve", _c("tensor_scalar", out=stat[:, dcol:dcol + 1], in0=stat[:, dcol:dcol + 1], scalar1=1.0 / D, scalar2=EPS,
                                           op0=ALU.mult, op1=ALU.add), r=["stat%d" % dcol], w=["stat%d" % dcol])
        A("act", _c("activation", out=stat[:, dcol:dcol + 1], in_=stat[:, dcol:dcol + 1], func=AF.Sqrt),
          r=["stat%d" % dcol], w=["stat%d" % dcol])
        A("dve", _c("reciprocal", stat[:, dcol:dcol + 1], stat[:, dcol:dcol + 1]), r=["stat%d" % dcol], w=["stat%d" % dcol])

    def norm_transpose(xt, xtok, gi, hT_, st, htok, dcol=0):
        rms_stats(xt, xtok, dcol)
        hi = hcnt[0] % 2
        hcnt[0] += 1
        hn = hnb[hi]
        hntok = "hn%d" % hi
        A("dve", _c("tensor_scalar", out=hn, in0=xt, scalar1=stat[:, dcol:dcol + 1], scalar2=None, op0=ALU.mult),
          r=[xtok, "stat%d" % dcol], w=[hntok])
        for g8 in range(KC // 8):
            bk = cnt["tp"] % 2
            cnt["tp"] += 1
            pt = psb(bk)
            for j in range(8):
                kc = g8 * 8 + j
                A("pe", _c("transpose", pt[:, j * 128:(j + 1) * 128], hn[:, kc * 128:(kc + 1) * 128], ident),
                  r=[hntok, "ident"], w=["ps%d" % bk])
            ptv = pt.rearrange("p (k t) -> p k t", k=8)
            gb = gcols[:, gi, g8 * 8:(g8 + 1) * 8].unsqueeze(2).to_broadcast([128, 8, 128])
            eng = "dve" if g8 % 2 == 0 else "pool"
            if eng == "pool":
                eng = "dve"
            A(eng, _c("tensor_tensor", out=hT_[:, g8 * 8:(g8 + 1) * 8, st * 128:(st + 1) * 128], in0=ptv, in1=gb,
                                                                  op=ALU.mult), r=["ps%d" % bk, "gcols"], w=[htok])

    def load_w(wb_ap, c0, W):
        i = cnt["w"] % NWB
        cnt["w"] += 1
        load("sp", wbuf[i][:, :, 0:W], wb_ap[:, c0:c0 + W].rearrange("(kc p) n -> p kc n", p=128), ["wbuf%d" % i], [wb_ap.tensor.name])
        return i

    ptc = [0]
    import os
    PTMAX = int(os.environ.get("PTMAX", "9999"))

    def proj_tile(xsrc, ntok, gi, pieces, cs_src=None):
        ptc[0] += 1
        if ptc[0] > PTMAX:
            return
        nsub = ntok // 128
        for st in range(nsub):
            xi = cnt["x"] % 2
            cnt["x"] += 1
            load("sp", xbuf[xi], xsrc[st * 128:(st + 1) * 128, :], ["xbuf%d" % xi])
            norm_transpose(xbuf[xi], "xbuf%d" % xi, gi, hT, st, "hT", dcol=st % 2)
        if cs_src is not None:
            load("sp", cst[:, :, 0:ntok], cs_src.rearrange("c p t -> p c t"), ["cst"])
        for (w_ap, c0, W, kind, destfn) in pieces:
            wi = load_w(w_ap, c0, W)
            wtok = "wbuf%d" % wi
            nch = W // 128
            si = cnt["stage"] % 2
            cnt["stage"] += 1
            stg = stage[si]
            stok = "stage%d" % si
            if kind == "tm":
                for st in range(nsub):
                    bk = 2 + cnt["acc"] % 3
                    cnt["acc"] += 1
                    for kc in range(KC):
                        A("pe", _c("matmul", psum[bk][:, 0:W], lhsT=hT[:, kc, st * 128:(st + 1) * 128],
                                                                       rhs=wbuf[wi][:, kc, 0:W], start=(kc == 0), stop=(kc == KC - 1)),
                          r=["hT", wtok], w=["ps%d" % bk])
                    sv = stg.rearrange("p a t -> p (a t)")[:, st * W:(st + 1) * W]
                    A("act", _c("activation", out=sv, in_=psum[bk][:, 0:W], func=AF.Copy), r=["ps%d" % bk], w=[stok])
                    A("pool", _c("dma_start", out=destfn(st), in_=sv), r=[stok], w=["scr"], dma=True)
                continue
            for ch in range(nch):
                bk = 2 + cnt["acc"] % 3
                cnt["acc"] += 1
                acc = psum[bk][:, 0:ntok]
                for kc in range(KC):
                    A("pe", _c("matmul", acc, lhsT=wbuf[wi][:, kc, ch * 128:(ch + 1) * 128], rhs=hT[:, kc, 0:ntok],
                                                                     start=(kc == 0), stop=(kc == KC - 1)), r=["hT", wtok], w=["ps%d" % bk])
                so = stg[:, ch, 0:ntok]
                if kind == "copy" or (kind.startswith("rope") and _os.environ.get("NOROPE")):
                    A("act", _c("activation", out=so, in_=acc, func=AF.Copy), r=["ps%d" % bk], w=[stok])
                elif kind == "silu":
                    A("act", _c("activation", out=so, in_=acc, func=AF.Silu), r=["ps%d" % bk], w=[stok])
                else:
                    qi = cnt["q"] % 2
                    cnt["q"] += 1
                    rb = 5 + qi
                    sc = 1.0 if kind == "rope_q" else 128.0 ** -0.5
                    A("act", _c("activation", out=qsb[qi][:, 0:ntok], in_=acc, func=AF.Copy), r=["ps%d" % bk], w=["qsb%d" % qi])
                    A("pe", _c("matmul", psum[rb][:, 0:ntok], lhsT=rswap, rhs=qsb[qi][:, 0:ntok], start=True, stop=True),
                      r=["qsb%d" % qi, "rswap"], w=["ps%d" % rb])
                    A("dve", _c("scalar_tensor_tensor", out=t1[qi][:, 0:ntok], in0=acc, scalar=sc, in1=cst[:, 0, 0:ntok],
                                                                                     op0=ALU.mult, op1=ALU.mult), r=["ps%d" % bk, "cst"], w=["t1_%d" % qi])
                    A("dve", _c("scalar_tensor_tensor", out=t2[qi][:, 0:ntok], in0=psum[rb][:, 0:ntok], scalar=sc,
                                                                                   in1=cst[:, 1, 0:ntok], op0=ALU.mult, op1=ALU.mult),
                      r=["ps%d" % rb, "cst"], w=["t2_%d" % qi])
                    A(POOLC, _c("tensor_tensor", out=so, in0=t1[qi][:, 0:ntok], in1=t2[qi][:, 0:ntok], op=ALU.add),
                      r=["t1_%d" % qi, "t2_%d" % qi], w=[stok])
            for (dst, sl) in destfn(nch):
                A("pool", _c("dma_start", out=dst, in_=stg[:, 0:nch, sl]), r=[stok], w=["scr"], dma=True)

    def fm_dest(arr, h0, t0, ntok):
        def f(nch):
            return [(arr[h0:h0 + nch, :, t0:t0 + ntok].rearrange("h p t -> p h t"), slice(0, ntok))]
        return f

    def pieces_for(colbase, width, kind, mk):
        out = []
        c0 = 0
        while c0 < width:
            W = min(WP, width - c0)
            out.append((w_in_b, colbase + c0, W, kind, mk(c0, W)))
            c0 += W
        return out

    def mem_pieces():
        ps_ = []
        ps_.append((w_ckv_b, 0, 512, "copy", lambda nch: [(kmT[0:4, :, 0:256].rearrange("h p t -> p h t"), slice(0, 256))]))
        ps_.append((w_ckv_b, 512, 512, "tm", lambda st: vm[st * 128:(st + 1) * 128, :]))
        return ps_

    proj_tile(mem, 256, 2, mem_pieces())

    def halo_pieces():
        ps_ = []
        kbase = 4 * c.RW + c.AQ
        vbase = kbase + c.AKV

        def mkk(c0, W):
            h0 = c0 // 128
            return lambda nch: [(kaT[h0:h0 + nch, :, 0:128].rearrange("h p t -> p h t"), slice(0, 128)),
                                (kaT[h0:h0 + nch, :, N + 128:N + 256].rearrange("h p t -> p h t"), slice(128, 256))]

        def mkv(c0, W):
            return lambda st: va[(0 if st == 0 else N + 128):(128 if st == 0 else N + 256), c0:c0 + W]
        ps_ += pieces_for(kbase, c.AKV, "copy", mkk)
        ps_ += pieces_for(vbase, c.AKV, "tm", mkv)
        return ps_

    proj_tile(x_halo, 256, 0, halo_pieces())

    for ti in range(N // T):
        t0 = ti * T

        def mkk(c0, W, t0=t0):
            return fm_dest(krT, c0 // 128, N + t0, T)

        def mkv(c0, W, t0=t0):
            return lambda st: vr[N + t0 + st * 128:N + t0 + (st + 1) * 128, c0:c0 + W]
        pcs = pieces_for(c.RW, c.RW, "rope_k", mkk) + pieces_for(2 * c.RW, c.RW, "tm", mkv)
        proj_tile(x_oth[t0:t0 + T, :], T, 0, pcs, cs_oth[:, :, t0:t0 + T])

    for ti in range(N // T):
        t0 = ti * T
        pcs = []
        pcs += pieces_for(0, c.RW, "rope_q", lambda c0, W, t0=t0: fm_dest(qrT, c0 // 128, t0, T))
        pcs += pieces_for(c.RW, c.RW, "rope_k", lambda c0, W, t0=t0: fm_dest(krT, c0 // 128, t0, T))
        pcs += pieces_for(2 * c.RW, c.RW, "tm", lambda c0, W, t0=t0: (lambda st: vr[t0 + st * 128:t0 + (st + 1) * 128, c0:c0 + W]))
        pcs += pieces_for(3 * c.RW, c.RW, "silu", lambda c0, W, t0=t0: fm_dest(gT, c0 // 128, t0, T))
        pcs += pieces_for(4 * c.RW, c.AQ, "copy", lambda c0, W, t0=t0: fm_dest(qaT, c0 // 128, t0, T))
        pcs += pieces_for(4 * c.RW + c.AQ, c.AKV, "copy", lambda c0, W, t0=t0: fm_dest(kaT, c0 // 128, 128 + t0, T))
        pcs += pieces_for(4 * c.RW + c.AQ + c.AKV, c.AKV, "tm",
                          lambda c0, W, t0=t0: (lambda st: va[128 + t0 + st * 128:128 + t0 + (st + 1) * 128, c0:c0 + W]))
        proj_tile(x_own[t0:t0 + T, :], T, 0, pcs, cs_own[:, :, t0:t0 + T])

    S.barrier()
    if stop <= 2:
        S.emit()
        return nc, S
    cast_w(w_out, w_out_b, "w_out_b", 2)
    cast_w(w_cq, w_cq_b, "w_cq_b", 1)
    cast_w(w_co, w_co_b, "w_co_b", 1)
    cast_w(w1, w1_b, "w1_b", 8)
    cast_w(w2, w2_b, "w2_b", 8)

    cur[0] = tables_end
    NT2 = 2 * N
    qT = [sb("qT%d" % i, [128, N], BF16) for i in range(2)]
    kT = [sb("kT%d" % i, [128, NT2], BF16) for i in range(2)]
    vtm = [sb("vtm%d" % i, [128, 2 * NCH, 128], BF16) for i in range(2)]
    gTs = [sb("gTs%d" % i, [128, N], BF16) for i in range(2)]
    DT = sb("DT", [128, 128], F32)
    dtmp = sb("dtmp", [128, 2, 128], F32)
    XI = sb("XI", [128, 2, 128], F32)
    cpw = sb("cpw", [128, 2, NCH], F32)
    kz = [sb("kz%d" % i, [128, 8, 128], BF16) for i in range(2)]
    SFs = sb("SFs", [128, 128], F32)
    SBs = sb("SBs", [128, 128], F32)
    SFst = sb("SFst", [128, NCH, 128], BF16)
    SBst = sb("SBst", [128, NCH, 128], BF16)
    qfb = [sb("qfb%d" % i, [128, 2, 512], BF16) for i in range(2)]
    attm = [sb("attm%d" % i, [128, 512], BF16) for i in range(2)]
    ysq = [sb("ysq%d" % i, [128, 512], BF16) for i in range(2)]
    rstd = [sb("rstd%d" % i, [128, 512], F32) for i in range(2)]
    ynf = sb("ynf", [128, 512], F32)
    yo = [sb("yo%d" % i, [128, 512], BF16) for i in range(2)]
    cexp = sb("cexp", [128, 2, NCH], F32)
    for cc in range(NCH):
        A("dve", _c("memset", cexp[:, 0, cc:cc + 1], 128.0 * (NCH - 1 - cc)), w=["cexp"])
        A("dve", _c("memset", cexp[:, 1, cc:cc + 1], 128.0 * cc), w=["cexp"])

    def r_loads(h):
        b = h % 2
        load("sp", qT[b], qrT[h], ["qT%d" % b], ["scr"])
        load("sp", kT[b], krT[h], ["kT%d" % b], ["scr"])
        load("sp", vtm[b], vr[:, h * 128:(h + 1) * 128].rearrange("(c p) e -> p c e", p=128), ["vtm%d" % b], ["scr"])
        load("sp", gTs[b], gT[h], ["gTs%d" % b], ["scr"])

    r_loads(0)
    kvc = [0]
    for h in range(HR):
        hb = h % 2
        qTh, kTh, vth, gTh = qT[hb], kT[hb], vtm[hb], gTs[hb]
        qtok, ktok, vtok, gtok = "qT%d" % hb, "kT%d" % hb, "vtm%d" % hb, "gTs%d" % hb
        if h + 1 < HR:
            r_loads(h + 1)
        for d_ in range(2):
            lgc = lg[:, d_ * HR + h:d_ * HR + h + 1]
            A("act", _c("activation", out=dtmp[:, d_, :], in_=adec[:, d_, :], func=AF.Exp, scale=lgc), r=["adec", "lg"], w=["dtmp"])
            A("act", _c("activation", out=XI[:, d_, :], in_=xiexp[:, d_, :], func=AF.Exp, scale=lgc), r=["xiexp", "lg"], w=["XI"])
            A("act", _c("activation", out=cpw[:, d_, :], in_=cexp[:, d_, :], func=AF.Exp, scale=lgc), r=["cexp", "lg"], w=["cpw"])
        A("dve", _c("tensor_tensor", out=DT, in0=dtmp[:, 0, :], in1=dtmp[:, 1, :], op=ALU.add), r=["dtmp"], w=["DT"])
        A("dve", _c("memset", SFs, 0.0), w=["SFs"])
        A("dve", _c("memset", SBs, 0.0), w=["SBs"])
        NG = NCH // 8
        groups = []
        for d_ in range(2):
            for g in range(NG):
                groups.append((d_, False, g))
        for g in range(NG):
            groups.append((0, True, g))
        for g in reversed(range(NG)):
            groups.append((1, True, g))

        def st_T(j):
            d_, own, g = groups[j]
            base = (0 if own else NCH) + g * 8
            i = j % 2
            pt = psb(0)
            for jj in range(8):
                cc = base + jj
                A("pe", _c("transpose", pt[:, jj * 128:(jj + 1) * 128], kTh[:, cc * 128:(cc + 1) * 128], ident), r=[ktok, "ident"], w=["ps0"])
            A("act", _c("activation", out=kz[i].rearrange("p a b -> p (a b)"), in_=pt, func=AF.Copy, scale=zfb[:, d_, h:h + 1]),
              r=["ps0", "zfb"], w=["kz%d" % i])

        def st_KV(j):
            d_, own, g = groups[j]
            i = j % 2
            Stile, stok = (SFs, "SFs") if d_ == 0 else (SBs, "SBs")
            Sst, sstok = (SFst, "SFst") if d_ == 0 else (SBst, "SBst")
            base = (0 if own else NCH) + g * 8
            if own and g == (0 if d_ == 0 else NG - 1):
                fl = flags[:, 1:2] if d_ == 0 else flags[:, 0:1]
                A("dve", _c("tensor_scalar", out=Stile, in0=Stile, scalar1=fl, scalar2=None, op0=ALU.mult), r=[stok, "flags"], w=[stok])
            halves = [0, 1] if (d_ == 0 or not own) else [1, 0]
            for half in halves:
                bk = 1 + kvc[0] % 2
                kvc[0] += 1
                for jq in range(4):
                    jj = half * 4 + jq
                    A("pe", _c("matmul", psum[bk][:, jq * 128:(jq + 1) * 128], lhsT=kz[i][:, jj, :], rhs=vth[:, base + jj, :], start=True, stop=True),
                      r=["kz%d" % i, vtok], w=["ps%d" % bk])
                js = [0, 1, 2, 3] if (d_ == 0 or not own) else [3, 2, 1, 0]
                for jq in js:
                    cc = g * 8 + half * 4 + jq
                    kvp = psum[bk][:, jq * 128:(jq + 1) * 128]
                    if not own:
                        A("dve", _c("scalar_tensor_tensor", out=Stile, in0=kvp, scalar=cpw[:, d_, cc:cc + 1], in1=Stile, op0=ALU.mult, op1=ALU.add),
                          r=["ps%d" % bk, "cpw", stok], w=[stok])
                    else:
                        A("act", _c("activation", out=Sst[:, cc, :], in_=Stile, func=AF.Copy), r=[stok], w=[sstok])
                        A("dve", _c("scalar_tensor_tensor", out=Stile, in0=Stile, scalar=cdec[:, d_ * HR + h:d_ * HR + h + 1], in1=kvp,
                                    op0=ALU.mult, op1=ALU.add), r=["ps%d" % bk, "cdec", stok], w=[stok])

        for j in range(len(groups) + 1):
            if j < len(groups):
                st_T(j)
            if j >= 1:
                st_KV(j - 1)

        G4 = NCH // 4

        def o_s1(i):
            i2 = i % 2
            tsl = slice(i * 512, (i + 1) * 512)
            for d_ in range(2):
                A("dve", _c("tensor_tensor", out=qfb[i2][:, d_, :].rearrange("p (a b) -> p a b", a=4), in0=qTh[:, tsl].rearrange("p (a b) -> p a b", a=4),
                            in1=XI[:, d_, :].unsqueeze(1).to_broadcast([128, 4, 128]), op=ALU.mult), r=[qtok, "XI"], w=["qfb%d" % i2])
            bS = 3 + i2
            for jq in range(4):
                cc = i * 4 + jq
                A("pe", _c("matmul", psum[bS][:, jq * 128:(jq + 1) * 128], lhsT=kTh[:, cc * 128:(cc + 1) * 128], rhs=qTh[:, cc * 128:(cc + 1) * 128],
                           start=True, stop=True), r=[ktok, qtok], w=["ps%d" % bS])
            A("dve", _c("tensor_tensor", out=attm[i2].rearrange("p (a b) -> p a b", a=4), in0=psum[bS].rearrange("p (a b) -> p a b", a=4),
                        in1=DT.unsqueeze(1).to_broadcast([128, 4, 128]), op=ALU.mult), r=["ps%d" % bS, "DT"], w=["attm%d" % i2])

        def o_s2(i):
            i2 = i % 2
            bY = 5 + i2
            for jq in range(4):
                cc = i * 4 + jq
                ysl = psum[bY][:, jq * 128:(jq + 1) * 128]
                A("pe", _c("matmul", ysl, lhsT=vth[:, cc, :], rhs=attm[i2][:, jq * 128:(jq + 1) * 128], start=True, stop=False),
                  r=[vtok, "attm%d" % i2], w=["ps%d" % bY])
                A("pe", _c("matmul", ysl, lhsT=SFst[:, cc, :], rhs=qfb[i2][:, 0, jq * 128:(jq + 1) * 128], start=False, stop=False),
                  r=["SFst", "qfb%d" % i2], w=["ps%d" % bY])
                A("pe", _c("matmul", ysl, lhsT=SBst[:, cc, :], rhs=qfb[i2][:, 1, jq * 128:(jq + 1) * 128], start=False, stop=True),
                  r=["SBst", "qfb%d" % i2], w=["ps%d" % bY])
            A("act", _c("activation", out=ysq[i2], in_=psum[bY], func=AF.Square), r=["ps%d" % bY], w=["ysq%d" % i2])

        def o_s3a(i):
            i2 = i % 2
            A("pe", _c("matmul", psum[7], lhsT=ones, rhs=ysq[i2], start=True, stop=True), r=["ysq%d" % i2, "ones"], w=["ps7"])
            A("dve", _c("tensor_scalar", out=rstd[i2], in0=psum[7], scalar1=1.0 / 128, scalar2=EPS, op0=ALU.mult, op1=ALU.add), r=["ps7"], w=["rstd%d" % i2])
            A("act", _c("activation", out=rstd[i2], in_=rstd[i2], func=AF.Sqrt), r=["rstd%d" % i2], w=["rstd%d" % i2])

        def o_s3b(i):
            i2 = i % 2
            bY = 5 + i2
            tsl = slice(i * 512, (i + 1) * 512)
            A("dve", _c("reciprocal", rstd[i2], rstd[i2]), r=["rstd%d" % i2], w=["rstd%d" % i2])
            A("dve", _c("tensor_tensor", out=ynf, in0=psum[bY], in1=rstd[i2], op=ALU.mult), r=["ps%d" % bY, "rstd%d" % i2], w=["ynf"])
            A("dve", _c("tensor_tensor", out=yo[i2], in0=ynf, in1=gTh[:, tsl], op=ALU.mult), r=["ynf", gtok], w=["yo%d" % i2])
            A("sp", _c("dma_start", out=ymixT[h, :, tsl], in_=yo[i2]), r=["yo%d" % i2], w=["ymix"], dma=True)

        for i in range(G4 + 2):
            if i - 2 >= 0:
                o_s3a(i - 2)
            if i < G4:
                o_s1(i)
            if 0 <= i - 1 < G4:
                o_s2(i - 1)
            if i - 2 >= 0:
                o_s3b(i - 2)

    S.barrier()
    if stop <= 3:
        S.emit()
        return nc, S
    cur[0] = tables_end
    qA = [sb("qA%d" % i, [128, N], BF16) for i in range(2)]
    kA = [sb("kA%d" % i, [128, N + 256], BF16) for i in range(2)]
    vA = [sb("vA%d" % i, [128, NCH + 2, 128], BF16) for i in range(2)]
    BI = [sb("BI%d" % i, [128, 384], F32) for i in range(2)]
    relb = sb("relb", [128, 32 * HA], F32)
    bitmp = sb("bitmp", [128, 384], F32)
    EB = sb("EB", [128, 33, 384], F32)
    sS = [sb("sS%d" % i, [128, 4, 384], F32) for i in range(2)]
    pS = [sb("pS%d" % i, [128, 4, 384], F32) for i in range(2)]
    pn = [sb("pn%d" % i, [128, 4, 384], BF16) for i in range(2)]
    pT = [sb("pT%d" % i, [128, 1536], BF16) for i in range(2)]
    sm = [sb("sm%d" % i, [128, 24], F32) for i in range(2)]
    yoA = [sb("yoA%d" % i, [128, 512], BF16) for i in range(2)]
    load("sp", relb, relb_d, ["relb"])
    load("sp", EB, ebuck_d.rearrange("b p j -> p b j"), ["EB"])
    scale = 128.0 ** -0.5
    psS = psall[:, 0:2048].rearrange("p (b c) -> p b c", b=4)[:, :, 0:384]
    psTb = psall[:, 2048:3072].bitcast(BF16)
    G4 = NCH // 4

    def a_loads(h):
        b = h % 2
        load("sp", qA[b], qaT[h], ["qA%d" % b], ["scr"])
        if h % 4 == 0:
            kb = (h // 4) % 2
            load("sp", kA[kb], kaT[h // 4], ["kA%d" % kb], ["scr"])
            load("sp", vA[kb], va[:, (h // 4) * 128:(h // 4 + 1) * 128].rearrange("(c p) e -> p c e", p=128), ["vA%d" % kb], ["scr"])
        bt = "BI%d" % b
        A("pool", _c("tensor_scalar", out=BI[b], in0=EB[:, 0, :], scalar1=relb[:, h:h + 1], scalar2=None, op0=ALU.mult), r=["EB", "relb"], w=[bt])
        for bb in range(1, 33):
            sc_ = relb[:, bb * HA + h:bb * HA + h + 1] if bb < 32 else NEG
            A("pool", _c("tensor_scalar", out=bitmp, in0=EB[:, bb, :], scalar1=sc_, scalar2=None, op0=ALU.mult), r=["EB", "relb"], w=["bitmp"])
            A("pool", _c("tensor_tensor", out=BI[b], in0=BI[b], in1=bitmp, op=ALU.add), r=["bitmp", bt], w=[bt])

    items = [(h, g) for h in range(HA) for g in range(G4)]

    def b_s1(k):
        h, g = items[k]
        i2 = k % 2
        hb = h % 2
        kb = (h // 4) % 2
        for b in range(4):
            n = g * 4 + b
            A("pe", _c("matmul", psS[:, b, :], lhsT=qA[hb][:, n * 128:(n + 1) * 128], rhs=kA[kb][:, n * 128:n * 128 + 384], start=True, stop=True),
              r=["qA%d" % hb, "kA%d" % kb], w=["ps%d" % b])
        pst = ["ps0", "ps1", "ps2", "ps3"]
        st_ = "sS%d" % i2
        mt = "sm%d" % i2
        m = sm[i2]
        A("dve", _c("scalar_tensor_tensor", out=sS[i2], in0=psS, scalar=scale, in1=BI[hb].unsqueeze(1).to_broadcast([128, 4, 384]),
                    op0=ALU.mult, op1=ALU.add), r=pst + ["BI%d" % hb], w=[st_])
        if g == 0:
            A("dve", _c("tensor_scalar", out=sS[i2][:, 0, 0:128], in0=sS[i2][:, 0, 0:128], scalar1=flags[:, 2:3], scalar2=None, op0=ALU.add),
              r=[st_, "flags"], w=[st_])
        if g == G4 - 1:
            A("dve", _c("tensor_scalar", out=sS[i2][:, 3, 256:384], in0=sS[i2][:, 3, 256:384], scalar1=flags[:, 3:4], scalar2=None, op0=ALU.add),
              r=[st_, "flags"], w=[st_])
        A("dve", _c("reduce_max", out=m[:, 0:4], in_=sS[i2], axis=AX.X), r=[st_], w=[mt])
        A("dve", _c("tensor_scalar", out=m[:, 4:8], in0=m[:, 0:4], scalar1=sink[:, h:h + 1], scalar2=-1.0, op0=ALU.max, op1=ALU.mult), r=[mt, "sink"], w=[mt])
        for b in range(4):
            A("act", _c("activation", out=pS[i2][:, b, :], in_=sS[i2][:, b, :], func=AF.Exp, bias=m[:, 4 + b:5 + b], accum_out=m[:, 8 + b:9 + b]),
              r=[st_, mt], w=["pS%d" % i2, mt])
        A("act", _c("activation", out=m[:, 12:16], in_=m[:, 4:8], func=AF.Exp, bias=sink[:, h:h + 1]), r=[mt, "sink"], w=[mt])

    def b_s2(k):
        h, g = items[k]
        i2 = k % 2
        mt = "sm%d" % i2
        m = sm[i2]
        A("dve", _c("tensor_tensor", out=m[:, 16:20], in0=m[:, 8:12], in1=m[:, 12:16], op=ALU.add), r=[mt], w=[mt])
        A("dve", _c("reciprocal", m[:, 20:24], m[:, 16:20]), r=[mt], w=[mt])
        A("pool", _c("tensor_tensor", out=pn[i2], in0=pS[i2], in1=m[:, 20:24].unsqueeze(2).to_broadcast([128, 4, 384]), op=ALU.mult),
          r=["pS%d" % i2, mt], w=["pn%d" % i2])
        for b in range(4):
            for t in range(3):
                q = b * 3 + t
                A("pe", _c("transpose", psTb[:, q * 128:(q + 1) * 128], pn[i2][:, b, t * 128:(t + 1) * 128], ident), r=["pn%d" % i2, "ident"], w=["ps4", "ps5"])
        A("act", _c("activation", out=pT[i2], in_=psTb[:, 0:1536], func=AF.Copy), r=["ps4", "ps5"], w=["pT%d" % i2])

    def b_s3(k):
        h, g = items[k]
        i2 = k % 2
        kb = (h // 4) % 2
        bO = 6 + i2
        for b in range(4):
            n = g * 4 + b
            for t in range(3):
                q = b * 3 + t
                A("pe", _c("matmul", psum[bO][:, b * 128:(b + 1) * 128], lhsT=vA[kb][:, n + t, :], rhs=pT[i2][:, q * 128:(q + 1) * 128],
                           start=(t == 0), stop=(t == 2)), r=["vA%d" % kb, "pT%d" % i2], w=["ps%d" % bO])
        A("act", _c("activation", out=yoA[i2], in_=psum[bO], func=AF.Copy), r=["ps%d" % bO], w=["yoA%d" % i2])
        A("sp", _c("dma_start", out=ymixT[HR + h, :, g * 512:(g + 1) * 512], in_=yoA[i2]), r=["yoA%d" % i2], w=["ymix"], dma=True)

    a_loads(0)
    for k in range(len(items) + 2):
        if k < len(items):
            h, g = items[k]
            if g == 0 and h + 1 < HA:
                a_loads(h + 1)
            b_s1(k)
        if 0 <= k - 1 < len(items):
            b_s2(k - 1)
        if k - 2 >= 0:
            b_s3(k - 2)
    S.barrier()
    if stop <= 4:
        S.emit()
        return nc, S
    cur[0] = p0
    NST = T // 128
    xacc = [sb("xacc%d" % i, [128, D], F32) for i in range(NST)]
    hT4 = sb("hT4", [128, KC, T], BF16)
    FCH = F // 128
    aT = [sb("aT%d" % i, [128, FCH, T], BF16) for i in range(2)]
    rl = [sb("rl%d" % i, [128, T], F32) for i in range(2)]
    qc = sb("qc", [128, 4, T], BF16)
    oT = sb("oT", [128, 4, T], BF16)
    kmS = sb("kmS", [128, 4, 256], BF16)
    vmS = sb("vmS", [128, 2, 512], BF16)
    hn4b = [sb("hn4_%d" % i, [128, D], BF16) for i in range(2)]
    junkS = sb("junkS", [128, D // 8], BF16)
    st4 = sb("st4", [128, 16], F32)
    h4c = [0]
    cS = [sb("cS%d" % i, [128, 256], F32) for i in range(2)]
    cP = [sb("cP%d" % i, [128, 256], BF16) for i in range(2)]
    cT = [sb("cT%d" % i, [128, 256], BF16) for i in range(2)]
    cm = [sb("cm%d" % i, [128, 8], F32) for i in range(2)]
    SLOT = 16384
    nring = (top - cur[0]) // SLOT
    nring = min(nring, 4)
    assert nring >= 2, ("ring too small", nring)
    print("P4 ring slots", nring, "free bytes", top - cur[0])
    ring = [sb("ring%d" % i, [128, SLOT // 2], BF16) for i in range(nring)]
    rc = [0]

    def ring_load(src_ap, shape):
        i = rc[0] % nring
        rc[0] += 1
        a, b = shape
        v = ring[i][:, 0:a * b].rearrange("p (a b) -> p a b", a=a)
        load("sp", v, src_ap, ["ring%d" % i], [src_ap.tensor.name])
        return v, "ring%d" % i

    load("sp", kmS, kmT.rearrange("h p t -> p h t"), ["kmS"], ["scr"])
    load("sp", vmS, vm.rearrange("(c p) e -> p c e", p=128), ["vmS"], ["scr"])
    acnt = [0]
    PW = min(512, F, SLOT // 2 // KC)
    assert PW >= 128

    def stats4(st, dcol):
        q4 = D // 8
        so = (dcol % 2) * 8
        for q in range(8):
            A("act", _c("activation", out=junkS, in_=xacc[st][:, q * q4:(q + 1) * q4], func=AF.Square, accum_out=st4[:, so + q:so + q + 1]),
              r=["xacc%d" % st], w=["junkS", "st4_%d" % (dcol % 2)])
        A("dve", _c("reduce_sum", out=stat[:, dcol:dcol + 1], in_=st4[:, so:so + 8], axis=AX.X), r=["st4_%d" % (dcol % 2)], w=["stat%d" % dcol])
        A("dve", _c("tensor_scalar", out=stat[:, dcol:dcol + 1], in0=stat[:, dcol:dcol + 1], scalar1=1.0 / D, scalar2=EPS,
                    op0=ALU.mult, op1=ALU.add), r=["stat%d" % dcol], w=["stat%d" % dcol])
        A("act", _c("activation", out=stat[:, dcol:dcol + 1], in_=stat[:, dcol:dcol + 1], func=AF.Sqrt), r=["stat%d" % dcol], w=["stat%d" % dcol])
        A("dve", _c("reciprocal", stat[:, dcol:dcol + 1], stat[:, dcol:dcol + 1]), r=["stat%d" % dcol], w=["stat%d" % dcol])

    def norm_T(gi):
        for st in range(NST):
            dcol = st % 4
            stats4(st, dcol)
            hi = h4c[0] % 2
            h4c[0] += 1
            hn4 = hn4b[hi]
            hntok = "hn4_%d" % hi
            A("dve", _c("tensor_scalar", out=hn4, in0=xacc[st], scalar1=stat[:, dcol:dcol + 1], scalar2=None, op0=ALU.mult),
              r=["xacc%d" % st, "stat%d" % dcol], w=[hntok])
            for g8 in range(KC // 8):
                bk = acnt[0] % 2
                acnt[0] += 1
                pt = psb(bk)
                for j in range(8):
                    kc = g8 * 8 + j
                    A("pe", _c("transpose", pt[:, j * 128:(j + 1) * 128], hn4[:, kc * 128:(kc + 1) * 128], ident),
                      r=[hntok, "ident"], w=["ps%d" % bk])
                ptv = pt.rearrange("p (k t) -> p k t", k=8)
                gb = gcols[:, gi, g8 * 8:(g8 + 1) * 8].unsqueeze(2).to_broadcast([128, 8, 128])
                A("dve", _c("tensor_tensor", out=hT4[:, g8 * 8:(g8 + 1) * 8, st * 128:(st + 1) * 128], in0=ptv,
                            in1=gb, op=ALU.mult), r=["ps%d" % bk, "gcols"], w=["hT4"])

    pcnt = [0]

    def tm_accumulate(wb_ap, KCn, lhs_fn, lhs_toks, first_add_src=None):
        pw = (SLOT // 2) // KCn
        pw = min(pw, D)
        pw = (pw // 512) * 512 if pw >= 512 else pw
        for c0 in range(0, D, pw):
            v, rtok = ring_load(wb_ap[:, c0:c0 + pw].rearrange("(kc p) n -> p kc n", p=128), (KCn, pw))
            for st in range(NST):
                for cb in range(0, pw, 512):
                    w_ = min(512, pw - cb)
                    bk = 2 + pcnt[0] % 3
                    pcnt[0] += 1
                    for kc in range(KCn):
                        A("pe", _c("matmul", psum[bk][:, 0:w_], lhsT=lhs_fn(kc, st), rhs=v[:, kc, cb:cb + w_],
                                                                                      start=(kc == 0), stop=(kc == KCn - 1)),
                          r=list(lhs_toks) + [rtok], w=["ps%d" % bk])
                    xs = xacc[st][:, c0 + cb:c0 + cb + w_]
                    A("dve", _c("tensor_tensor", out=xs, in0=psum[bk][:, 0:w_], in1=xs, op=ALU.add),
                      r=["ps%d" % bk, "xacc%d" % st], w=["xacc%d" % st])

    for ti in range(N // T):
        t0 = ti * T
        load("sp", hT4, ymixT[:, :, t0:t0 + T].rearrange("c p t -> p c t"), ["hT4"], ["ymix"])
        for st in range(NST):
            load("sp", xacc[st], x_own[t0 + st * 128:t0 + (st + 1) * 128, :], ["xacc%d" % st])
        tm_accumulate(w_out_b, KC, lambda kc, st: hT4[:, kc, st * 128:(st + 1) * 128], ["hT4"])
        norm_T(1)
        for c0 in range(0, 512, PW):
            v, rtok = ring_load(w_cq_b[:, c0:c0 + PW].rearrange("(kc p) n -> p kc n", p=128), (KC, PW))
            for ch in range(PW // 128):
                hd = (c0 + ch * 128) // 128
                bk = 2 + pcnt[0] % 3
                pcnt[0] += 1
                for kc in range(KC):
                    A("pe", _c("matmul", psum[bk][:, 0:T], lhsT=v[:, kc, ch * 128:(ch + 1) * 128], rhs=hT4[:, kc, :],
                                                                       start=(kc == 0), stop=(kc == KC - 1)), r=["hT4", rtok], w=["ps%d" % bk])
                A("act", _c("activation", out=qc[:, hd, :], in_=psum[bk][:, 0:T], func=AF.Copy), r=["ps%d" % bk], w=["qc"])
        for hd in range(4):
            for st in range(NST):
                i2 = (hd * NST + st) % 2
                bS = 5
                A("pe", _c("matmul", psum[bS][:, 0:256], lhsT=qc[:, hd, st * 128:(st + 1) * 128], rhs=kmS[:, hd, :], start=True, stop=True),
                  r=["qc", "kmS"], w=["ps%d" % bS])
                m = cm[i2]
                mt = "cm%d" % i2
                A("dve", _c("reduce_max", out=m[:, 0:1], in_=psum[bS][:, 0:256], axis=AX.X), r=["ps%d" % bS], w=[mt])
                A("dve", _c("tensor_scalar", out=m[:, 1:2], in0=m[:, 0:1], scalar1=-scale, scalar2=None, op0=ALU.mult), r=[mt], w=[mt])
                A("act", _c("activation", out=cS[i2], in_=psum[bS][:, 0:256], func=AF.Exp, bias=m[:, 1:2], scale=scale, accum_out=m[:, 2:3]),
                  r=["ps%d" % bS, mt], w=["cS%d" % i2, mt])
                A("dve", _c("reciprocal", m[:, 3:4], m[:, 2:3]), r=[mt], w=[mt])
                A("pool", _c("tensor_scalar", out=cP[i2], in0=cS[i2], scalar1=m[:, 3:4], scalar2=None, op0=ALU.mult),
                  r=["cS%d" % i2, mt], w=["cP%d" % i2])
                bT = 6
                ptv = psb(bT)
                for t in range(2):
                    A("pe", _c("transpose", ptv[:, t * 128:(t + 1) * 128], cP[i2][:, t * 128:(t + 1) * 128], ident),
                      r=["cP%d" % i2, "ident"], w=["ps%d" % bT])
                A("act", _c("activation", out=cT[i2], in_=ptv[:, 0:256], func=AF.Copy), r=["ps%d" % bT], w=["cT%d" % i2])
                bO = 7
                for t in range(2):
                    A("pe", _c("matmul", psum[bO][:, st * 128:(st + 1) * 128], lhsT=vmS[:, t, hd * 128:(hd + 1) * 128],
                                                                       rhs=cT[i2][:, t * 128:(t + 1) * 128], start=(t == 0), stop=(t == 1)),
                      r=["vmS", "cT%d" % i2], w=["ps%d" % bO])
            A("act", _c("activation", out=oT[:, hd, :], in_=psum[7][:, 0:T], func=AF.Copy), r=["ps7"], w=["oT"])
        tm_accumulate(w_co_b, 4, lambda kc, st: oT[:, kc, st * 128:(st + 1) * 128], ["oT"])
        norm_T(3)
        NFB = DFF // F

        def mlp_s1(fb):
            ai = fb % 2
            for c0 in range(0, F, PW):
                v, rtok = ring_load(w1_b[:, fb * F + c0:fb * F + c0 + PW].rearrange("(kc p) n -> p kc n", p=128), (KC, PW))
                for ch in range(PW // 128):
                    dc = (c0 + ch * 128) // 128
                    bk = 2 + pcnt[0] % 3
                    pcnt[0] += 1
                    for kc in range(KC):
                        A("pe", _c("matmul", psum[bk][:, 0:T], lhsT=v[:, kc, ch * 128:(ch + 1) * 128], rhs=hT4[:, kc, :],
                                   start=(kc == 0), stop=(kc == KC - 1)), r=["hT4", rtok], w=["ps%d" % bk])
                    ri = dc % 2
                    A("act", _c("activation", out=rl[ri], in_=psum[bk][:, 0:T], func=AF.Relu), r=["ps%d" % bk], w=["rl%d" % ri])
                    A("pool", _c("tensor_tensor", out=aT[ai][:, dc, :], in0=rl[ri], in1=rl[ri], op=ALU.mult),
                      r=["rl%d" % ri], w=["aT%d" % ai])

        def mlp_s2(fb):
            ai = fb % 2
            tm_accumulate(w2_b[fb * F:(fb + 1) * F, :], FCH, lambda kc, st, ai=ai: aT[ai][:, kc, st * 128:(st + 1) * 128], ["aT%d" % ai])

        mlp_s1(0)
        for fb in range(NFB):
            if fb + 1 < NFB:
                mlp_s1(fb + 1)
            mlp_s2(fb)
        gi_ = rc[0] % nring
        rc[0] += 1
        gview = ring[gi_].bitcast(F32)[:, 0:D]
        load("sp", gview, gfin_d, ["ring%d" % gi_])
        for st in range(NST):
            dcol = 4 + st % 4
            stats4(st, dcol)
            A("dve", _c("scalar_tensor_tensor", out=xacc[st], in0=xacc[st], scalar=stat[:, dcol:dcol + 1], in1=gview,
                        op0=ALU.mult, op1=ALU.mult), r=["xacc%d" % st, "stat%d" % dcol, "ring%d" % gi_], w=["xacc%d" % st])
            A("pool", _c("dma_start", out=y_out[t0 + st * 128:t0 + (st + 1) * 128, :], in_=xacc[st]),
              r=["xacc%d" % st], w=["yout"], dma=True)

    S.emit()
    return nc, S


def core_inputs(cfg, inputs, seqs):
    c = cfg
    N, D, KC, HR, HA = c.N, c.D, c.KC, c.HR, c.HA
    st = static_tables()
    f32 = np.float32

    def colmajor(g):
        return np.ascontiguousarray(np.asarray(g, f32).reshape(KC, 128).T)

    gcols = np.ascontiguousarray(np.stack([colmajor(inputs["norm_mix"][0]), colmajor(inputs["norm_cross"][0]),
                                           colmajor(inputs["norm_mem"][0]), colmajor(inputs["norm_mlp"][0])], 1))
    gfin = np.ascontiguousarray(np.broadcast_to(np.asarray(inputs["norm_final"], f32)[None, :], (128, D)))
    dec = np.concatenate([np.asarray(inputs["ret_decay_f"][0], f32), np.asarray(inputs["ret_decay_b"][0], f32)])
    dec = np.ascontiguousarray(np.broadcast_to(dec[None], (128, 2 * HR)))
    sink = np.ascontiguousarray(np.broadcast_to(np.asarray(inputs["attn_sink"][0], f32)[None], (128, HA)))
    relb = np.ascontiguousarray(np.broadcast_to(np.asarray(inputs["rel_bias"], f32).reshape(1, -1), (128, 32 * HA)))
    half = 64
    inv = (np.float32(10000.0) ** (-np.arange(half, dtype=f32) / np.float32(half))).astype(f32)

    def cs_table(p0):
        pos = np.arange(p0, p0 + N, dtype=f32)
        ang = (pos[:, None] * inv[None, :]).astype(f32)
        co = np.cos(ang).astype(f32).T
        si = np.sin(ang).astype(f32).T
        return np.ascontiguousarray(np.stack([np.concatenate([co, co], 0), np.concatenate([-si, si], 0)], 0))

    shared = dict(w_in=inputs["w_in"][0], w_out=inputs["w_out"][0], w_cq=inputs["w_cq"][0], w_ckv=inputs["w_ckv"][0],
                  w_co=inputs["w_co"][0], w1=inputs["w_mlp_in"][0], w2=inputs["w_mlp_out"][0], gcols=gcols, gfin=gfin,
                  dec=dec, sink=sink, relb=relb, **st)
    maps = []
    zeros_oth = np.zeros((N, D), f32)
    for (xarr, b, start, marr) in seqs:
        L = xarr.shape[1]
        m = dict(shared)
        m["x_own"] = np.ascontiguousarray(xarr[b, start:start + N])
        halo = np.zeros((256, D), f32)
        fa = fb = 0.0
        lneg = rneg = NEG
        oth = zeros_oth
        p_oth = 0
        if start > 0:
            halo[0:128] = xarr[b, start - 128:start]
            lneg = 0.0
            fb = 1.0
            oth = np.ascontiguousarray(xarr[b, start - N:start])
            p_oth = start - N
        if start + N < L:
            halo[128:256] = xarr[b, start + N:start + N + 128]
            rneg = 0.0
            fa = 1.0
            oth = np.ascontiguousarray(xarr[b, start + N:start + 2 * N])
            p_oth = start + N
        m["x_oth"] = oth
        m["x_halo"] = halo
        m["mem"] = np.ascontiguousarray(marr[b])
        m["cs_own"] = cs_table(start)
        m["cs_oth"] = cs_table(p_oth)
        m["flags"] = np.ascontiguousarray(np.broadcast_to(np.array([fa, fb, lneg, rneg], f32)[None], (128, 4)))
        maps.append(m)
    return maps


_CACHE = {}


def run(cfg, inputs, debug_outs=(), trace=False, stop=99):
    xp = np.asarray(inputs["x_prompt"], np.float32)
    xs = np.asarray(inputs["x_sample"], np.float32)
    mp = np.asarray(inputs["mem_prompt"], np.float32)
    ms = np.asarray(inputs["mem_sample"], np.float32)
    N = cfg.N
    seqs = [(xp, b, 0, mp) for b in range(4)] + [(xs, b, hh * N, ms) for b in range(2) for hh in range(2)]
    maps = core_inputs(cfg, inputs, seqs)
    key = (cfg.D, cfg.N, cfg.DFF, cfg.T, cfg.F, tuple(debug_outs))
    nc, S = build_nc(cfg, debug_outs, stop)
    res = run_bass_kernel_spmd(nc, maps, core_ids=list(range(8)), **({"trace": True} if trace else {}))
    yp = np.stack([res.results[b]["y"] for b in range(4)], 0)
    ysm = np.stack([np.concatenate([res.results[4 + 2 * b]["y"], res.results[5 + 2 * b]["y"]], 0) for b in range(2)], 0)
    return (yp.astype(np.float32), ysm.astype(np.float32)), res, S


def kernel(**inputs):
    cfg = Cfg()
    (yp, ysm), _, _ = run(cfg, inputs)
    return (yp, ysm)
```

```python
import numpy as np
import concourse.bass as bass
import concourse.mybir as mybir
from concourse.bass_utils import run_bass_kernel_spmd

F32 = mybir.dt.float32
BF16 = mybir.dt.bfloat16
ALU = mybir.AluOpType
AF = mybir.ActivationFunctionType
AX = mybir.AxisListType
NEG = -1e30
import os as _os
POOLC = _os.environ.get("POOLC", "dve")
EPS = 1e-6


def _c(name, *args, **kwargs):
    return lambda e: getattr(e, name)(*args, **kwargs)


class Op:
    __slots__ = ("eng", "fn", "dma", "deps", "sig", "signaled", "idx", "pre")

    def __init__(self, eng, fn, dma):
        self.eng = eng
        self.fn = fn
        self.dma = dma
        self.deps = {}
        self.sig = None
        self.signaled = False
        self.pre = None


class Sched:
    COMPUTE = ("pe", "act", "dve", "pool")
    NSLOT = {"sp": 10, "act": 4, "pool": 40}

    def __init__(self, nc, same_eng_sync=True):
        self.nc = nc
        self.ops = []
        self.last_w = {}
        self.readers = {}
        self.same_eng_sync = same_eng_sync
        self.bar_deps = None
        self.bar_seen = {}
        self.last_on = {}
        self.last_dma = {}
        self.ndma = {"sp": 0, "act": 0, "pool": 0}

    def add(self, eng, fn, r=(), w=(), dma=False):
        op = Op(eng, fn, dma)
        op.idx = len(self.ops)
        deps = {}
        w = list(w) + [t for t in r if t.startswith("ps") and t not in w]
        for t in r:
            lw = self.last_w.get(t)
            if lw is not None:
                deps[lw] = "raw"
        for t in w:
            lw = self.last_w.get(t)
            if lw is not None:
                deps[lw] = "waw"
            for rd in self.readers.get(t, ()):
                if rd not in deps:
                    deps[rd] = "war"
        key = (eng, dma)
        if self.bar_deps is not None and not self.bar_seen.get(key):
            self.bar_seen[key] = True
            for d in self.bar_deps:
                if d not in deps:
                    deps[d] = "bar"
        for d, kind in deps.items():
            if d is op:
                continue
            if not d.dma and not dma and d.eng == eng:
                if eng == "pe":
                    continue
                if kind == "war" or not self.same_eng_sync:
                    continue
            gk = ("dma", d.eng, d.sig[2]) if d.dma else ("c", d.eng)
            best = op.deps.get(gk)
            if best is None or d.idx > best.idx:
                op.deps[gk] = d
        for d in op.deps.values():
            d.signaled = True
        for t in r:
            self.readers.setdefault(t, []).append(op)
        for t in w:
            self.last_w[t] = op
            self.readers[t] = []
        if dma:
            k = self.ndma[eng]
            self.ndma[eng] = k + 1
            ns = self.NSLOT[eng]
            slot = k % ns
            op.sig = ("dma", eng, slot, 16 * (k // ns + 1))
            op.signaled = True
            op.pre = self.last_dma.get((eng, slot))
            self.last_dma[(eng, slot)] = op
        else:
            self.last_on[eng] = op
        self.ops.append(op)
        return op

    def barrier(self):
        deps = list(self.last_on.values()) + list(self.last_dma.values())
        for d in deps:
            d.signaled = True
        self.bar_deps = deps
        self.bar_seen = {}
        self.last_w = {}
        self.readers = {}

    def emit(self, final_wait_eng="sp"):
        nc = self.nc
        for o in self.last_on.values():
            o.signaled = True
        cnt = {e: 0 for e in self.COMPUTE}
        for op in self.ops:
            if not op.dma and op.signaled:
                cnt[op.eng] += 1
                op.sig = ("c", op.eng, 0, cnt[op.eng])
        sems = {}
        for op in self.ops:
            if op.sig is not None and op.sig[:3] not in sems:
                sems[op.sig[:3]] = nc.alloc_semaphore(name="s_%s_%s_%d" % op.sig[:3])
        streams = {e: [] for e in ("pe", "act", "dve", "pool", "sp")}
        for op in self.ops:
            streams[op.eng].append(op)
        finals = list(self.last_dma.values()) + list(self.last_on.values())
        nwaits = [0]

        def run(ename, e):
            waited = {}

            def wait(sig):
                k = sig[:3]
                if waited.get(k, 0) >= sig[3]:
                    return
                waited[k] = sig[3]
                e.wait_ge(sems[k], sig[3])
                nwaits[0] += 1

            for op in streams[ename]:
                for d in op.deps.values():
                    wait(d.sig)
                if op.dma and op.pre is not None:
                    wait(op.pre.sig)
                ins = op.fn(e)
                if op.signaled:
                    ins.then_inc(sems[op.sig[:3]], 16 if op.dma else 1)
            if ename == final_wait_eng:
                for d in finals:
                    wait(d.sig)

        with nc.Block() as block:
            block.tensor(lambda e: run("pe", e))
            block.scalar(lambda e: run("act", e))
            block.vector(lambda e: run("dve", e))
            block.gpsimd(lambda e: run("pool", e))
            block.sync(lambda e: run("sp", e))
        self.stats = dict(nops=len(self.ops), nwaits=nwaits[0], nsems=len(sems),
                          per_eng={k: len(v) for k, v in streams.items()})


class Cfg:
    def __init__(self, D=4096, N=4096, DFF=16384, T=512, F=512):
        self.D, self.N, self.DFF, self.T, self.F = D, N, DFF, T, F
        self.KC = D // 128
        self.HR = (D // 2) // 128
        self.HA = self.HR
        self.KV = self.HA // 4
        self.RW = self.HR * 128
        self.AQ = self.HA * 128
        self.AKV = self.KV * 128
        self.INW = 4 * self.RW + self.AQ + 2 * self.AKV
        self.MEM = 256
        self.MH = 4
        self.MW = 512
        self.NCH = N // 128


def t5_buckets(rel):
    nb = 16
    max_exact = 8
    base = np.where(rel > 0, nb, 0)
    n = np.abs(rel)
    large = max_exact + (np.log(np.maximum(n, 1) / max_exact) / np.log(128 / max_exact) * (nb - max_exact)).astype(np.int32)
    large = np.minimum(large, nb - 1)
    return (base + np.where(n < max_exact, n, large)).astype(np.int32)


def static_tables():
    st = {}
    st["ident"] = np.eye(128, dtype=np.float32)
    rs = np.zeros((128, 128), np.float32)
    for dp in range(128):
        rs[(dp + 64) % 128, dp] = 1.0
    st["rswap"] = rs
    j = np.arange(128)[:, None].astype(np.float32)
    i = np.arange(128)[None, :].astype(np.float32)
    af = np.where(i >= j, i - j, 1e30).astype(np.float32)
    ab = np.where(j > i, j - i, 1e30).astype(np.float32)
    st["adec"] = np.stack([af, ab], 1)
    jj = np.arange(128).astype(np.float32)
    st["zexp"] = np.stack([127.0 - jj, jj], 1).astype(np.float32)
    ii = np.arange(128).astype(np.float32)
    xi = np.stack([ii + 1.0, 128.0 - ii], 0)
    st["xiexp"] = np.ascontiguousarray(np.broadcast_to(xi[None], (128, 2, 128))).astype(np.float32)
    qi = np.arange(128)[:, None]
    kj = np.arange(384)[None, :] - 128
    rel = kj - qi
    bk = t5_buckets(rel)
    inw = np.abs(rel) <= 128
    eb = np.zeros((33, 128, 384), np.float32)
    for b in range(32):
        eb[b] = ((bk == b) & inw)
    eb[32] = ~inw
    st["ebuck"] = eb
    return st


def build_nc(cfg, debug_outs=(), stop=99):
    c = cfg
    D, N, KC, T, HR, HA, KV, DFF, F = c.D, c.N, c.KC, c.T, c.HR, c.HA, c.KV, c.DFF, c.F
    NCH = c.NCH
    nc = bass.Bass("TRN2", target_bir_lowering=False)

    def din(name, shape, dt=F32):
        return nc.dram_tensor(name, list(shape), dt, kind="ExternalInput").ap()

    def dscr(name, shape, dt=BF16):
        kind = "ExternalOutput" if name in debug_outs else "Internal"
        return nc.dram_tensor(name, list(shape), dt, kind=kind).ap()

    x_own = din("x_own", [N, D])
    x_oth = din("x_oth", [N, D])
    x_halo = din("x_halo", [256, D])
    mem = din("mem", [256, D])
    w_in = din("w_in", [D, c.INW])
    w_out = din("w_out", [D, D])
    w_cq = din("w_cq", [D, 512])
    w_ckv = din("w_ckv", [D, 1024])
    w_co = din("w_co", [512, D])
    w1 = din("w1", [D, DFF])
    w2 = din("w2", [DFF, D])
    gcols_d = din("gcols", [128, 4, KC])
    gfin_d = din("gfin", [128, D])
    cs_own = din("cs_own", [2, 128, N])
    cs_oth = din("cs_oth", [2, 128, N])
    dec_d = din("dec", [128, 2 * HR])
    sink_d = din("sink", [128, HA])
    relb_d = din("relb", [128, 32 * HA])
    flags_d = din("flags", [128, 4])
    ident_d = din("ident", [128, 128])
    rswap_d = din("rswap", [128, 128])
    adec_d = din("adec", [128, 2, 128])
    zexp_d = din("zexp", [128, 2])
    xiexp_d = din("xiexp", [128, 2, 128])
    ebuck_d = din("ebuck", [33, 128, 384])
    y_out = nc.dram_tensor("y", [N, D], F32, kind="ExternalOutput").ap()

    w_in_b = dscr("w_in_b", [D, c.INW])
    w_out_b = dscr("w_out_b", [D, D])
    w_cq_b = dscr("w_cq_b", [D, 512])
    w_ckv_b = dscr("w_ckv_b", [D, 1024])
    w_co_b = dscr("w_co_b", [512, D])
    w1_b = dscr("w1_b", [D, DFF])
    w2_b = dscr("w2_b", [DFF, D])
    qrT = dscr("qrT", [HR, 128, N])
    krT = dscr("krT", [HR, 128, 2 * N])
    vr = dscr("vr", [2 * N, c.RW])
    gT = dscr("gT", [HR, 128, N])
    qaT = dscr("qaT", [HA, 128, N])
    kaT = dscr("kaT", [KV, 128, N + 256])
    va = dscr("va", [N + 256, c.AKV])
    kmT = dscr("kmT", [4, 128, 256])
    vm = dscr("vm", [256, 512])
    ymixT = dscr("ymixT", [2 * HR, 128, N])

    S = Sched(nc, same_eng_sync=bool(int(_os.environ.get("SES", "1"))))
    A = S.add

    base0 = nc.sbuf_base
    top = nc.sbuf_top
    cur = [(base0 + 63) // 64 * 64]

    def sb(name, shape, dt):
        nbytes = int(np.prod(shape[1:])) * (4 if dt == F32 else 2)
        off = cur[0]
        cur[0] = (off + nbytes + 63) // 64 * 64
        assert cur[0] <= top, ("SBUF overflow", name, cur[0], top)
        return nc.alloc_sbuf_tensor_at(name, list(shape), dt, offset=off).ap()

    ident = sb("ident", [128, 128], BF16)
    rswap = sb("rswap", [128, 128], BF16)
    ones = sb("ones", [128, 128], BF16)
    gcols = sb("gcols", [128, 4, KC], F32)
    flags = sb("flags", [128, 4], F32)
    lg = sb("lg", [128, 2 * HR], F32)
    cdec = sb("cdec", [128, 2 * HR], F32)
    zfb = sb("zfb", [128, 2, HR], F32)
    sink = sb("sink", [128, HA], F32)
    stat = sb("stat", [128, 8], F32)
    tmpf = sb("tmpf", [128, 128], F32)
    persist_end = cur[0]

    psall = nc.alloc_psum_tensor("psall", [128, 4096], F32).ap()
    psum = [psall[:, i * 512:(i + 1) * 512] for i in range(8)]

    def psb(i):
        return psum[i].bitcast(BF16)

    def cast_w(src, dst, name, nsplit):
        rows = src.shape[0]
        rp = rows // nsplit
        for i in range(nsplit):
            A("pool", _c("dma_start", out=dst[i * rp:(i + 1) * rp, :], in_=src[i * rp:(i + 1) * rp, :]),
              w=[name], dma=True)

    cast_w(w_ckv, w_ckv_b, "w_ckv_b", 1)
    cast_w(w_in, w_in_b, "w_in_b", 4)

    def load(eng, dst, src, wtok, rtok=()):
        return A(eng, _c("dma_start", out=dst, in_=src), r=list(rtok), w=list(wtok), dma=True)

    p0 = cur[0]
    identf = sb("identf", [128, 128], F32)
    rswapf = sb("rswapf", [128, 128], F32)
    adec = sb("adec", [128, 2, 128], F32)
    zexp = sb("zexp", [128, 2], F32)
    xiexp = sb("xiexp", [128, 2, 128], F32)
    decs = sb("decs", [128, 2 * HR], F32)
    load("sp", identf, ident_d, ["identf"])
    load("sp", rswapf, rswap_d, ["rswapf"])
    load("sp", gcols, gcols_d, ["gcols"])
    load("sp", flags, flags_d, ["flags"])
    load("sp", decs, dec_d, ["decs"])
    load("sp", sink, sink_d, ["sink"])
    load("sp", adec, adec_d, ["adec"])
    load("sp", zexp, zexp_d, ["zexp"])
    load("sp", xiexp, xiexp_d, ["xiexp"])
    A("dve", _c("tensor_copy", ident, identf), r=["identf"], w=["ident"])
    A("dve", _c("tensor_copy", rswap, rswapf), r=["rswapf"], w=["rswap"])
    A("dve", _c("memset", ones, 1.0), w=["ones"])
    A("act", _c("activation", out=lg, in_=decs, func=AF.Exp), r=["decs"], w=["lg"])
    A("dve", _c("tensor_scalar", out=lg, in0=lg, scalar1=-1.0, scalar2=None, op0=ALU.mult), r=["lg"], w=["lg"])
    A("act", _c("activation", out=cdec, in_=lg, func=AF.Exp, scale=128.0), r=["lg"], w=["cdec"])
    for d_ in range(2):
        A("dve", _c("tensor_scalar", out=zfb[:, d_, :], in0=lg[:, d_ * HR:(d_ + 1) * HR], scalar1=zexp[:, d_:d_ + 1],
                                                 scalar2=None, op0=ALU.mult), r=["lg", "zexp"], w=["zfb"])
    A("act", _c("activation", out=zfb, in_=zfb, func=AF.Exp), r=["zfb"], w=["zfb"])
    S.barrier()
    tables_end = cur[0]
    if stop <= 0:
        S.emit()
        return nc, S

    cur[0] = tables_end
    WP = 512
    hT = sb("hT", [128, KC, T], BF16)
    xbuf = [sb("xbuf%d" % i, [128, D], F32) for i in range(2)]
    hnb = [sb("hn%d" % i, [128, D], BF16) for i in range(2)]
    hcnt = [0]
    junk = sb("junk", [128, D], BF16)
    NWB = 2
    wbuf = [sb("wbuf%d" % i, [128, KC, WP], BF16) for i in range(NWB)]
    cst = sb("cst", [128, 2, T], F32)
    stage = [sb("stage%d" % i, [128, 4, T], BF16) for i in range(2)]
    qsb = [sb("qsb%d" % i, [128, T], BF16) for i in range(2)]
    t1 = [sb("t1_%d" % i, [128, T], F32) for i in range(2)]
    t2 = [sb("t2_%d" % i, [128, T], F32) for i in range(2)]
    cnt = {"x": 0, "w": 0, "stage": 0, "acc": 0, "q": 0, "tp": 0}

    def rms_stats(xt, xtok, dcol):
        A("act", _c("activation", out=junk, in_=xt, func=AF.Square, accum_out=stat[:, dcol:dcol + 1]),
          r=[xtok], w=["junk", "stat%d" % dcol])
        A("dve", _c("tensor_scalar", out=stat[:, dcol:dcol + 1], in0=stat[:, dcol:dcol + 1], scalar1=1.0 / D, scalar2=EPS,
                                           op0=ALU.mult, op1=ALU.add), r=["stat%d" % dcol], w=["stat%d" % dcol])
        A("act", _c("activation", out=stat[:, dcol:dcol + 1], in_=stat[:, dcol:dcol + 1], func=AF.Sqrt),
          r=["stat%d" % dcol], w=["stat%d" % dcol])
        A("dve", _c("reciprocal", stat[:, dcol:dcol + 1], stat[:, dcol:dcol + 1]), r=["stat%d" % dcol], w=["stat%d" % dcol])

    def norm_transpose(xt, xtok, gi, hT_, st, htok, dcol=0):
        rms_stats(xt, xtok, dcol)
        hi = hcnt[0] % 2
        hcnt[0] += 1
        hn = hnb[hi]
        hntok = "hn%d" % hi
        A("dve", _c("tensor_scalar", out=hn, in0=xt, scalar1=stat[:, dcol:dcol + 1], scalar2=None, op0=ALU.mult),
          r=[xtok, "stat%d" % dcol], w=[hntok])
        for g8 in range(KC // 8):
            bk = cnt["tp"] % 2
            cnt["tp"] += 1
            pt = psb(bk)
            for j in range(8):
                kc = g8 * 8 + j
                A("pe", _c("transpose", pt[:, j * 128:(j + 1) * 128], hn[:, kc * 128:(kc + 1) * 128], ident),
                  r=[hntok, "ident"], w=["ps%d" % bk])
            ptv = pt.rearrange("p (k t) -> p k t", k=8)
            gb = gcols[:, gi, g8 * 8:(g8 + 1) * 8].unsqueeze(2).to_broadcast([128, 8, 128])
            eng = "dve" if g8 % 2 == 0 else "pool"
            if eng == "pool":
                eng = "dve"
            A(eng, _c("tensor_tensor", out=hT_[:, g8 * 8:(g8 + 1) * 8, st * 128:(st + 1) * 128], in0=ptv, in1=gb,
                                                                  op=ALU.mult), r=["ps%d" % bk, "gcols"], w=[htok])

    def load_w(wb_ap, c0, W):
        i = cnt["w"] % NWB
        cnt["w"] += 1
        load("sp", wbuf[i][:, :, 0:W], wb_ap[:, c0:c0 + W].rearrange("(kc p) n -> p kc n", p=128), ["wbuf%d" % i], [wb_ap.tensor.name])
        return i

    ptc = [0]
    import os
    PTMAX = int(os.environ.get("PTMAX", "9999"))

    def proj_tile(xsrc, ntok, gi, pieces, cs_src=None):
        ptc[0] += 1
        if ptc[0] > PTMAX:
            return
        nsub = ntok // 128
        for st in range(nsub):
            xi = cnt["x"] % 2
            cnt["x"] += 1
            load("sp", xbuf[xi], xsrc[st * 128:(st + 1) * 128, :], ["xbuf%d" % xi])
            norm_transpose(xbuf[xi], "xbuf%d" % xi, gi, hT, st, "hT", dcol=st % 2)
        if cs_src is not None:
            load("sp", cst[:, :, 0:ntok], cs_src.rearrange("c p t -> p c t"), ["cst"])
        for (w_ap, c0, W, kind, destfn) in pieces:
            wi = load_w(w_ap, c0, W)
            wtok = "wbuf%d" % wi
            nch = W // 128
            si = cnt["stage"] % 2
            cnt["stage"] += 1
            stg = stage[si]
            stok = "stage%d" % si
            if kind == "tm":
                for st in range(nsub):
                    bk = 2 + cnt["acc"] % 3
                    cnt["acc"] += 1
                    for kc in range(KC):
                        A("pe", _c("matmul", psum[bk][:, 0:W], lhsT=hT[:, kc, st * 128:(st + 1) * 128],
                                                                       rhs=wbuf[wi][:, kc, 0:W], start=(kc == 0), stop=(kc == KC - 1)),
                          r=["hT", wtok], w=["ps%d" % bk])
                    sv = stg.rearrange("p a t -> p (a t)")[:, st * W:(st + 1) * W]
                    A("act", _c("activation", out=sv, in_=psum[bk][:, 0:W], func=AF.Copy), r=["ps%d" % bk], w=[stok])
                    A("pool", _c("dma_start", out=destfn(st), in_=sv), r=[stok], w=["scr"], dma=True)
                continue
            for ch in range(nch):
                bk = 2 + cnt["acc"] % 3
                cnt["acc"] += 1
                acc = psum[bk][:, 0:ntok]
                for kc in range(KC):
                    A("pe", _c("matmul", acc, lhsT=wbuf[wi][:, kc, ch * 128:(ch + 1) * 128], rhs=hT[:, kc, 0:ntok],
                                                                     start=(kc == 0), stop=(kc == KC - 1)), r=["hT", wtok], w=["ps%d" % bk])
                so = stg[:, ch, 0:ntok]
                if kind == "copy" or (kind.startswith("rope") and _os.environ.get("NOROPE")):
                    A("act", _c("activation", out=so, in_=acc, func=AF.Copy), r=["ps%d" % bk], w=[stok])
                elif kind == "silu":
                    A("act", _c("activation", out=so, in_=acc, func=AF.Silu), r=["ps%d" % bk], w=[stok])
                else:
                    qi = cnt["q"] % 2
                    cnt["q"] += 1
                    rb = 5 + qi
                    sc = 1.0 if kind == "rope_q" else 128.0 ** -0.5
                    A("act", _c("activation", out=qsb[qi][:, 0:ntok], in_=acc, func=AF.Copy), r=["ps%d" % bk], w=["qsb%d" % qi])
                    A("pe", _c("matmul", psum[rb][:, 0:ntok], lhsT=rswap, rhs=qsb[qi][:, 0:ntok], start=True, stop=True),
                      r=["qsb%d" % qi, "rswap"], w=["ps%d" % rb])
                    A("dve", _c("scalar_tensor_tensor", out=t1[qi][:, 0:ntok], in0=acc, scalar=sc, in1=cst[:, 0, 0:ntok],
                                                                                     op0=ALU.mult, op1=ALU.mult), r=["ps%d" % bk, "cst"], w=["t1_%d" % qi])
                    A("dve", _c("scalar_tensor_tensor", out=t2[qi][:, 0:ntok], in0=psum[rb][:, 0:ntok], scalar=sc,
                                                                                   in1=cst[:, 1, 0:ntok], op0=ALU.mult, op1=ALU.mult),
                      r=["ps%d" % rb, "cst"], w=["t2_%d" % qi])
                    A(POOLC, _c("tensor_tensor", out=so, in0=t1[qi][:, 0:ntok], in1=t2[qi][:, 0:ntok], op=ALU.add),
                      r=["t1_%d" % qi, "t2_%d" % qi], w=[stok])
            for (dst, sl) in destfn(nch):
                A("pool", _c("dma_start", out=dst, in_=stg[:, 0:nch, sl]), r=[stok], w=["scr"], dma=True)

    def fm_dest(arr, h0, t0, ntok):
        def f(nch):
            return [(arr[h0:h0 + nch, :, t0:t0 + ntok].rearrange("h p t -> p h t"), slice(0, ntok))]
        return f

    def pieces_for(colbase, width, kind, mk):
        out = []
        c0 = 0
        while c0 < width:
            W = min(WP, width - c0)
            out.append((w_in_b, colbase + c0, W, kind, mk(c0, W)))
            c0 += W
        return out

    def mem_pieces():
        ps_ = []
        ps_.append((w_ckv_b, 0, 512, "copy", lambda nch: [(kmT[0:4, :, 0:256].rearrange("h p t -> p h t"), slice(0, 256))]))
        ps_.append((w_ckv_b, 512, 512, "tm", lambda st: vm[st * 128:(st + 1) * 128, :]))
        return ps_

    proj_tile(mem, 256, 2, mem_pieces())

    def halo_pieces():
        ps_ = []
        kbase = 4 * c.RW + c.AQ
        vbase = kbase + c.AKV

        def mkk(c0, W):
            h0 = c0 // 128
            return lambda nch: [(kaT[h0:h0 + nch, :, 0:128].rearrange("h p t -> p h t"), slice(0, 128)),
                                (kaT[h0:h0 + nch, :, N + 128:N + 256].rearrange("h p t -> p h t"), slice(128, 256))]

        def mkv(c0, W):
            return lambda st: va[(0 if st == 0 else N + 128):(128 if st == 0 else N + 256), c0:c0 + W]
        ps_ += pieces_for(kbase, c.AKV, "copy", mkk)
        ps_ += pieces_for(vbase, c.AKV, "tm", mkv)
        return ps_

    proj_tile(x_halo, 256, 0, halo_pieces())

    for ti in range(N // T):
        t0 = ti * T

        def mkk(c0, W, t0=t0):
            return fm_dest(krT, c0 // 128, N + t0, T)

        def mkv(c0, W, t0=t0):
            return lambda st: vr[N + t0 + st * 128:N + t0 + (st + 1) * 128, c0:c0 + W]
        pcs = pieces_for(c.RW, c.RW, "rope_k", mkk) + pieces_for(2 * c.RW, c.RW, "tm", mkv)
        proj_tile(x_oth[t0:t0 + T, :], T, 0, pcs, cs_oth[:, :, t0:t0 + T])

    for ti in range(N // T):
        t0 = ti * T
        pcs = []
        pcs += pieces_for(0, c.RW, "rope_q", lambda c0, W, t0=t0: fm_dest(qrT, c0 // 128, t0, T))
        pcs += pieces_for(c.RW, c.RW, "rope_k", lambda c0, W, t0=t0: fm_dest(krT, c0 // 128, t0, T))
        pcs += pieces_for(2 * c.RW, c.RW, "tm", lambda c0, W, t0=t0: (lambda st: vr[t0 + st * 128:t0 + (st + 1) * 128, c0:c0 + W]))
        pcs += pieces_for(3 * c.RW, c.RW, "silu", lambda c0, W, t0=t0: fm_dest(gT, c0 // 128, t0, T))
        pcs += pieces_for(4 * c.RW, c.AQ, "copy", lambda c0, W, t0=t0: fm_dest(qaT, c0 // 128, t0, T))
        pcs += pieces_for(4 * c.RW + c.AQ, c.AKV, "copy", lambda c0, W, t0=t0: fm_dest(kaT, c0 // 128, 128 + t0, T))
        pcs += pieces_for(4 * c.RW + c.AQ + c.AKV, c.AKV, "tm",
                          lambda c0, W, t0=t0: (lambda st: va[128 + t0 + st * 128:128 + t0 + (st + 1) * 128, c0:c0 + W]))
        proj_tile(x_own[t0:t0 + T, :], T, 0, pcs, cs_own[:, :, t0:t0 + T])

    S.barrier()
    if stop <= 2:
        S.emit()
        return nc, S
    cast_w(w_out, w_out_b, "w_out_b", 2)
    cast_w(w_cq, w_cq_b, "w_cq_b", 1)
    cast_w(w_co, w_co_b, "w_co_b", 1)
    cast_w(w1, w1_b, "w1_b", 8)
    cast_w(w2, w2_b, "w2_b", 8)

    cur[0] = tables_end
    NT2 = 2 * N
    qT = [sb("qT%d" % i, [128, N], BF16) for i in range(2)]
    kT = [sb("kT%d" % i, [128, NT2], BF16) for i in range(2)]
    vtm = [sb("vtm%d" % i, [128, 2 * NCH, 128], BF16) for i in range(2)]
    gTs = [sb("gTs%d" % i, [128, N], BF16) for i in range(2)]
    DT = sb("DT", [128, 128], F32)
    dtmp = sb("dtmp", [128, 2, 128], F32)
    XI = sb("XI", [128, 2, 128], F32)
    cpw = sb("cpw", [128, 2, NCH], F32)
    kz = [sb("kz%d" % i, [128, 8, 128], BF16) for i in range(2)]
    SFs = sb("SFs", [128, 128], F32)
    SBs = sb("SBs", [128, 128], F32)
    SFst = sb("SFst", [128, NCH, 128], BF16)
    SBst = sb("SBst", [128, NCH, 128], BF16)
    qfb = [sb("qfb%d" % i, [128, 2, 512], BF16) for i in range(2)]
    attm = [sb("attm%d" % i, [128, 512], BF16) for i in range(2)]
    ysq = [sb("ysq%d" % i, [128, 512], BF16) for i in range(2)]
    rstd = [sb("rstd%d" % i, [128, 512], F32) for i in range(2)]
    ynf = sb("ynf", [128, 512], F32)
    yo = [sb("yo%d" % i, [128, 512], BF16) for i in range(2)]
    cexp = sb("cexp", [128, 2, NCH], F32)
    for cc in range(NCH):
        A("dve", _c("memset", cexp[:, 0, cc:cc + 1], 128.0 * (NCH - 1 - cc)), w=["cexp"])
        A("dve", _c("memset", cexp[:, 1, cc:cc + 1], 128.0 * cc), w=["cexp"])

    def r_loads(h):
        b = h % 2
        load("sp", qT[b], qrT[h], ["qT%d" % b], ["scr"])
        load("sp", kT[b], krT[h], ["kT%d" % b], ["scr"])
        load("sp", vtm[b], vr[:, h * 128:(h + 1) * 128].rearrange("(c p) e -> p c e", p=128), ["vtm%d" % b], ["scr"])
        load("sp", gTs[b], gT[h], ["gTs%d" % b], ["scr"])

    r_loads(0)
    kvc = [0]
    for h in range(HR):
        hb = h % 2
        qTh, kTh, vth, gTh = qT[hb], kT[hb], vtm[hb], gTs[hb]
        qtok, ktok, vtok, gtok = "qT%d" % hb, "kT%d" % hb, "vtm%d" % hb, "gTs%d" % hb
        if h + 1 < HR:
            r_loads(h + 1)
        for d_ in range(2):
            lgc = lg[:, d_ * HR + h:d_ * HR + h + 1]
            A("act", _c("activation", out=dtmp[:, d_, :], in_=adec[:, d_, :], func=AF.Exp, scale=lgc), r=["adec", "lg"], w=["dtmp"])
            A("act", _c("activation", out=XI[:, d_, :], in_=xiexp[:, d_, :], func=AF.Exp, scale=lgc), r=["xiexp", "lg"], w=["XI"])
            A("act", _c("activation", out=cpw[:, d_, :], in_=cexp[:, d_, :], func=AF.Exp, scale=lgc), r=["cexp", "lg"], w=["cpw"])
        A("dve", _c("tensor_tensor", out=DT, in0=dtmp[:, 0, :], in1=dtmp[:, 1, :], op=ALU.add), r=["dtmp"], w=["DT"])
        A("dve", _c("memset", SFs, 0.0), w=["SFs"])
        A("dve", _c("memset", SBs, 0.0), w=["SBs"])
        NG = NCH // 8
        groups = []
        for d_ in range(2):
            for g in range(NG):
                groups.append((d_, False, g))
        for g in range(NG):
            groups.append((0, True, g))
        for g in reversed(range(NG)):
            groups.append((1, True, g))

        def st_T(j):
            d_, own, g = groups[j]
            base = (0 if own else NCH) + g * 8
            i = j % 2
            pt = psb(0)
            for jj in range(8):
                cc = base + jj
                A("pe", _c("transpose", pt[:, jj * 128:(jj + 1) * 128], kTh[:, cc * 128:(cc + 1) * 128], ident), r=[ktok, "ident"], w=["ps0"])
            A("act", _c("activation", out=kz[i].rearrange("p a b -> p (a b)"), in_=pt, func=AF.Copy, scale=zfb[:, d_, h:h + 1]),
              r=["ps0", "zfb"], w=["kz%d" % i])

        def st_KV(j):
            d_, own, g = groups[j]
            i = j % 2
            Stile, stok = (SFs, "SFs") if d_ == 0 else (SBs, "SBs")
            Sst, sstok = (SFst, "SFst") if d_ == 0 else (SBst, "SBst")
            base = (0 if own else NCH) + g * 8
            if own and g == (0 if d_ == 0 else NG - 1):
                fl = flags[:, 1:2] if d_ == 0 else flags[:, 0:1]
                A("dve", _c("tensor_scalar", out=Stile, in0=Stile, scalar1=fl, scalar2=None, op0=ALU.mult), r=[stok, "flags"], w=[stok])
            halves = [0, 1] if (d_ == 0 or not own) else [1, 0]
            for half in halves:
                bk = 1 + kvc[0] % 2
                kvc[0] += 1
                for jq in range(4):
                    jj = half * 4 + jq
                    A("pe", _c("matmul", psum[bk][:, jq * 128:(jq + 1) * 128], lhsT=kz[i][:, jj, :], rhs=vth[:, base + jj, :], start=True, stop=True),
                      r=["kz%d" % i, vtok], w=["ps%d" % bk])
                js = [0, 1, 2, 3] if (d_ == 0 or not own) else [3, 2, 1, 0]
                for jq in js:
                    cc = g * 8 + half * 4 + jq
                    kvp = psum[bk][:, jq * 128:(jq + 1) * 128]
                    if not own:
                        A("dve", _c("scalar_tensor_tensor", out=Stile, in0=kvp, scalar=cpw[:, d_, cc:cc + 1], in1=Stile, op0=ALU.mult, op1=ALU.add),
                          r=["ps%d" % bk, "cpw", stok], w=[stok])
                    else:
                        A("act", _c("activation", out=Sst[:, cc, :], in_=Stile, func=AF.Copy), r=[stok], w=[sstok])
                        A("dve", _c("scalar_tensor_tensor", out=Stile, in0=Stile, scalar=cdec[:, d_ * HR + h:d_ * HR + h + 1], in1=kvp,
                                    op0=ALU.mult, op1=ALU.add), r=["ps%d" % bk, "cdec", stok], w=[stok])

        for j in range(len(groups) + 1):
            if j < len(groups):
                st_T(j)
            if j >= 1:
                st_KV(j - 1)

        G4 = NCH // 4

        def o_s1(i):
            i2 = i % 2
            tsl = slice(i * 512, (i + 1) * 512)
            for d_ in range(2):
                A("dve", _c("tensor_tensor", out=qfb[i2][:, d_, :].rearrange("p (a b) -> p a b", a=4), in0=qTh[:, tsl].rearrange("p (a b) -> p a b", a=4),
                            in1=XI[:, d_, :].unsqueeze(1).to_broadcast([128, 4, 128]), op=ALU.mult), r=[qtok, "XI"], w=["qfb%d" % i2])
            bS = 3 + i2
            for jq in range(4):
                cc = i * 4 + jq
                A("pe", _c("matmul", psum[bS][:, jq * 128:(jq + 1) * 128], lhsT=kTh[:, cc * 128:(cc + 1) * 128], rhs=qTh[:, cc * 128:(cc + 1) * 128],
                           start=True, stop=True), r=[ktok, qtok], w=["ps%d" % bS])
            A("dve", _c("tensor_tensor", out=attm[i2].rearrange("p (a b) -> p a b", a=4), in0=psum[bS].rearrange("p (a b) -> p a b", a=4),
                        in1=DT.unsqueeze(1).to_broadcast([128, 4, 128]), op=ALU.mult), r=["ps%d" % bS, "DT"], w=["attm%d" % i2])

        def o_s2(i):
            i2 = i % 2
            bY = 5 + i2
            for jq in range(4):
                cc = i * 4 + jq
                ysl = psum[bY][:, jq * 128:(jq + 1) * 128]
                A("pe", _c("matmul", ysl, lhsT=vth[:, cc, :], rhs=attm[i2][:, jq * 128:(jq + 1) * 128], start=True, stop=False),
                  r=[vtok, "attm%d" % i2], w=["ps%d" % bY])
                A("pe", _c("matmul", ysl, lhsT=SFst[:, cc, :], rhs=qfb[i2][:, 0, jq * 128:(jq + 1) * 128], start=False, stop=False),
                  r=["SFst", "qfb%d" % i2], w=["ps%d" % bY])
                A("pe", _c("matmul", ysl, lhsT=SBst[:, cc, :], rhs=qfb[i2][:, 1, jq * 128:(jq + 1) * 128], start=False, stop=True),
                  r=["SBst", "qfb%d" % i2], w=["ps%d" % bY])
            A("act", _c("activation", out=ysq[i2], in_=psum[bY], func=AF.Square), r=["ps%d" % bY], w=["ysq%d" % i2])

        def o_s3a(i):
            i2 = i % 2
            A("pe", _c("matmul", psum[7], lhsT=ones, rhs=ysq[i2], start=True, stop=True), r=["ysq%d" % i2, "ones"], w=["ps7"])
            A("dve", _c("tensor_scalar", out=rstd[i2], in0=psum[7], scalar1=1.0 / 128, scalar2=EPS, op0=ALU.mult, op1=ALU.add), r=["ps7"], w=["rstd%d" % i2])
            A("act", _c("activation", out=rstd[i2], in_=rstd[i2], func=AF.Sqrt), r=["rstd%d" % i2], w=["rstd%d" % i2])

        def o_s3b(i):
            i2 = i % 2
            bY = 5 + i2
            tsl = slice(i * 512, (i + 1) * 512)
            A("dve", _c("reciprocal", rstd[i2], rstd[i2]), r=["rstd%d" % i2], w=["rstd%d" % i2])
            A("dve", _c("tensor_tensor", out=ynf, in0=psum[bY], in1=rstd[i2], op=ALU.mult), r=["ps%d" % bY, "rstd%d" % i2], w=["ynf"])
            A("dve", _c("tensor_tensor", out=yo[i2], in0=ynf, in1=gTh[:, tsl], op=ALU.mult), r=["ynf", gtok], w=["yo%d" % i2])
            A("sp", _c("dma_start", out=ymixT[h, :, tsl], in_=yo[i2]), r=["yo%d" % i2], w=["ymix"], dma=True)

        for i in range(G4 + 2):
            if i - 2 >= 0:
                o_s3a(i - 2)
            if i < G4:
                o_s1(i)
            if 0 <= i - 1 < G4:
                o_s2(i - 1)
            if i - 2 >= 0:
                o_s3b(i - 2)

    S.barrier()
    if stop <= 3:
        S.emit()
        return nc, S
    cur[0] = tables_end
    qA = [sb("qA%d" % i, [128, N], BF16) for i in range(2)]
    kA = [sb("kA%d" % i, [128, N + 256], BF16) for i in range(2)]
    vA = [sb("vA%d" % i, [128, NCH + 2, 128], BF16) for i in range(2)]
    BI = [sb("BI%d" % i, [128, 384], F32) for i in range(2)]
    relb = sb("relb", [128, 32 * HA], F32)
    EB = sb("EB", [128, 33, 384], F32)
    sS = [sb("sS%d" % i, [128, 4, 384], F32) for i in range(2)]
    pS = [sb("pS%d" % i, [128, 4, 384], F32) for i in range(2)]
    pn = [sb("pn%d" % i, [128, 4, 384], BF16) for i in range(2)]
    pT = [sb("pT%d" % i, [128, 1536], BF16) for i in range(2)]
    sm = [sb("sm%d" % i, [128, 24], F32) for i in range(2)]
    yoA = [sb("yoA%d" % i, [128, 512], BF16) for i in range(2)]
    load("sp", relb, relb_d, ["relb"])
    load("sp", EB, ebuck_d.rearrange("b p j -> p b j"), ["EB"])
    scale = 128.0 ** -0.5
    psS = psall[:, 0:2048].rearrange("p (b c) -> p b c", b=4)[:, :, 0:384]
    psTb = psall[:, 2048:3072].bitcast(BF16)
    G4 = NCH // 4

    BIp = sb("BIp", [128, 4, 384], F32)

    def a_loads(h):
        b = h % 2
        load("sp", qA[b], qaT[h], ["qA%d" % b], ["scr"])
        if h % 4 == 0:
            kb = (h // 4) % 2
            load("sp", kA[kb], kaT[h // 4], ["kA%d" % kb], ["scr"])
            load("sp", vA[kb], va[:, (h // 4) * 128:(h // 4 + 1) * 128].rearrange("(c p) e -> p c e", p=128), ["vA%d" % kb], ["scr"])
        bt = "BI%d" % b
        for bb in range(33):
            a_ = bb % 4
            sc_ = relb[:, bb * HA + h:bb * HA + h + 1] if bb < 32 else NEG
            if bb < 4:
                A("dve", _c("tensor_scalar", out=BIp[:, a_, :], in0=EB[:, bb, :], scalar1=sc_, scalar2=None, op0=ALU.mult), r=["EB", "relb"], w=["BIp%d" % a_])
            else:
                A("dve", _c("scalar_tensor_tensor", out=BIp[:, a_, :], in0=EB[:, bb, :], scalar=sc_, in1=BIp[:, a_, :], op0=ALU.mult, op1=ALU.add),
                  r=["EB", "relb", "BIp%d" % a_], w=["BIp%d" % a_])
        A("dve", _c("tensor_tensor", out=BIp[:, 0, :], in0=BIp[:, 0, :], in1=BIp[:, 1, :], op=ALU.add), r=["BIp0", "BIp1"], w=["BIp0"])
        A("dve", _c("tensor_tensor", out=BIp[:, 2, :], in0=BIp[:, 2, :], in1=BIp[:, 3, :], op=ALU.add), r=["BIp2", "BIp3"], w=["BIp2"])
        A("dve", _c("tensor_tensor", out=BI[b], in0=BIp[:, 0, :], in1=BIp[:, 2, :], op=ALU.add), r=["BIp0", "BIp2"], w=[bt])

    items = [(h, g) for h in range(HA) for g in range(G4)]

    def b_A(k):
        h, g = items[k]
        hb = h % 2
        kb = (h // 4) % 2
        for b in range(4):
            n = g * 4 + b
            A("pe", _c("matmul", psS[:, b, :], lhsT=qA[hb][:, n * 128:(n + 1) * 128], rhs=kA[kb][:, n * 128:n * 128 + 384], start=True, stop=True),
              r=["qA%d" % hb, "kA%d" % kb], w=["ps%d" % b])

    def b_B(k):
        h, g = items[k]
        i2 = k % 2
        hb = h % 2
        pst = ["ps0", "ps1", "ps2", "ps3"]
        st_ = "sS%d" % i2
        m = sm[i2]
        ops = []
        ops.append(("dve", _c("scalar_tensor_tensor", out=sS[i2], in0=psS, scalar=scale, in1=BI[hb].unsqueeze(1).to_broadcast([128, 4, 384]),
                              op0=ALU.mult, op1=ALU.add), pst + ["BI%d" % hb], [st_]))
        if g == 0:
            ops.append(("dve", _c("tensor_scalar", out=sS[i2][:, 0, 0:128], in0=sS[i2][:, 0, 0:128], scalar1=flags[:, 2:3], scalar2=None, op0=ALU.add),
                        [st_, "flags"], [st_]))
        if g == G4 - 1:
            ops.append(("dve", _c("tensor_scalar", out=sS[i2][:, 3, 256:384], in0=sS[i2][:, 3, 256:384], scalar1=flags[:, 3:4], scalar2=None, op0=ALU.add),
                        [st_, "flags"], [st_]))
        ops.append(("dve", _c("reduce_max", out=m[:, 0:4], in_=sS[i2], axis=AX.X), [st_], ["smx%d" % i2]))
        ops.append(("dve", _c("tensor_scalar", out=m[:, 4:8], in0=m[:, 0:4], scalar1=sink[:, h:h + 1], scalar2=-1.0, op0=ALU.max, op1=ALU.mult),
                    ["smx%d" % i2, "sink"], ["snm%d" % i2]))
        return ops

    def b_C(k):
        h, g = items[k]
        i2 = k % 2
        m = sm[i2]
        for b in range(4):
            A("act", _c("activation", out=pS[i2][:, b, :], in_=sS[i2][:, b, :], func=AF.Exp, bias=m[:, 4 + b:5 + b], accum_out=m[:, 8 + b:9 + b]),
              r=["sS%d" % i2, "snm%d" % i2], w=["pS%d_%d" % (i2, b), "sac%d_%d" % (i2, b)])
        A("act", _c("activation", out=m[:, 12:16], in_=m[:, 4:8], func=AF.Exp, bias=sink[:, h:h + 1]), r=["snm%d" % i2, "sink"], w=["ses%d" % i2])

    def b_D(k):
        i2 = k % 2
        m = sm[i2]
        ops = []
        ops.append(("dve", _c("tensor_tensor", out=m[:, 16:20], in0=m[:, 8:12], in1=m[:, 12:16], op=ALU.add),
                    ["sac%d_%d" % (i2, b) for b in range(4)] + ["ses%d" % i2], ["sdn%d" % i2]))
        ops.append(("dve", _c("reciprocal", m[:, 20:24], m[:, 16:20]), ["sdn%d" % i2], ["srd%d" % i2]))
        ops.append(("dve", _c("tensor_tensor", out=pn[i2], in0=pS[i2], in1=m[:, 20:24].unsqueeze(2).to_broadcast([128, 4, 384]), op=ALU.mult),
                    ["pS%d_%d" % (i2, b) for b in range(4)] + ["srd%d" % i2], ["pn%d" % i2]))
        return ops

    def b_EF(k):
        i2 = k % 2
        for b in range(4):
            for t in range(3):
                q = b * 3 + t
                A("pe", _c("transpose", psTb[:, q * 128:(q + 1) * 128], pn[i2][:, b, t * 128:(t + 1) * 128], ident), r=["pn%d" % i2, "ident"], w=["ps4", "ps5"])
        A("act", _c("activation", out=pT[i2], in_=psTb[:, 0:1536], func=AF.Copy), r=["ps4", "ps5"], w=["pT%d" % i2])

    def b_GH(k):
        h, g = items[k]
        i2 = k % 2
        kb = (h // 4) % 2
        bO = 6 + i2
        for b in range(4):
            n = g * 4 + b
            for t in range(3):
                q = b * 3 + t
                A("pe", _c("matmul", psum[bO][:, b * 128:(b + 1) * 128], lhsT=vA[kb][:, n + t, :], rhs=pT[i2][:, q * 128:(q + 1) * 128],
                           start=(t == 0), stop=(t == 2)), r=["vA%d" % kb, "pT%d" % i2], w=["ps%d" % bO])
        A("act", _c("activation", out=yoA[i2], in_=psum[bO], func=AF.Copy), r=["ps%d" % bO], w=["yoA%d" % i2])
        A("sp", _c("dma_start", out=ymixT[HR + h, :, g * 512:(g + 1) * 512], in_=yoA[i2]), r=["yoA%d" % i2], w=["ymix"], dma=True)

    def interleave(l1, l2):
        out = []
        for i in range(max(len(l1), len(l2))):
            if i < len(l1):
                out.append(l1[i])
            if i < len(l2):
                out.append(l2[i])
        return out

    a_loads(0)
    for k in range(len(items) + 2):
        lb, ld = [], []
        if k < len(items):
            h, g = items[k]
            if g == 0 and h + 1 < HA:
                a_loads(h + 1)
            b_A(k)
            lb = b_B(k)
        if 0 <= k - 1 < len(items):
            ld = b_D(k - 1)
        for (eng_, fn_, r_, w_) in interleave(lb, ld):
            A(eng_, fn_, r=r_, w=w_)
        if k < len(items):
            b_C(k)
        if 0 <= k - 1 < len(items):
            b_EF(k - 1)
        if k - 2 >= 0:
            b_GH(k - 2)

    S.barrier()
    if stop <= 4:
        S.emit()
        return nc, S
    cur[0] = p0
    NST = T // 128
    xacc = [sb("xacc%d" % i, [128, D], F32) for i in range(NST)]
    hT4 = sb("hT4", [128, KC, T], BF16)
    FCH = F // 128
    aT = [sb("aT%d" % i, [128, FCH, T], BF16) for i in range(2)]
    rl = [sb("rl%d" % i, [128, T], F32) for i in range(2)]
    qc = sb("qc", [128, 4, T], BF16)
    oT = sb("oT", [128, 4, T], BF16)
    kmS = sb("kmS", [128, 4, 256], BF16)
    vmS = sb("vmS", [128, 2, 512], BF16)
    hn4b = [sb("hn4_%d" % i, [128, D], BF16) for i in range(2)]
    junkS = sb("junkS", [128, D // 8], BF16)
    st4 = sb("st4", [128, 16], F32)
    h4c = [0]
    cS = [sb("cS%d" % i, [128, 256], F32) for i in range(2)]
    cP = [sb("cP%d" % i, [128, 256], BF16) for i in range(2)]
    cT = [sb("cT%d" % i, [128, 256], BF16) for i in range(2)]
    cm = [sb("cm%d" % i, [128, 8], F32) for i in range(2)]
    SLOT = 16384
    nring = (top - cur[0]) // SLOT
    nring = min(nring, 4)
    assert nring >= 2, ("ring too small", nring)
    print("P4 ring slots", nring, "free bytes", top - cur[0])
    ring = [sb("ring%d" % i, [128, SLOT // 2], BF16) for i in range(nring)]
    rc = [0]

    def ring_load(src_ap, shape):
        i = rc[0] % nring
        rc[0] += 1
        a, b = shape
        v = ring[i][:, 0:a * b].rearrange("p (a b) -> p a b", a=a)
        load("sp", v, src_ap, ["ring%d" % i], [src_ap.tensor.name])
        return v, "ring%d" % i

    load("sp", kmS, kmT.rearrange("h p t -> p h t"), ["kmS"], ["scr"])
    load("sp", vmS, vm.rearrange("(c p) e -> p c e", p=128), ["vmS"], ["scr"])
    acnt = [0]
    PW = min(512, F, SLOT // 2 // KC)
    assert PW >= 128

    def stats4(st, dcol):
        q4 = D // 8
        so = (dcol % 2) * 8
        for q in range(8):
            A("act", _c("activation", out=junkS, in_=xacc[st][:, q * q4:(q + 1) * q4], func=AF.Square, accum_out=st4[:, so + q:so + q + 1]),
              r=["xacc%d" % st], w=["junkS", "st4_%d" % (dcol % 2)])
        A("dve", _c("reduce_sum", out=stat[:, dcol:dcol + 1], in_=st4[:, so:so + 8], axis=AX.X), r=["st4_%d" % (dcol % 2)], w=["stat%d" % dcol])
        A("dve", _c("tensor_scalar", out=stat[:, dcol:dcol + 1], in0=stat[:, dcol:dcol + 1], scalar1=1.0 / D, scalar2=EPS,
                    op0=ALU.mult, op1=ALU.add), r=["stat%d" % dcol], w=["stat%d" % dcol])
        A("act", _c("activation", out=stat[:, dcol:dcol + 1], in_=stat[:, dcol:dcol + 1], func=AF.Sqrt), r=["stat%d" % dcol], w=["stat%d" % dcol])
        A("dve", _c("reciprocal", stat[:, dcol:dcol + 1], stat[:, dcol:dcol + 1]), r=["stat%d" % dcol], w=["stat%d" % dcol])

    def norm_T(gi):
        for st in range(NST):
            dcol = st % 4
            stats4(st, dcol)
            hi = h4c[0] % 2
            h4c[0] += 1
            hn4 = hn4b[hi]
            hntok = "hn4_%d" % hi
            A("dve", _c("tensor_scalar", out=hn4, in0=xacc[st], scalar1=stat[:, dcol:dcol + 1], scalar2=None, op0=ALU.mult),
              r=["xacc%d" % st, "stat%d" % dcol], w=[hntok])
            for g8 in range(KC // 8):
                bk = acnt[0] % 2
                acnt[0] += 1
                pt = psb(bk)
                for j in range(8):
                    kc = g8 * 8 + j
                    A("pe", _c("transpose", pt[:, j * 128:(j + 1) * 128], hn4[:, kc * 128:(kc + 1) * 128], ident),
                      r=[hntok, "ident"], w=["ps%d" % bk])
                ptv = pt.rearrange("p (k t) -> p k t", k=8)
                gb = gcols[:, gi, g8 * 8:(g8 + 1) * 8].unsqueeze(2).to_broadcast([128, 8, 128])
                A("dve", _c("tensor_tensor", out=hT4[:, g8 * 8:(g8 + 1) * 8, st * 128:(st + 1) * 128], in0=ptv,
                            in1=gb, op=ALU.mult), r=["ps%d" % bk, "gcols"], w=["hT4"])

    pcnt = [0]

    def tm_accumulate(wb_ap, KCn, lhs_fn, lhs_toks, first_add_src=None):
        pw = (SLOT // 2) // KCn
        pw = min(pw, D)
        pw = (pw // 512) * 512 if pw >= 512 else pw
        for c0 in range(0, D, pw):
            v, rtok = ring_load(wb_ap[:, c0:c0 + pw].rearrange("(kc p) n -> p kc n", p=128), (KCn, pw))
            for st in range(NST):
                for cb in range(0, pw, 512):
                    w_ = min(512, pw - cb)
                    bk = 2 + pcnt[0] % 3
                    pcnt[0] += 1
                    for kc in range(KCn):
                        A("pe", _c("matmul", psum[bk][:, 0:w_], lhsT=lhs_fn(kc, st), rhs=v[:, kc, cb:cb + w_],
                                                                                      start=(kc == 0), stop=(kc == KCn - 1)),
                          r=list(lhs_toks) + [rtok], w=["ps%d" % bk])
                    xs = xacc[st][:, c0 + cb:c0 + cb + w_]
                    A("dve", _c("tensor_tensor", out=xs, in0=psum[bk][:, 0:w_], in1=xs, op=ALU.add),
                      r=["ps%d" % bk, "xacc%d" % st], w=["xacc%d" % st])

    for ti in range(N // T):
        t0 = ti * T
        load("sp", hT4, ymixT[:, :, t0:t0 + T].rearrange("c p t -> p c t"), ["hT4"], ["ymix"])
        for st in range(NST):
            load("sp", xacc[st], x_own[t0 + st * 128:t0 + (st + 1) * 128, :], ["xacc%d" % st])
        tm_accumulate(w_out_b, KC, lambda kc, st: hT4[:, kc, st * 128:(st + 1) * 128], ["hT4"])
        norm_T(1)
        for c0 in range(0, 512, PW):
            v, rtok = ring_load(w_cq_b[:, c0:c0 + PW].rearrange("(kc p) n -> p kc n", p=128), (KC, PW))
            for ch in range(PW // 128):
                hd = (c0 + ch * 128) // 128
                bk = 2 + pcnt[0] % 3
                pcnt[0] += 1
                for kc in range(KC):
                    A("pe", _c("matmul", psum[bk][:, 0:T], lhsT=v[:, kc, ch * 128:(ch + 1) * 128], rhs=hT4[:, kc, :],
                                                                       start=(kc == 0), stop=(kc == KC - 1)), r=["hT4", rtok], w=["ps%d" % bk])
                A("act", _c("activation", out=qc[:, hd, :], in_=psum[bk][:, 0:T], func=AF.Copy), r=["ps%d" % bk], w=["qc"])
        for hd in range(4):
            for st in range(NST):
                i2 = (hd * NST + st) % 2
                bS = 5
                A("pe", _c("matmul", psum[bS][:, 0:256], lhsT=qc[:, hd, st * 128:(st + 1) * 128], rhs=kmS[:, hd, :], start=True, stop=True),
                  r=["qc", "kmS"], w=["ps%d" % bS])
                m = cm[i2]
                mt = "cm%d" % i2
                A("dve", _c("reduce_max", out=m[:, 0:1], in_=psum[bS][:, 0:256], axis=AX.X), r=["ps%d" % bS], w=[mt])
                A("dve", _c("tensor_scalar", out=m[:, 1:2], in0=m[:, 0:1], scalar1=-scale, scalar2=None, op0=ALU.mult), r=[mt], w=[mt])
                A("act", _c("activation", out=cS[i2], in_=psum[bS][:, 0:256], func=AF.Exp, bias=m[:, 1:2], scale=scale, accum_out=m[:, 2:3]),
                  r=["ps%d" % bS, mt], w=["cS%d" % i2, mt])
                A("dve", _c("reciprocal", m[:, 3:4], m[:, 2:3]), r=[mt], w=[mt])
                A("pool", _c("tensor_scalar", out=cP[i2], in0=cS[i2], scalar1=m[:, 3:4], scalar2=None, op0=ALU.mult),
                  r=["cS%d" % i2, mt], w=["cP%d" % i2])
                bT = 6
                ptv = psb(bT)
                for t in range(2):
                    A("pe", _c("transpose", ptv[:, t * 128:(t + 1) * 128], cP[i2][:, t * 128:(t + 1) * 128], ident),
                      r=["cP%d" % i2, "ident"], w=["ps%d" % bT])
                A("act", _c("activation", out=cT[i2], in_=ptv[:, 0:256], func=AF.Copy), r=["ps%d" % bT], w=["cT%d" % i2])
                bO = 7
                for t in range(2):
                    A("pe", _c("matmul", psum[bO][:, st * 128:(st + 1) * 128], lhsT=vmS[:, t, hd * 128:(hd + 1) * 128],
                                                                       rhs=cT[i2][:, t * 128:(t + 1) * 128], start=(t == 0), stop=(t == 1)),
                      r=["vmS", "cT%d" % i2], w=["ps%d" % bO])
            A("act", _c("activation", out=oT[:, hd, :], in_=psum[7][:, 0:T], func=AF.Copy), r=["ps7"], w=["oT"])
        tm_accumulate(w_co_b, 4, lambda kc, st: oT[:, kc, st * 128:(st + 1) * 128], ["oT"])
        norm_T(3)
        NFB = DFF // F

        def mlp_s1(fb):
            ai = fb % 2
            for c0 in range(0, F, PW):
                v, rtok = ring_load(w1_b[:, fb * F + c0:fb * F + c0 + PW].rearrange("(kc p) n -> p kc n", p=128), (KC, PW))
                for ch in range(PW // 128):
                    dc = (c0 + ch * 128) // 128
                    bk = 2 + pcnt[0] % 3
                    pcnt[0] += 1
                    for kc in range(KC):
                        A("pe", _c("matmul", psum[bk][:, 0:T], lhsT=v[:, kc, ch * 128:(ch + 1) * 128], rhs=hT4[:, kc, :],
                                   start=(kc == 0), stop=(kc == KC - 1)), r=["hT4", rtok], w=["ps%d" % bk])
                    ri = dc % 2
                    A("act", _c("activation", out=rl[ri], in_=psum[bk][:, 0:T], func=AF.Relu), r=["ps%d" % bk], w=["rl%d" % ri])
                    A("pool", _c("tensor_tensor", out=aT[ai][:, dc, :], in0=rl[ri], in1=rl[ri], op=ALU.mult),
                      r=["rl%d" % ri], w=["aT%d" % ai])

        def mlp_s2(fb):
            ai = fb % 2
            tm_accumulate(w2_b[fb * F:(fb + 1) * F, :], FCH, lambda kc, st, ai=ai: aT[ai][:, kc, st * 128:(st + 1) * 128], ["aT%d" % ai])

        mlp_s1(0)
        for fb in range(NFB):
            if fb + 1 < NFB:
                mlp_s1(fb + 1)
            mlp_s2(fb)
        gi_ = rc[0] % nring
        rc[0] += 1
        gview = ring[gi_].bitcast(F32)[:, 0:D]
        load("sp", gview, gfin_d, ["ring%d" % gi_])
        for st in range(NST):
            dcol = 4 + st % 4
            stats4(st, dcol)
            A("dve", _c("scalar_tensor_tensor", out=xacc[st], in0=xacc[st], scalar=stat[:, dcol:dcol + 1], in1=gview,
                        op0=ALU.mult, op1=ALU.mult), r=["xacc%d" % st, "stat%d" % dcol, "ring%d" % gi_], w=["xacc%d" % st])
            A("pool", _c("dma_start", out=y_out[t0 + st * 128:t0 + (st + 1) * 128, :], in_=xacc[st]),
              r=["xacc%d" % st], w=["yout"], dma=True)

    S.emit()
    return nc, S


def core_inputs(cfg, inputs, seqs):
    c = cfg
    N, D, KC, HR, HA = c.N, c.D, c.KC, c.HR, c.HA
    st = static_tables()
    f32 = np.float32

    def colmajor(g):
        return np.ascontiguousarray(np.asarray(g, f32).reshape(KC, 128).T)

    gcols = np.ascontiguousarray(np.stack([colmajor(inputs["norm_mix"][0]), colmajor(inputs["norm_cross"][0]),
                                           colmajor(inputs["norm_mem"][0]), colmajor(inputs["norm_mlp"][0])], 1))
    gfin = np.ascontiguousarray(np.broadcast_to(np.asarray(inputs["norm_final"], f32)[None, :], (128, D)))
    dec = np.concatenate([np.asarray(inputs["ret_decay_f"][0], f32), np.asarray(inputs["ret_decay_b"][0], f32)])
    dec = np.ascontiguousarray(np.broadcast_to(dec[None], (128, 2 * HR)))
    sink = np.ascontiguousarray(np.broadcast_to(np.asarray(inputs["attn_sink"][0], f32)[None], (128, HA)))
    relb = np.ascontiguousarray(np.broadcast_to(np.asarray(inputs["rel_bias"], f32).reshape(1, -1), (128, 32 * HA)))
    half = 64
    inv = (np.float32(10000.0) ** (-np.arange(half, dtype=f32) / np.float32(half))).astype(f32)

    def cs_table(p0):
        pos = np.arange(p0, p0 + N, dtype=f32)
        ang = (pos[:, None] * inv[None, :]).astype(f32)
        co = np.cos(ang).astype(f32).T
        si = np.sin(ang).astype(f32).T
        return np.ascontiguousarray(np.stack([np.concatenate([co, co], 0), np.concatenate([-si, si], 0)], 0))

    shared = dict(w_in=inputs["w_in"][0], w_out=inputs["w_out"][0], w_cq=inputs["w_cq"][0], w_ckv=inputs["w_ckv"][0],
                  w_co=inputs["w_co"][0], w1=inputs["w_mlp_in"][0], w2=inputs["w_mlp_out"][0], gcols=gcols, gfin=gfin,
                  dec=dec, sink=sink, relb=relb, **st)
    maps = []
    zeros_oth = np.zeros((N, D), f32)
    for (xarr, b, start, marr) in seqs:
        L = xarr.shape[1]
        m = dict(shared)
        m["x_own"] = np.ascontiguousarray(xarr[b, start:start + N])
        halo = np.zeros((256, D), f32)
        fa = fb = 0.0
        lneg = rneg = NEG
        oth = zeros_oth
        p_oth = 0
        if start > 0:
            halo[0:128] = xarr[b, start - 128:start]
            lneg = 0.0
            fb = 1.0
            oth = np.ascontiguousarray(xarr[b, start - N:start])
            p_oth = start - N
        if start + N < L:
            halo[128:256] = xarr[b, start + N:start + N + 128]
            rneg = 0.0
            fa = 1.0
            oth = np.ascontiguousarray(xarr[b, start + N:start + 2 * N])
            p_oth = start + N
        m["x_oth"] = oth
        m["x_halo"] = halo
        m["mem"] = np.ascontiguousarray(marr[b])
        m["cs_own"] = cs_table(start)
        m["cs_oth"] = cs_table(p_oth)
        m["flags"] = np.ascontiguousarray(np.broadcast_to(np.array([fa, fb, lneg, rneg], f32)[None], (128, 4)))
        maps.append(m)
    return maps


_CACHE = {}


def run(cfg, inputs, debug_outs=(), trace=False, stop=99):
    xp = np.asarray(inputs["x_prompt"], np.float32)
    xs = np.asarray(inputs["x_sample"], np.float32)
    mp = np.asarray(inputs["mem_prompt"], np.float32)
    ms = np.asarray(inputs["mem_sample"], np.float32)
    N = cfg.N
    seqs = [(xp, b, 0, mp) for b in range(4)] + [(xs, b, hh * N, ms) for b in range(2) for hh in range(2)]
    maps = core_inputs(cfg, inputs, seqs)
    key = (cfg.D, cfg.N, cfg.DFF, cfg.T, cfg.F, tuple(debug_outs))
    nc, S = build_nc(cfg, debug_outs, stop)
    res = run_bass_kernel_spmd(nc, maps, core_ids=list(range(8)), **({"trace": True} if trace else {}))
    yp = np.stack([res.results[b]["y"] for b in range(4)], 0)
    ysm = np.stack([np.concatenate([res.results[4 + 2 * b]["y"], res.results[5 + 2 * b]["y"]], 0) for b in range(2)], 0)
    return (yp.astype(np.float32), ysm.astype(np.float32)), res, S


def kernel(**inputs):
    cfg = Cfg()
    (yp, ysm), _, _ = run(cfg, inputs)
    return (yp, ysm)
```

```python
import numpy as np
import concourse.bass as bass
import concourse.mybir as mybir
from concourse.bass_utils import run_bass_kernel_spmd

F32 = mybir.dt.float32
BF16 = mybir.dt.bfloat16
ALU = mybir.AluOpType
AF = mybir.ActivationFunctionType
AX = mybir.AxisListType
NEG = -1e30
import os as _os
POOLC = _os.environ.get("POOLC", "dve")
EPS = 1e-6


def _c(name, *args, **kwargs):
    return lambda e: getattr(e, name)(*args, **kwargs)


class Op:
    __slots__ = ("eng", "fn", "dma", "deps", "sig", "signaled", "idx", "pre")

    def __init__(self, eng, fn, dma):
        self.eng = eng
        self.fn = fn
        self.dma = dma
        self.deps = {}
        self.sig = None
        self.signaled = False
        self.pre = None


class Sched:
    COMPUTE = ("pe", "act", "dve", "pool")
    NSLOT = {"sp": 10, "act": 4, "pool": 40}

    def __init__(self, nc, same_eng_sync=True):
        self.nc = nc
        self.ops = []
        self.last_w = {}
        self.readers = {}
        self.same_eng_sync = same_eng_sync
        self.bar_deps = None
        self.bar_seen = {}
        self.last_on = {}
        self.last_dma = {}
        self.ndma = {"sp": 0, "act": 0, "pool": 0}

    def add(self, eng, fn, r=(), w=(), dma=False):
        op = Op(eng, fn, dma)
        op.idx = len(self.ops)
        deps = {}
        w = list(w) + [t for t in r if t.startswith("ps") and t not in w]
        for t in r:
            lw = self.last_w.get(t)
            if lw is not None:
                deps[lw] = "raw"
        for t in w:
            lw = self.last_w.get(t)
            if lw is not None:
                deps[lw] = "waw"
            for rd in self.readers.get(t, ()):
                if rd not in deps:
                    deps[rd] = "war"
        key = (eng, dma)
        if self.bar_deps is not None and not self.bar_seen.get(key):
            self.bar_seen[key] = True
            for d in self.bar_deps:
                if d not in deps:
                    deps[d] = "bar"
        for d, kind in deps.items():
            if d is op:
                continue
            if not d.dma and not dma and d.eng == eng:
                if eng == "pe":
                    continue
                if not self.same_eng_sync:
                    continue
            gk = ("dma", d.eng, d.sig[2]) if d.dma else ("c", d.eng)
            best = op.deps.get(gk)
            if best is None or d.idx > best.idx:
                op.deps[gk] = d
        for d in op.deps.values():
            d.signaled = True
        for t in r:
            self.readers.setdefault(t, []).append(op)
        for t in w:
            self.last_w[t] = op
            self.readers[t] = []
        if dma:
            k = self.ndma[eng]
            self.ndma[eng] = k + 1
            ns = self.NSLOT[eng]
            slot = k % ns
            op.sig = ("dma", eng, slot, 16 * (k // ns + 1))
            op.signaled = True
            op.pre = self.last_dma.get((eng, slot))
            self.last_dma[(eng, slot)] = op
        else:
            self.last_on[eng] = op
        self.ops.append(op)
        return op

    def barrier(self):
        deps = list(self.last_on.values()) + list(self.last_dma.values())
        for d in deps:
            d.signaled = True
        self.bar_deps = deps
        self.bar_seen = {}
        self.last_w = {}
        self.readers = {}

    def emit(self, final_wait_eng="sp"):
        nc = self.nc
        for o in self.last_on.values():
            o.signaled = True
        cnt = {e: 0 for e in self.COMPUTE}
        for op in self.ops:
            if not op.dma and op.signaled:
                cnt[op.eng] += 1
                op.sig = ("c", op.eng, 0, cnt[op.eng])
        sems = {}
        for op in self.ops:
            if op.sig is not None and op.sig[:3] not in sems:
                sems[op.sig[:3]] = nc.alloc_semaphore(name="s_%s_%s_%d" % op.sig[:3])
        streams = {e: [] for e in ("pe", "act", "dve", "pool", "sp")}
        for op in self.ops:
            streams[op.eng].append(op)
        finals = list(self.last_dma.values()) + list(self.last_on.values())
        nwaits = [0]

        def run(ename, e):
            waited = {}

            def wait(sig):
                k = sig[:3]
                if waited.get(k, 0) >= sig[3]:
                    return
                waited[k] = sig[3]
                e.wait_ge(sems[k], sig[3])
                nwaits[0] += 1

            for op in streams[ename]:
                for d in op.deps.values():
                    wait(d.sig)
                if op.dma and op.pre is not None:
                    wait(op.pre.sig)
                ins = op.fn(e)
                if op.signaled:
                    ins.then_inc(sems[op.sig[:3]], 16 if op.dma else 1)
            if ename == final_wait_eng:
                for d in finals:
                    wait(d.sig)

        with nc.Block() as block:
            block.tensor(lambda e: run("pe", e))
            block.scalar(lambda e: run("act", e))
            block.vector(lambda e: run("dve", e))
            block.gpsimd(lambda e: run("pool", e))
            block.sync(lambda e: run("sp", e))
        self.stats = dict(nops=len(self.ops), nwaits=nwaits[0], nsems=len(sems),
                          per_eng={k: len(v) for k, v in streams.items()})


class Cfg:
    def __init__(self, D=4096, N=4096, DFF=16384, T=512, F=512):
        self.D, self.N, self.DFF, self.T, self.F = D, N, DFF, T, F
        self.KC = D // 128
        self.HR = (D // 2) // 128
        self.HA = self.HR
        self.KV = self.HA // 4
        self.RW = self.HR * 128
        self.AQ = self.HA * 128
        self.AKV = self.KV * 128
        self.INW = 4 * self.RW + self.AQ + 2 * self.AKV
        self.MEM = 256
        self.MH = 4
        self.MW = 512
        self.NCH = N // 128


def t5_buckets(rel):
    nb = 16
    max_exact = 8
    base = np.where(rel > 0, nb, 0)
    n = np.abs(rel)
    large = max_exact + (np.log(np.maximum(n, 1) / max_exact) / np.log(128 / max_exact) * (nb - max_exact)).astype(np.int32)
    large = np.minimum(large, nb - 1)
    return (base + np.where(n < max_exact, n, large)).astype(np.int32)


def static_tables():
    st = {}
    st["ident"] = np.eye(128, dtype=np.float32)
    rs = np.zeros((128, 128), np.float32)
    for dp in range(128):
        rs[(dp + 64) % 128, dp] = 1.0
    st["rswap"] = rs
    j = np.arange(128)[:, None].astype(np.float32)
    i = np.arange(128)[None, :].astype(np.float32)
    af = np.where(i >= j, i - j, 1e30).astype(np.float32)
    ab = np.where(j > i, j - i, 1e30).astype(np.float32)
    st["adec"] = np.stack([af, ab], 1)
    jj = np.arange(128).astype(np.float32)
    st["zexp"] = np.stack([127.0 - jj, jj], 1).astype(np.float32)
    ii = np.arange(128).astype(np.float32)
    xi = np.stack([ii + 1.0, 128.0 - ii], 0)
    st["xiexp"] = np.ascontiguousarray(np.broadcast_to(xi[None], (128, 2, 128))).astype(np.float32)
    qi = np.arange(128)[:, None]
    kj = np.arange(384)[None, :] - 128
    rel = kj - qi
    bk = t5_buckets(rel)
    inw = np.abs(rel) <= 128
    eb = np.zeros((33, 128, 384), np.float32)
    for b in range(32):
        eb[b] = ((bk == b) & inw)
    eb[32] = ~inw
    st["ebuck"] = eb
    return st


def build_nc(cfg, debug_outs=(), stop=99):
    c = cfg
    D, N, KC, T, HR, HA, KV, DFF, F = c.D, c.N, c.KC, c.T, c.HR, c.HA, c.KV, c.DFF, c.F
    NCH = c.NCH
    nc = bass.Bass("TRN2", target_bir_lowering=False)

    def din(name, shape, dt=F32):
        return nc.dram_tensor(name, list(shape), dt, kind="ExternalInput").ap()

    def dscr(name, shape, dt=BF16):
        kind = "ExternalOutput" if name in debug_outs else "Internal"
        return nc.dram_tensor(name, list(shape), dt, kind=kind).ap()

    x_own = din("x_own", [N, D])
    x_oth = din("x_oth", [N, D])
    x_halo = din("x_halo", [256, D])
    mem = din("mem", [256, D])
    w_in = din("w_in", [D, c.INW])
    w_out = din("w_out", [D, D])
    w_cq = din("w_cq", [D, 512])
    w_ckv = din("w_ckv", [D, 1024])
    w_co = din("w_co", [512, D])
    w1 = din("w1", [D, DFF])
    w2 = din("w2", [DFF, D])
    gcols_d = din("gcols", [128, 4, KC])
    gfin_d = din("gfin", [128, D])
    cs_own = din("cs_own", [2, 128, N])
    cs_oth = din("cs_oth", [2, 128, N])
    dec_d = din("dec", [128, 2 * HR])
    sink_d = din("sink", [128, HA])
    relb_d = din("relb", [128, 32 * HA])
    flags_d = din("flags", [128, 4])
    ident_d = din("ident", [128, 128])
    rswap_d = din("rswap", [128, 128])
    adec_d = din("adec", [128, 2, 128])
    zexp_d = din("zexp", [128, 2])
    xiexp_d = din("xiexp", [128, 2, 128])
    ebuck_d = din("ebuck", [33, 128, 384])
    y_out = nc.dram_tensor("y", [N, D], F32, kind="ExternalOutput").ap()

    w_in_b = dscr("w_in_b", [D, c.INW])
    w_out_b = dscr("w_out_b", [D, D])
    w_cq_b = dscr("w_cq_b", [D, 512])
    w_ckv_b = dscr("w_ckv_b", [D, 1024])
    w_co_b = dscr("w_co_b", [512, D])
    w1_b = dscr("w1_b", [D, DFF])
    w2_b = dscr("w2_b", [DFF, D])
    qrT = dscr("qrT", [HR, 128, N])
    krT = dscr("krT", [HR, 128, 2 * N])
    vr = dscr("vr", [2 * N, c.RW])
    gT = dscr("gT", [HR, 128, N])
    qaT = dscr("qaT", [HA, 128, N])
    kaT = dscr("kaT", [KV, 128, N + 256])
    va = dscr("va", [N + 256, c.AKV])
    kmT = dscr("kmT", [4, 128, 256])
    vm = dscr("vm", [256, 512])
    ymixT = dscr("ymixT", [2 * HR, 128, N])

    S = Sched(nc, same_eng_sync=bool(int(_os.environ.get("SES", "1"))))
    A = S.add

    base0 = nc.sbuf_base
    top = nc.sbuf_top
    cur = [(base0 + 63) // 64 * 64]

    def sb(name, shape, dt):
        nbytes = int(np.prod(shape[1:])) * (4 if dt == F32 else 2)
        off = cur[0]
        cur[0] = (off + nbytes + 63) // 64 * 64
        assert cur[0] <= top, ("SBUF overflow", name, cur[0], top)
        return nc.alloc_sbuf_tensor_at(name, list(shape), dt, offset=off).ap()

    ident = sb("ident", [128, 128], BF16)
    rswap = sb("rswap", [128, 128], BF16)
    ones = sb("ones", [128, 128], BF16)
    gcols = sb("gcols", [128, 4, KC], F32)
    flags = sb("flags", [128, 4], F32)
    lg = sb("lg", [128, 2 * HR], F32)
    cdec = sb("cdec", [128, 2 * HR], F32)
    zfb = sb("zfb", [128, 2, HR], F32)
    sink = sb("sink", [128, HA], F32)
    stat = sb("stat", [128, 8], F32)
    tmpf = sb("tmpf", [128, 128], F32)
    persist_end = cur[0]

    psall = nc.alloc_psum_tensor("psall", [128, 4096], F32).ap()
    psum = [psall[:, i * 512:(i + 1) * 512] for i in range(8)]

    def psb(i):
        return psum[i].bitcast(BF16)

    def cast_w(src, dst, name, nsplit):
        rows = src.shape[0]
        rp = rows // nsplit
        for i in range(nsplit):
            A("pool", _c("dma_start", out=dst[i * rp:(i + 1) * rp, :], in_=src[i * rp:(i + 1) * rp, :]),
              w=[name], dma=True)

    cast_w(w_ckv, w_ckv_b, "w_ckv_b", 1)
    cast_w(w_in, w_in_b, "w_in_b", 4)

    def load(eng, dst, src, wtok, rtok=()):
        return A(eng, _c("dma_start", out=dst, in_=src), r=list(rtok), w=list(wtok), dma=True)

    p0 = cur[0]
    identf = sb("identf", [128, 128], F32)
    rswapf = sb("rswapf", [128, 128], F32)
    adec = sb("adec", [128, 2, 128], F32)
    zexp = sb("zexp", [128, 2], F32)
    xiexp = sb("xiexp", [128, 2, 128], F32)
    decs = sb("decs", [128, 2 * HR], F32)
    load("sp", identf, ident_d, ["identf"])
    load("sp", rswapf, rswap_d, ["rswapf"])
    load("sp", gcols, gcols_d, ["gcols"])
    load("sp", flags, flags_d, ["flags"])
    load("sp", decs, dec_d, ["decs"])
    load("sp", sink, sink_d, ["sink"])
    load("sp", adec, adec_d, ["adec"])
    load("sp", zexp, zexp_d, ["zexp"])
    load("sp", xiexp, xiexp_d, ["xiexp"])
    A("dve", _c("tensor_copy", ident, identf), r=["identf"], w=["ident"])
    A("dve", _c("tensor_copy", rswap, rswapf), r=["rswapf"], w=["rswap"])
    A("dve", _c("memset", ones, 1.0), w=["ones"])
    A("act", _c("activation", out=lg, in_=decs, func=AF.Exp), r=["decs"], w=["lg"])
    A("dve", _c("tensor_scalar", out=lg, in0=lg, scalar1=-1.0, scalar2=None, op0=ALU.mult), r=["lg"], w=["lg"])
    A("act", _c("activation", out=cdec, in_=lg, func=AF.Exp, scale=128.0), r=["lg"], w=["cdec"])
    for d_ in range(2):
        A("dve", _c("tensor_scalar", out=zfb[:, d_, :], in0=lg[:, d_ * HR:(d_ + 1) * HR], scalar1=zexp[:, d_:d_ + 1],
                                                 scalar2=None, op0=ALU.mult), r=["lg", "zexp"], w=["zfb"])
    A("act", _c("activation", out=zfb, in_=zfb, func=AF.Exp), r=["zfb"], w=["zfb"])
    S.barrier()
    tables_end = cur[0]
    if stop <= 0:
        S.emit()
        return nc, S

    cur[0] = tables_end
    WP = 512
    hT = sb("hT", [128, KC, T], BF16)
    xbuf = [sb("xbuf%d" % i, [128, D], F32) for i in range(2)]
    hnb = [sb("hn%d" % i, [128, D], BF16) for i in range(2)]
    hcnt = [0]
    junk = sb("junk", [128, D], BF16)
    NWB = 2
    wbuf = [sb("wbuf%d" % i, [128, KC, WP], BF16) for i in range(NWB)]
    cst = sb("cst", [128, 2, T], F32)
    stage = [sb("stage%d" % i, [128, 4, T], BF16) for i in range(2)]
    qsb = [sb("qsb%d" % i, [128, T], BF16) for i in range(2)]
    t1 = [sb("t1_%d" % i, [128, T], F32) for i in range(2)]
    t2 = [sb("t2_%d" % i, [128, T], F32) for i in range(2)]
    cnt = {"x": 0, "w": 0, "stage": 0, "acc": 0, "q": 0, "tp": 0}

    def rms_stats(xt, xtok, dcol):
        A("act", _c("activation", out=junk, in_=xt, func=AF.Square, accum_out=stat[:, dcol:dcol + 1]),
          r=[xtok], w=["junk", "stat%d" % dcol])
        A("dve", _c("tensor_scalar", out=stat[:, dcol:dcol + 1], in0=stat[:, dcol:dcol + 1], scalar1=1.0 / D, scalar2=EPS,
                                           op0=ALU.mult, op1=ALU.add), r=["stat%d" % dcol], w=["stat%d" % dcol])
        A("act", _c("activation", out=stat[:, dcol:dcol + 1], in_=stat[:, dcol:dcol + 1], func=AF.Sqrt),
          r=["stat%d" % dcol], w=["stat%d" % dcol])
        A("dve", _c("reciprocal", stat[:, dcol:dcol + 1], stat[:, dcol:dcol + 1]), r=["stat%d" % dcol], w=["stat%d" % dcol])

    def norm_transpose(xt, xtok, gi, hT_, st, htok, dcol=0):
        rms_stats(xt, xtok, dcol)
        hi = hcnt[0] % 2
        hcnt[0] += 1
        hn = hnb[hi]
        hntok = "hn%d" % hi
        A("dve", _c("tensor_scalar", out=hn, in0=xt, scalar1=stat[:, dcol:dcol + 1], scalar2=None, op0=ALU.mult),
          r=[xtok, "stat%d" % dcol], w=[hntok])
        for g8 in range(KC // 8):
            bk = cnt["tp"] % 2
            cnt["tp"] += 1
            pt = psb(bk)
            for j in range(8):
                kc = g8 * 8 + j
                A("pe", _c("transpose", pt[:, j * 128:(j + 1) * 128], hn[:, kc * 128:(kc + 1) * 128], ident),
                  r=[hntok, "ident"], w=["ps%d" % bk])
            ptv = pt.rearrange("p (k t) -> p k t", k=8)
            gb = gcols[:, gi, g8 * 8:(g8 + 1) * 8].unsqueeze(2).to_broadcast([128, 8, 128])
            eng = "dve" if g8 % 2 == 0 else "pool"
            if eng == "pool":
                eng = "dve"
            A(eng, _c("tensor_tensor", out=hT_[:, g8 * 8:(g8 + 1) * 8, st * 128:(st + 1) * 128], in0=ptv, in1=gb,
                                                                  op=ALU.mult), r=["ps%d" % bk, "gcols"], w=[htok])

    def load_w(wb_ap, c0, W):
        i = cnt["w"] % NWB
        cnt["w"] += 1
        load("sp", wbuf[i][:, :, 0:W], wb_ap[:, c0:c0 + W].rearrange("(kc p) n -> p kc n", p=128), ["wbuf%d" % i], [wb_ap.tensor.name])
        return i

    ptc = [0]
    import os
    PTMAX = int(os.environ.get("PTMAX", "9999"))

    def proj_tile(xsrc, ntok, gi, pieces, cs_src=None):
        ptc[0] += 1
        if ptc[0] > PTMAX:
            return
        nsub = ntok // 128
        for st in range(nsub):
            xi = cnt["x"] % 2
            cnt["x"] += 1
            load("sp", xbuf[xi], xsrc[st * 128:(st + 1) * 128, :], ["xbuf%d" % xi])
            norm_transpose(xbuf[xi], "xbuf%d" % xi, gi, hT, st, "hT", dcol=st % 2)
        if cs_src is not None:
            load("sp", cst[:, :, 0:ntok], cs_src.rearrange("c p t -> p c t"), ["cst"])
        for (w_ap, c0, W, kind, destfn) in pieces:
            wi = load_w(w_ap, c0, W)
            wtok = "wbuf%d" % wi
            nch = W // 128
            si = cnt["stage"] % 2
            cnt["stage"] += 1
            stg = stage[si]
            stok = "stage%d" % si
            if kind == "tm":
                for st in range(nsub):
                    bk = 2 + cnt["acc"] % 3
                    cnt["acc"] += 1
                    for kc in range(KC):
                        A("pe", _c("matmul", psum[bk][:, 0:W], lhsT=hT[:, kc, st * 128:(st + 1) * 128],
                                                                       rhs=wbuf[wi][:, kc, 0:W], start=(kc == 0), stop=(kc == KC - 1)),
                          r=["hT", wtok], w=["ps%d" % bk])
                    sv = stg.rearrange("p a t -> p (a t)")[:, st * W:(st + 1) * W]
                    A("act", _c("activation", out=sv, in_=psum[bk][:, 0:W], func=AF.Copy), r=["ps%d" % bk], w=[stok])
                    A("pool", _c("dma_start", out=destfn(st), in_=sv), r=[stok], w=["scr"], dma=True)
                continue
            for ch in range(nch):
                bk = 2 + cnt["acc"] % 3
                cnt["acc"] += 1
                acc = psum[bk][:, 0:ntok]
                for kc in range(KC):
                    A("pe", _c("matmul", acc, lhsT=wbuf[wi][:, kc, ch * 128:(ch + 1) * 128], rhs=hT[:, kc, 0:ntok],
                                                                     start=(kc == 0), stop=(kc == KC - 1)), r=["hT", wtok], w=["ps%d" % bk])
                so = stg[:, ch, 0:ntok]
                if kind == "copy" or (kind.startswith("rope") and _os.environ.get("NOROPE")):
                    A("act", _c("activation", out=so, in_=acc, func=AF.Copy), r=["ps%d" % bk], w=[stok])
                elif kind == "silu":
                    A("act", _c("activation", out=so, in_=acc, func=AF.Silu), r=["ps%d" % bk], w=[stok])
                else:
                    qi = cnt["q"] % 2
                    cnt["q"] += 1
                    rb = 5 + qi
                    sc = 1.0 if kind == "rope_q" else 128.0 ** -0.5
                    A("act", _c("activation", out=qsb[qi][:, 0:ntok], in_=acc, func=AF.Copy), r=["ps%d" % bk], w=["qsb%d" % qi])
                    A("pe", _c("matmul", psum[rb][:, 0:ntok], lhsT=rswap, rhs=qsb[qi][:, 0:ntok], start=True, stop=True),
                      r=["qsb%d" % qi, "rswap"], w=["ps%d" % rb])
                    A("dve", _c("scalar_tensor_tensor", out=t1[qi][:, 0:ntok], in0=acc, scalar=sc, in1=cst[:, 0, 0:ntok],
                                                                                     op0=ALU.mult, op1=ALU.mult), r=["ps%d" % bk, "cst"], w=["t1_%d" % qi])
                    A("dve", _c("scalar_tensor_tensor", out=t2[qi][:, 0:ntok], in0=psum[rb][:, 0:ntok], scalar=sc,
                                                                                   in1=cst[:, 1, 0:ntok], op0=ALU.mult, op1=ALU.mult),
                      r=["ps%d" % rb, "cst"], w=["t2_%d" % qi])
                    A(POOLC, _c("tensor_tensor", out=so, in0=t1[qi][:, 0:ntok], in1=t2[qi][:, 0:ntok], op=ALU.add),
                      r=["t1_%d" % qi, "t2_%d" % qi], w=[stok])
            for (dst, sl) in destfn(nch):
                A("pool", _c("dma_start", out=dst, in_=stg[:, 0:nch, sl]), r=[stok], w=["scr"], dma=True)

    def fm_dest(arr, h0, t0, ntok):
        def f(nch):
            return [(arr[h0:h0 + nch, :, t0:t0 + ntok].rearrange("h p t -> p h t"), slice(0, ntok))]
        return f

    def pieces_for(colbase, width, kind, mk):
        out = []
        c0 = 0
        while c0 < width:
            W = min(WP, width - c0)
            out.append((w_in_b, colbase + c0, W, kind, mk(c0, W)))
            c0 += W
        return out

    def mem_pieces():
        ps_ = []
        ps_.append((w_ckv_b, 0, 512, "copy", lambda nch: [(kmT[0:4, :, 0:256].rearrange("h p t -> p h t"), slice(0, 256))]))
        ps_.append((w_ckv_b, 512, 512, "tm", lambda st: vm[st * 128:(st + 1) * 128, :]))
        return ps_

    proj_tile(mem, 256, 2, mem_pieces())

    def halo_pieces():
        ps_ = []
        kbase = 4 * c.RW + c.AQ
        vbase = kbase + c.AKV

        def mkk(c0, W):
            h0 = c0 // 128
            return lambda nch: [(kaT[h0:h0 + nch, :, 0:128].rearrange("h p t -> p h t"), slice(0, 128)),
                                (kaT[h0:h0 + nch, :, N + 128:N + 256].rearrange("h p t -> p h t"), slice(128, 256))]

        def mkv(c0, W):
            return lambda st: va[(0 if st == 0 else N + 128):(128 if st == 0 else N + 256), c0:c0 + W]
        ps_ += pieces_for(kbase, c.AKV, "copy", mkk)
        ps_ += pieces_for(vbase, c.AKV, "tm", mkv)
        return ps_

    proj_tile(x_halo, 256, 0, halo_pieces())

    for ti in range(N // T):
        t0 = ti * T

        def mkk(c0, W, t0=t0):
            return fm_dest(krT, c0 // 128, N + t0, T)

        def mkv(c0, W, t0=t0):
            return lambda st: vr[N + t0 + st * 128:N + t0 + (st + 1) * 128, c0:c0 + W]
        pcs = pieces_for(c.RW, c.RW, "rope_k", mkk) + pieces_for(2 * c.RW, c.RW, "tm", mkv)
        proj_tile(x_oth[t0:t0 + T, :], T, 0, pcs, cs_oth[:, :, t0:t0 + T])

    for ti in range(N // T):
        t0 = ti * T
        pcs = []
        pcs += pieces_for(0, c.RW, "rope_q", lambda c0, W, t0=t0: fm_dest(qrT, c0 // 128, t0, T))
        pcs += pieces_for(c.RW, c.RW, "rope_k", lambda c0, W, t0=t0: fm_dest(krT, c0 // 128, t0, T))
        pcs += pieces_for(2 * c.RW, c.RW, "tm", lambda c0, W, t0=t0: (lambda st: vr[t0 + st * 128:t0 + (st + 1) * 128, c0:c0 + W]))
        pcs += pieces_for(3 * c.RW, c.RW, "silu", lambda c0, W, t0=t0: fm_dest(gT, c0 // 128, t0, T))
        pcs += pieces_for(4 * c.RW, c.AQ, "copy", lambda c0, W, t0=t0: fm_dest(qaT, c0 // 128, t0, T))
        pcs += pieces_for(4 * c.RW + c.AQ, c.AKV, "copy", lambda c0, W, t0=t0: fm_dest(kaT, c0 // 128, 128 + t0, T))
        pcs += pieces_for(4 * c.RW + c.AQ + c.AKV, c.AKV, "tm",
                          lambda c0, W, t0=t0: (lambda st: va[128 + t0 + st * 128:128 + t0 + (st + 1) * 128, c0:c0 + W]))
        proj_tile(x_own[t0:t0 + T, :], T, 0, pcs, cs_own[:, :, t0:t0 + T])

    S.barrier()
    if stop <= 2:
        S.emit()
        return nc, S
    cast_w(w_out, w_out_b, "w_out_b", 2)
    cast_w(w_cq, w_cq_b, "w_cq_b", 1)
    cast_w(w_co, w_co_b, "w_co_b", 1)
    cast_w(w1, w1_b, "w1_b", 8)
    cast_w(w2, w2_b, "w2_b", 8)

    cur[0] = tables_end
    NT2 = 2 * N
    qT = [sb("qT%d" % i, [128, N], BF16) for i in range(2)]
    kT = [sb("kT%d" % i, [128, NT2], BF16) for i in range(2)]
    vtm = [sb("vtm%d" % i, [128, 2 * NCH, 128], BF16) for i in range(2)]
    gTs = [sb("gTs%d" % i, [128, N], BF16) for i in range(2)]
    DT = sb("DT", [128, 128], F32)
    dtmp = sb("dtmp", [128, 2, 128], F32)
    XI = sb("XI", [128, 2, 128], F32)
    cpw = sb("cpw", [128, 2, NCH], F32)
    kz = [sb("kz%d" % i, [128, 8, 128], BF16) for i in range(2)]
    SFall = sb("SFall", [128, NCH + 1, 128], F32)
    SBall = sb("SBall", [128, NCH + 1, 128], F32)
    zc = sb("zc", [128, 2, NCH], F32)
    kz4 = [sb("kz4_%d" % i, [128, 8, 128], BF16) for i in range(4)]
    SFst = sb("SFst", [128, NCH, 128], BF16)
    SBst = sb("SBst", [128, NCH, 128], BF16)
    qfb = [sb("qfb%d" % i, [128, 2, 512], BF16) for i in range(2)]
    attm = [sb("attm%d" % i, [128, 512], BF16) for i in range(2)]
    ysq = [sb("ysq%d" % i, [128, 512], BF16) for i in range(2)]
    rstd = [sb("rstd%d" % i, [128, 512], F32) for i in range(2)]
    ynf = sb("ynf", [128, 512], F32)
    yo = [sb("yo%d" % i, [128, 512], BF16) for i in range(2)]
    cexp = sb("cexp", [128, 2, NCH], F32)
    for cc in range(NCH):
        A("dve", _c("memset", cexp[:, 0, cc:cc + 1], 128.0 * (NCH - 1 - cc)), w=["cexp"])
        A("dve", _c("memset", cexp[:, 1, cc:cc + 1], 128.0 * cc), w=["cexp"])

    def r_loads(h):
        b = h % 2
        load("sp", qT[b], qrT[h], ["qT%d" % b], ["scr"])
        load("sp", kT[b], krT[h], ["kT%d" % b], ["scr"])
        load("sp", vtm[b], vr[:, h * 128:(h + 1) * 128].rearrange("(c p) e -> p c e", p=128), ["vtm%d" % b], ["scr"])
        load("sp", gTs[b], gT[h], ["gTs%d" % b], ["scr"])

    r_loads(0)
    kvc = [0]
    for h in range(HR):
        hb = h % 2
        qTh, kTh, vth, gTh = qT[hb], kT[hb], vtm[hb], gTs[hb]
        qtok, ktok, vtok, gtok = "qT%d" % hb, "kT%d" % hb, "vtm%d" % hb, "gTs%d" % hb
        if h + 1 < HR:
            r_loads(h + 1)
        for d_ in range(2):
            lgc = lg[:, d_ * HR + h:d_ * HR + h + 1]
            A("act", _c("activation", out=dtmp[:, d_, :], in_=adec[:, d_, :], func=AF.Exp, scale=lgc), r=["adec", "lg"], w=["dtmp"])
            A("act", _c("activation", out=XI[:, d_, :], in_=xiexp[:, d_, :], func=AF.Exp, scale=lgc), r=["xiexp", "lg"], w=["XI"])
            A("act", _c("activation", out=cpw[:, d_, :], in_=cexp[:, d_, :], func=AF.Exp, scale=lgc), r=["cexp", "lg"], w=["cpw"])
        A("dve", _c("tensor_tensor", out=DT, in0=dtmp[:, 0, :], in1=dtmp[:, 1, :], op=ALU.add), r=["dtmp"], w=["DT"])
        for d_ in range(2):
            A("dve", _c("tensor_scalar", out=zc[:, d_, :], in0=cpw[:, d_, :], scalar1=zfb[:, d_, h:h + 1], scalar2=None, op0=ALU.mult),
              r=["cpw", "zfb"], w=["zc"])
        NG = NCH // 8
        oth = [(d_, g) for d_ in range(2) for g in range(NG)]

        def o_T(j):
            d_, g = oth[j]
            i = j % 2
            pt = psb(0)
            for jj in range(8):
                cc = NCH + g * 8 + jj
                A("pe", _c("transpose", pt[:, jj * 128:(jj + 1) * 128], kTh[:, cc * 128:(cc + 1) * 128], ident), r=[ktok, "ident"], w=["ps0"])
            A("dve", _c("tensor_tensor", out=kz[i], in0=pt.rearrange("p (a b) -> p a b", a=8),
                        in1=zc[:, d_, g * 8:(g + 1) * 8].unsqueeze(2).to_broadcast([128, 8, 128]), op=ALU.mult), r=["ps0", "zc"], w=["kz%d" % i])

        def o_KV(j):
            d_, g = oth[j]
            i = j % 2
            acc = psum[7][:, d_ * 128:(d_ + 1) * 128]
            for jj in range(8):
                A("pe", _c("matmul", acc, lhsT=kz[i][:, jj, :], rhs=vth[:, NCH + g * 8 + jj, :], start=(g == 0 and jj == 0),
                           stop=(g == NG - 1 and jj == 7)), r=["kz%d" % i, vtok], w=["ps7"])
            if g == NG - 1:
                if d_ == 0:
                    A("dve", _c("tensor_scalar", out=SFall[:, 0, :], in0=acc, scalar1=flags[:, 1:2], scalar2=None, op0=ALU.mult),
                      r=["ps7", "flags"], w=["SFall"])
                else:
                    A("dve", _c("tensor_scalar", out=SBall[:, NCH, :], in0=acc, scalar1=flags[:, 0:1], scalar2=None, op0=ALU.mult),
                      r=["ps7", "flags"], w=["SBall"])

        for j in range(len(oth) + 1):
            if j < len(oth):
                o_T(j)
            if j >= 1:
                o_KV(j - 1)

        def w_T(j):
            for (d_, g, bank, ki) in ((0, j, 0, 2 * (j % 2)), (1, NG - 1 - j, 7, 2 * (j % 2) + 1)):
                pt = psb(bank)
                for jj in range(8):
                    cc = g * 8 + jj
                    A("pe", _c("transpose", pt[:, jj * 128:(jj + 1) * 128], kTh[:, cc * 128:(cc + 1) * 128], ident), r=[ktok, "ident"], w=["ps%d" % bank])
                A("act", _c("activation", out=kz4[ki].rearrange("p a b -> p (a b)"), in_=pt, func=AF.Copy, scale=zfb[:, d_, h:h + 1]),
                  r=["ps%d" % bank, "zfb"], w=["kzo%d" % ki])

        def w_KV(j):
            gF, gB = j, NG - 1 - j
            kiF = 2 * (j % 2)
            kiB = kiF + 1
            for hf in range(2):
                halfF, halfB = hf, 1 - hf
                for jq in range(4):
                    A("pe", _c("matmul", psum[1][:, jq * 128:(jq + 1) * 128], lhsT=kz4[kiF][:, halfF * 4 + jq, :], rhs=vth[:, gF * 8 + halfF * 4 + jq, :],
                               start=True, stop=True), r=["kzo%d" % kiF, vtok], w=["ps1"])
                for jq in range(4):
                    A("pe", _c("matmul", psum[2][:, jq * 128:(jq + 1) * 128], lhsT=kz4[kiB][:, halfB * 4 + jq, :], rhs=vth[:, gB * 8 + halfB * 4 + jq, :],
                               start=True, stop=True), r=["kzo%d" % kiB, vtok], w=["ps2"])
                for q in range(4):
                    jqF, jqB = q, 3 - q
                    ccF = gF * 8 + halfF * 4 + jqF
                    ccB = gB * 8 + halfB * 4 + jqB
                    A("dve", _c("scalar_tensor_tensor", out=SFall[:, ccF + 1, :], in0=SFall[:, ccF, :], scalar=cdec[:, h:h + 1],
                                in1=psum[1][:, jqF * 128:(jqF + 1) * 128], op0=ALU.mult, op1=ALU.add), r=["ps1", "cdec", "SFall"], w=["SFall"])
                    A("dve", _c("scalar_tensor_tensor", out=SBall[:, ccB, :], in0=SBall[:, ccB + 1, :], scalar=cdec[:, HR + h:HR + h + 1],
                                in1=psum[2][:, jqB * 128:(jqB + 1) * 128], op0=ALU.mult, op1=ALU.add), r=["ps2", "cdec", "SBall"], w=["SBall"])

        for j in range(NG + 1):
            if j < NG:
                w_T(j)
            if j >= 1:
                w_KV(j - 1)
        A("act", _c("activation", out=SFst.rearrange("p a b -> p (a b)"), in_=SFall[:, 0:NCH, :].rearrange("p a b -> p (a b)"), func=AF.Copy),
          r=["SFall"], w=["SFst"])
        A("act", _c("activation", out=SBst.rearrange("p a b -> p (a b)"), in_=SBall[:, 1:NCH + 1, :].rearrange("p a b -> p (a b)"), func=AF.Copy),
          r=["SBall"], w=["SBst"])

        G4 = NCH // 4

        def o_s1(i):
            i2 = i % 2
            tsl = slice(i * 512, (i + 1) * 512)
            for d_ in range(2):
                A("dve", _c("tensor_tensor", out=qfb[i2][:, d_, :].rearrange("p (a b) -> p a b", a=4), in0=qTh[:, tsl].rearrange("p (a b) -> p a b", a=4),
                            in1=XI[:, d_, :].unsqueeze(1).to_broadcast([128, 4, 128]), op=ALU.mult), r=[qtok, "XI"], w=["qfb%d" % i2])
            bS = 3 + i2
            for jq in range(4):
                cc = i * 4 + jq
                A("pe", _c("matmul", psum[bS][:, jq * 128:(jq + 1) * 128], lhsT=kTh[:, cc * 128:(cc + 1) * 128], rhs=qTh[:, cc * 128:(cc + 1) * 128],
                           start=True, stop=True), r=[ktok, qtok], w=["ps%d" % bS])
            A("dve", _c("tensor_tensor", out=attm[i2].rearrange("p (a b) -> p a b", a=4), in0=psum[bS].rearrange("p (a b) -> p a b", a=4),
                        in1=DT.unsqueeze(1).to_broadcast([128, 4, 128]), op=ALU.mult), r=["ps%d" % bS, "DT"], w=["attm%d" % i2])

        def o_s2(i):
            i2 = i % 2
            bY = 5 + i2
            for jq in range(4):
                cc = i * 4 + jq
                ysl = psum[bY][:, jq * 128:(jq + 1) * 128]
                A("pe", _c("matmul", ysl, lhsT=vth[:, cc, :], rhs=attm[i2][:, jq * 128:(jq + 1) * 128], start=True, stop=False),
                  r=[vtok, "attm%d" % i2], w=["ps%d" % bY])
                A("pe", _c("matmul", ysl, lhsT=SFst[:, cc, :], rhs=qfb[i2][:, 0, jq * 128:(jq + 1) * 128], start=False, stop=False),
                  r=["SFst", "qfb%d" % i2], w=["ps%d" % bY])
                A("pe", _c("matmul", ysl, lhsT=SBst[:, cc, :], rhs=qfb[i2][:, 1, jq * 128:(jq + 1) * 128], start=False, stop=True),
                  r=["SBst", "qfb%d" % i2], w=["ps%d" % bY])
            A("act", _c("activation", out=ysq[i2], in_=psum[bY], func=AF.Square), r=["ps%d" % bY], w=["ysq%d" % i2])

        def o_s3a(i):
            i2 = i % 2
            A("pe", _c("matmul", psum[7], lhsT=ones, rhs=ysq[i2], start=True, stop=True), r=["ysq%d" % i2, "ones"], w=["ps7"])
            A("dve", _c("tensor_scalar", out=rstd[i2], in0=psum[7], scalar1=1.0 / 128, scalar2=EPS, op0=ALU.mult, op1=ALU.add), r=["ps7"], w=["rstd%d" % i2])
            A("act", _c("activation", out=rstd[i2], in_=rstd[i2], func=AF.Sqrt), r=["rstd%d" % i2], w=["rstd%d" % i2])

        def o_s3b(i):
            i2 = i % 2
            bY = 5 + i2
            tsl = slice(i * 512, (i + 1) * 512)
            A("dve", _c("reciprocal", rstd[i2], rstd[i2]), r=["rstd%d" % i2], w=["rstd%d" % i2])
            A("dve", _c("tensor_tensor", out=ynf, in0=psum[bY], in1=rstd[i2], op=ALU.mult), r=["ps%d" % bY, "rstd%d" % i2], w=["ynf"])
            A("dve", _c("tensor_tensor", out=yo[i2], in0=ynf, in1=gTh[:, tsl], op=ALU.mult), r=["ynf", gtok], w=["yo%d" % i2])
            A("sp", _c("dma_start", out=ymixT[h, :, tsl], in_=yo[i2]), r=["yo%d" % i2], w=["ymix"], dma=True)

        for i in range(G4 + 2):
            if i - 2 >= 0:
                o_s3a(i - 2)
            if i < G4:
                o_s1(i)
            if 0 <= i - 1 < G4:
                o_s2(i - 1)
            if i - 2 >= 0:
                o_s3b(i - 2)

    S.barrier()
    if stop <= 3:
        S.emit()
        return nc, S
    cur[0] = tables_end
    qA = [sb("qA%d" % i, [128, N], BF16) for i in range(2)]
    kA = [sb("kA%d" % i, [128, N + 256], BF16) for i in range(2)]
    vA = [sb("vA%d" % i, [128, NCH + 2, 128], BF16) for i in range(2)]
    BI = [sb("BI%d" % i, [128, 384], F32) for i in range(2)]
    relb = sb("relb", [128, 32 * HA], F32)
    EB = sb("EB", [128, 33, 384], F32)
    sS = [sb("sS%d" % i, [128, 4, 384], F32) for i in range(2)]
    pS = [sb("pS%d" % i, [128, 4, 384], F32) for i in range(2)]
    pn = [sb("pn%d" % i, [128, 4, 384], BF16) for i in range(2)]
    pT = [sb("pT%d" % i, [128, 1536], BF16) for i in range(2)]
    sm = [sb("sm%d" % i, [128, 24], F32) for i in range(2)]
    yoA = [sb("yoA%d" % i, [128, 512], BF16) for i in range(2)]
    load("sp", relb, relb_d, ["relb"])
    load("sp", EB, ebuck_d.rearrange("b p j -> p b j"), ["EB"])
    scale = 128.0 ** -0.5
    psS = psall[:, 0:2048].rearrange("p (b c) -> p b c", b=4)[:, :, 0:384]
    psTb = psall[:, 2048:3072].bitcast(BF16)
    G4 = NCH // 4

    BIp = sb("BIp", [128, 4, 384], F32)

    def a_loads(h):
        b = h % 2
        load("sp", qA[b], qaT[h], ["qA%d" % b], ["scr"])
        if h % 4 == 0:
            kb = (h // 4) % 2
            load("sp", kA[kb], kaT[h // 4], ["kA%d" % kb], ["scr"])
            load("sp", vA[kb], va[:, (h // 4) * 128:(h // 4 + 1) * 128].rearrange("(c p) e -> p c e", p=128), ["vA%d" % kb], ["scr"])
        bt = "BI%d" % b
        for bb in range(33):
            a_ = bb % 4
            sc_ = relb[:, bb * HA + h:bb * HA + h + 1] if bb < 32 else NEG
            if bb < 4:
                A("dve", _c("tensor_scalar", out=BIp[:, a_, :], in0=EB[:, bb, :], scalar1=sc_, scalar2=None, op0=ALU.mult), r=["EB", "relb"], w=["BIp%d" % a_])
            else:
                A("dve", _c("scalar_tensor_tensor", out=BIp[:, a_, :], in0=EB[:, bb, :], scalar=sc_, in1=BIp[:, a_, :], op0=ALU.mult, op1=ALU.add),
                  r=["EB", "relb", "BIp%d" % a_], w=["BIp%d" % a_])
        A("dve", _c("tensor_tensor", out=BIp[:, 0, :], in0=BIp[:, 0, :], in1=BIp[:, 1, :], op=ALU.add), r=["BIp0", "BIp1"], w=["BIp0"])
        A("dve", _c("tensor_tensor", out=BIp[:, 2, :], in0=BIp[:, 2, :], in1=BIp[:, 3, :], op=ALU.add), r=["BIp2", "BIp3"], w=["BIp2"])
        A("dve", _c("tensor_tensor", out=BI[b], in0=BIp[:, 0, :], in1=BIp[:, 2, :], op=ALU.add), r=["BIp0", "BIp2"], w=[bt])

    items = [(h, g) for h in range(HA) for g in range(G4)]

    def b_A(k):
        h, g = items[k]
        hb = h % 2
        kb = (h // 4) % 2
        for b in range(4):
            n = g * 4 + b
            A("pe", _c("matmul", psS[:, b, :], lhsT=qA[hb][:, n * 128:(n + 1) * 128], rhs=kA[kb][:, n * 128:n * 128 + 384], start=True, stop=True),
              r=["qA%d" % hb, "kA%d" % kb], w=["ps%d" % b])

    def b_B(k):
        h, g = items[k]
        i2 = k % 2
        hb = h % 2
        pst = ["ps0", "ps1", "ps2", "ps3"]
        st_ = "sS%d" % i2
        m = sm[i2]
        ops = []
        ops.append(("dve", _c("scalar_tensor_tensor", out=sS[i2], in0=psS, scalar=scale, in1=BI[hb].unsqueeze(1).to_broadcast([128, 4, 384]),
                              op0=ALU.mult, op1=ALU.add), pst + ["BI%d" % hb], [st_]))
        if g == 0:
            ops.append(("dve", _c("tensor_scalar", out=sS[i2][:, 0, 0:128], in0=sS[i2][:, 0, 0:128], scalar1=flags[:, 2:3], scalar2=None, op0=ALU.add),
                        [st_, "flags"], [st_]))
        if g == G4 - 1:
            ops.append(("dve", _c("tensor_scalar", out=sS[i2][:, 3, 256:384], in0=sS[i2][:, 3, 256:384], scalar1=flags[:, 3:4], scalar2=None, op0=ALU.add),
                        [st_, "flags"], [st_]))
        ops.append(("dve", _c("reduce_max", out=m[:, 0:4], in_=sS[i2], axis=AX.X), [st_], ["smx%d" % i2]))
        ops.append(("dve", _c("tensor_scalar", out=m[:, 4:8], in0=m[:, 0:4], scalar1=sink[:, h:h + 1], scalar2=-1.0, op0=ALU.max, op1=ALU.mult),
                    ["smx%d" % i2, "sink"], ["snm%d" % i2]))
        return ops

    def b_C(k):
        h, g = items[k]
        i2 = k % 2
        m = sm[i2]
        for b in range(4):
            A("act", _c("activation", out=pS[i2][:, b, :], in_=sS[i2][:, b, :], func=AF.Exp, bias=m[:, 4 + b:5 + b], accum_out=m[:, 8 + b:9 + b]),
              r=["sS%d" % i2, "snm%d" % i2], w=["pS%d_%d" % (i2, b), "sac%d_%d" % (i2, b)])
        A("act", _c("activation", out=m[:, 12:16], in_=m[:, 4:8], func=AF.Exp, bias=sink[:, h:h + 1]), r=["snm%d" % i2, "sink"], w=["ses%d" % i2])

    def b_D(k):
        i2 = k % 2
        m = sm[i2]
        ops = []
        ops.append(("dve", _c("tensor_tensor", out=m[:, 16:20], in0=m[:, 8:12], in1=m[:, 12:16], op=ALU.add),
                    ["sac%d_%d" % (i2, b) for b in range(4)] + ["ses%d" % i2], ["sdn%d" % i2]))
        ops.append(("dve", _c("reciprocal", m[:, 20:24], m[:, 16:20]), ["sdn%d" % i2], ["srd%d" % i2]))
        ops.append(("dve", _c("tensor_tensor", out=pn[i2], in0=pS[i2], in1=m[:, 20:24].unsqueeze(2).to_broadcast([128, 4, 384]), op=ALU.mult),
                    ["pS%d_%d" % (i2, b) for b in range(4)] + ["srd%d" % i2], ["pn%d" % i2]))
        return ops

    def b_EF(k):
        i2 = k % 2
        for b in range(4):
            for t in range(3):
                q = b * 3 + t
                A("pe", _c("transpose", psTb[:, q * 128:(q + 1) * 128], pn[i2][:, b, t * 128:(t + 1) * 128], ident), r=["pn%d" % i2, "ident"], w=["ps4", "ps5"])
        A("act", _c("activation", out=pT[i2], in_=psTb[:, 0:1536], func=AF.Copy), r=["ps4", "ps5"], w=["pT%d" % i2])

    def b_GH(k):
        h, g = items[k]
        i2 = k % 2
        kb = (h // 4) % 2
        bO = 6 + i2
        for b in range(4):
            n = g * 4 + b
            for t in range(3):
                q = b * 3 + t
                A("pe", _c("matmul", psum[bO][:, b * 128:(b + 1) * 128], lhsT=vA[kb][:, n + t, :], rhs=pT[i2][:, q * 128:(q + 1) * 128],
                           start=(t == 0), stop=(t == 2)), r=["vA%d" % kb, "pT%d" % i2], w=["ps%d" % bO])
        A("act", _c("activation", out=yoA[i2], in_=psum[bO], func=AF.Copy), r=["ps%d" % bO], w=["yoA%d" % i2])
        A("sp", _c("dma_start", out=ymixT[HR + h, :, g * 512:(g + 1) * 512], in_=yoA[i2]), r=["yoA%d" % i2], w=["ymix"], dma=True)

    def interleave(l1, l2):
        out = []
        for i in range(max(len(l1), len(l2))):
            if i < len(l1):
                out.append(l1[i])
            if i < len(l2):
                out.append(l2[i])
        return out

    a_loads(0)
    for k in range(len(items) + 2):
        lb, ld = [], []
        if k < len(items):
            h, g = items[k]
            if g == 0 and h + 1 < HA:
                a_loads(h + 1)
            b_A(k)
            lb = b_B(k)
        if 0 <= k - 1 < len(items):
            ld = b_D(k - 1)
        for (eng_, fn_, r_, w_) in interleave(lb, ld):
            A(eng_, fn_, r=r_, w=w_)
        if k < len(items):
            b_C(k)
        if 0 <= k - 1 < len(items):
            b_EF(k - 1)
        if k - 2 >= 0:
            b_GH(k - 2)

    S.barrier()
    if stop <= 4:
        S.emit()
        return nc, S
    cur[0] = p0
    NST = T // 128
    xacc = [sb("xacc%d" % i, [128, D], F32) for i in range(NST)]
    hT4 = sb("hT4", [128, KC, T], BF16)
    FCH = F // 128
    aT = [sb("aT%d" % i, [128, FCH, T], BF16) for i in range(2)]
    rl = [sb("rl%d" % i, [128, T], F32) for i in range(2)]
    qc = sb("qc", [128, 4, T], BF16)
    oT = sb("oT", [128, 4, T], BF16)
    kmS = sb("kmS", [128, 4, 256], BF16)
    vmS = sb("vmS", [128, 2, 512], BF16)
    hn4b = [sb("hn4_%d" % i, [128, D], BF16) for i in range(2)]
    junkS = sb("junkS", [128, D // 8], BF16)
    st4 = sb("st4", [128, 16], F32)
    h4c = [0]
    cS = [sb("cS%d" % i, [128, 256], F32) for i in range(2)]
    cP = [sb("cP%d" % i, [128, 256], BF16) for i in range(2)]
    cT = [sb("cT%d" % i, [128, 256], BF16) for i in range(2)]
    cm = [sb("cm%d" % i, [128, 8], F32) for i in range(2)]
    SLOT = 16384
    nring = (top - cur[0]) // SLOT
    nring = min(nring, 4)
    assert nring >= 2, ("ring too small", nring)
    print("P4 ring slots", nring, "free bytes", top - cur[0])
    ring = [sb("ring%d" % i, [128, SLOT // 2], BF16) for i in range(nring)]
    rc = [0]

    def ring_load(src_ap, shape):
        i = rc[0] % nring
        rc[0] += 1
        a, b = shape
        v = ring[i][:, 0:a * b].rearrange("p (a b) -> p a b", a=a)
        load("sp", v, src_ap, ["ring%d" % i], [src_ap.tensor.name])
        return v, "ring%d" % i

    load("sp", kmS, kmT.rearrange("h p t -> p h t"), ["kmS"], ["scr"])
    load("sp", vmS, vm.rearrange("(c p) e -> p c e", p=128), ["vmS"], ["scr"])
    acnt = [0]
    PW = min(512, F, SLOT // 2 // KC)
    assert PW >= 128

    def stats4(st, dcol):
        q4 = D // 8
        so = (dcol % 2) * 8
        for q in range(8):
            A("act", _c("activation", out=junkS, in_=xacc[st][:, q * q4:(q + 1) * q4], func=AF.Square, accum_out=st4[:, so + q:so + q + 1]),
              r=["xacc%d" % st], w=["junkS", "st4_%d" % (dcol % 2)])
        A("dve", _c("reduce_sum", out=stat[:, dcol:dcol + 1], in_=st4[:, so:so + 8], axis=AX.X), r=["st4_%d" % (dcol % 2)], w=["stat%d" % dcol])
        A("dve", _c("tensor_scalar", out=stat[:, dcol:dcol + 1], in0=stat[:, dcol:dcol + 1], scalar1=1.0 / D, scalar2=EPS,
                    op0=ALU.mult, op1=ALU.add), r=["stat%d" % dcol], w=["stat%d" % dcol])
        A("act", _c("activation", out=stat[:, dcol:dcol + 1], in_=stat[:, dcol:dcol + 1], func=AF.Sqrt), r=["stat%d" % dcol], w=["stat%d" % dcol])
        A("dve", _c("reciprocal", stat[:, dcol:dcol + 1], stat[:, dcol:dcol + 1]), r=["stat%d" % dcol], w=["stat%d" % dcol])

    def norm_T(gi):
        for st in range(NST):
            dcol = st % 4
            stats4(st, dcol)
            hi = h4c[0] % 2
            h4c[0] += 1
            hn4 = hn4b[hi]
            hntok = "hn4_%d" % hi
            A("dve", _c("tensor_scalar", out=hn4, in0=xacc[st], scalar1=stat[:, dcol:dcol + 1], scalar2=None, op0=ALU.mult),
              r=["xacc%d" % st, "stat%d" % dcol], w=[hntok])
            for g8 in range(KC // 8):
                bk = acnt[0] % 2
                acnt[0] += 1
                pt = psb(bk)
                for j in range(8):
                    kc = g8 * 8 + j
                    A("pe", _c("transpose", pt[:, j * 128:(j + 1) * 128], hn4[:, kc * 128:(kc + 1) * 128], ident),
                      r=[hntok, "ident"], w=["ps%d" % bk])
                ptv = pt.rearrange("p (k t) -> p k t", k=8)
                gb = gcols[:, gi, g8 * 8:(g8 + 1) * 8].unsqueeze(2).to_broadcast([128, 8, 128])
                A("dve", _c("tensor_tensor", out=hT4[:, g8 * 8:(g8 + 1) * 8, st * 128:(st + 1) * 128], in0=ptv,
                            in1=gb, op=ALU.mult), r=["ps%d" % bk, "gcols"], w=["hT4"])

    pcnt = [0]

    def tm_accumulate(wb_ap, KCn, lhs_fn, lhs_toks, first_add_src=None):
        pw = (SLOT // 2) // KCn
        pw = min(pw, D)
        pw = (pw // 512) * 512 if pw >= 512 else pw
        for c0 in range(0, D, pw):
            v, rtok = ring_load(wb_ap[:, c0:c0 + pw].rearrange("(kc p) n -> p kc n", p=128), (KCn, pw))
            for st in range(NST):
                for cb in range(0, pw, 512):
                    w_ = min(512, pw - cb)
                    bk = 2 + pcnt[0] % 3
                    pcnt[0] += 1
                    for kc in range(KCn):
                        A("pe", _c("matmul", psum[bk][:, 0:w_], lhsT=lhs_fn(kc, st), rhs=v[:, kc, cb:cb + w_],
                                                                                      start=(kc == 0), stop=(kc == KCn - 1)),
                          r=list(lhs_toks) + [rtok], w=["ps%d" % bk])
                    xs = xacc[st][:, c0 + cb:c0 + cb + w_]
                    A("dve", _c("tensor_tensor", out=xs, in0=psum[bk][:, 0:w_], in1=xs, op=ALU.add),
                      r=["ps%d" % bk, "xacc%d" % st], w=["xacc%d" % st])

    for ti in range(N // T):
        t0 = ti * T
        load("sp", hT4, ymixT[:, :, t0:t0 + T].rearrange("c p t -> p c t"), ["hT4"], ["ymix"])
        for st in range(NST):
            load("sp", xacc[st], x_own[t0 + st * 128:t0 + (st + 1) * 128, :], ["xacc%d" % st])
        tm_accumulate(w_out_b, KC, lambda kc, st: hT4[:, kc, st * 128:(st + 1) * 128], ["hT4"])
        norm_T(1)
        for c0 in range(0, 512, PW):
            v, rtok = ring_load(w_cq_b[:, c0:c0 + PW].rearrange("(kc p) n -> p kc n", p=128), (KC, PW))
            for ch in range(PW // 128):
                hd = (c0 + ch * 128) // 128
                bk = 2 + pcnt[0] % 3
                pcnt[0] += 1
                for kc in range(KC):
                    A("pe", _c("matmul", psum[bk][:, 0:T], lhsT=v[:, kc, ch * 128:(ch + 1) * 128], rhs=hT4[:, kc, :],
                                                                       start=(kc == 0), stop=(kc == KC - 1)), r=["hT4", rtok], w=["ps%d" % bk])
                A("act", _c("activation", out=qc[:, hd, :], in_=psum[bk][:, 0:T], func=AF.Copy), r=["ps%d" % bk], w=["qc"])
        for hd in range(4):
            for st in range(NST):
                i2 = (hd * NST + st) % 2
                bS = 5
                A("pe", _c("matmul", psum[bS][:, 0:256], lhsT=qc[:, hd, st * 128:(st + 1) * 128], rhs=kmS[:, hd, :], start=True, stop=True),
                  r=["qc", "kmS"], w=["ps%d" % bS])
                m = cm[i2]
                mt = "cm%d" % i2
                A("dve", _c("reduce_max", out=m[:, 0:1], in_=psum[bS][:, 0:256], axis=AX.X), r=["ps%d" % bS], w=[mt])
                A("dve", _c("tensor_scalar", out=m[:, 1:2], in0=m[:, 0:1], scalar1=-scale, scalar2=None, op0=ALU.mult), r=[mt], w=[mt])
                A("act", _c("activation", out=cS[i2], in_=psum[bS][:, 0:256], func=AF.Exp, bias=m[:, 1:2], scale=scale, accum_out=m[:, 2:3]),
                  r=["ps%d" % bS, mt], w=["cS%d" % i2, mt])
                A("dve", _c("reciprocal", m[:, 3:4], m[:, 2:3]), r=[mt], w=[mt])
                A("pool", _c("tensor_scalar", out=cP[i2], in0=cS[i2], scalar1=m[:, 3:4], scalar2=None, op0=ALU.mult),
                  r=["cS%d" % i2, mt], w=["cP%d" % i2])
                bT = 6
                ptv = psb(bT)
                for t in range(2):
                    A("pe", _c("transpose", ptv[:, t * 128:(t + 1) * 128], cP[i2][:, t * 128:(t + 1) * 128], ident),
                      r=["cP%d" % i2, "ident"], w=["ps%d" % bT])
                A("act", _c("activation", out=cT[i2], in_=ptv[:, 0:256], func=AF.Copy), r=["ps%d" % bT], w=["cT%d" % i2])
                bO = 7
                for t in range(2):
                    A("pe", _c("matmul", psum[bO][:, st * 128:(st + 1) * 128], lhsT=vmS[:, t, hd * 128:(hd + 1) * 128],
                                                                       rhs=cT[i2][:, t * 128:(t + 1) * 128], start=(t == 0), stop=(t == 1)),
                      r=["vmS", "cT%d" % i2], w=["ps%d" % bO])
            A("act", _c("activation", out=oT[:, hd, :], in_=psum[7][:, 0:T], func=AF.Copy), r=["ps7"], w=["oT"])
        tm_accumulate(w_co_b, 4, lambda kc, st: oT[:, kc, st * 128:(st + 1) * 128], ["oT"])
        norm_T(3)
        NFB = DFF // F

        def mlp_s1(fb):
            ai = fb % 2
            for c0 in range(0, F, PW):
                v, rtok = ring_load(w1_b[:, fb * F + c0:fb * F + c0 + PW].rearrange("(kc p) n -> p kc n", p=128), (KC, PW))
                for ch in range(PW // 128):
                    dc = (c0 + ch * 128) // 128
                    bk = 2 + pcnt[0] % 3
                    pcnt[0] += 1
                    for kc in range(KC):
                        A("pe", _c("matmul", psum[bk][:, 0:T], lhsT=v[:, kc, ch * 128:(ch + 1) * 128], rhs=hT4[:, kc, :],
                                   start=(kc == 0), stop=(kc == KC - 1)), r=["hT4", rtok], w=["ps%d" % bk])
                    ri = dc % 2
                    A("act", _c("activation", out=rl[ri], in_=psum[bk][:, 0:T], func=AF.Relu), r=["ps%d" % bk], w=["rl%d" % ri])
                    A("pool", _c("tensor_tensor", out=aT[ai][:, dc, :], in0=rl[ri], in1=rl[ri], op=ALU.mult),
                      r=["rl%d" % ri], w=["aT%d" % ai])

        def mlp_s2(fb):
            ai = fb % 2
            tm_accumulate(w2_b[fb * F:(fb + 1) * F, :], FCH, lambda kc, st, ai=ai: aT[ai][:, kc, st * 128:(st + 1) * 128], ["aT%d" % ai])

        mlp_s1(0)
        for fb in range(NFB):
            if fb + 1 < NFB:
                mlp_s1(fb + 1)
            mlp_s2(fb)
        gi_ = rc[0] % nring
        rc[0] += 1
        gview = ring[gi_].bitcast(F32)[:, 0:D]
        load("sp", gview, gfin_d, ["ring%d" % gi_])
        for st in range(NST):
            dcol = 4 + st % 4
            stats4(st, dcol)
            A("dve", _c("scalar_tensor_tensor", out=xacc[st], in0=xacc[st], scalar=stat[:, dcol:dcol + 1], in1=gview,
                        op0=ALU.mult, op1=ALU.mult), r=["xacc%d" % st, "stat%d" % dcol, "ring%d" % gi_], w=["xacc%d" % st])
            A("pool", _c("dma_start", out=y_out[t0 + st * 128:t0 + (st + 1) * 128, :], in_=xacc[st]),
              r=["xacc%d" % st], w=["yout"], dma=True)

    S.emit()
    return nc, S


def core_inputs(cfg, inputs, seqs):
    c = cfg
    N, D, KC, HR, HA = c.N, c.D, c.KC, c.HR, c.HA
    st = static_tables()
    f32 = np.float32

    def colmajor(g):
        return np.ascontiguousarray(np.asarray(g, f32).reshape(KC, 128).T)

    gcols = np.ascontiguousarray(np.stack([colmajor(inputs["norm_mix"][0]), colmajor(inputs["norm_cross"][0]),
                                           colmajor(inputs["norm_mem"][0]), colmajor(inputs["norm_mlp"][0])], 1))
    gfin = np.ascontiguousarray(np.broadcast_to(np.asarray(inputs["norm_final"], f32)[None, :], (128, D)))
    dec = np.concatenate([np.asarray(inputs["ret_decay_f"][0], f32), np.asarray(inputs["ret_decay_b"][0], f32)])
    dec = np.ascontiguousarray(np.broadcast_to(dec[None], (128, 2 * HR)))
    sink = np.ascontiguousarray(np.broadcast_to(np.asarray(inputs["attn_sink"][0], f32)[None], (128, HA)))
    relb = np.ascontiguousarray(np.broadcast_to(np.asarray(inputs["rel_bias"], f32).reshape(1, -1), (128, 32 * HA)))
    half = 64
    inv = (np.float32(10000.0) ** (-np.arange(half, dtype=f32) / np.float32(half))).astype(f32)

    def cs_table(p0):
        pos = np.arange(p0, p0 + N, dtype=f32)
        ang = (pos[:, None] * inv[None, :]).astype(f32)
        co = np.cos(ang).astype(f32).T
        si = np.sin(ang).astype(f32).T
        return np.ascontiguousarray(np.stack([np.concatenate([co, co], 0), np.concatenate([-si, si], 0)], 0))

    shared = dict(w_in=inputs["w_in"][0], w_out=inputs["w_out"][0], w_cq=inputs["w_cq"][0], w_ckv=inputs["w_ckv"][0],
                  w_co=inputs["w_co"][0], w1=inputs["w_mlp_in"][0], w2=inputs["w_mlp_out"][0], gcols=gcols, gfin=gfin,
                  dec=dec, sink=sink, relb=relb, **st)
    maps = []
    zeros_oth = np.zeros((N, D), f32)
    for (xarr, b, start, marr) in seqs:
        L = xarr.shape[1]
        m = dict(shared)
        m["x_own"] = np.ascontiguousarray(xarr[b, start:start + N])
        halo = np.zeros((256, D), f32)
        fa = fb = 0.0
        lneg = rneg = NEG
        oth = zeros_oth
        p_oth = 0
        if start > 0:
            halo[0:128] = xarr[b, start - 128:start]
            lneg = 0.0
            fb = 1.0
            oth = np.ascontiguousarray(xarr[b, start - N:start])
            p_oth = start - N
        if start + N < L:
            halo[128:256] = xarr[b, start + N:start + N + 128]
            rneg = 0.0
            fa = 1.0
            oth = np.ascontiguousarray(xarr[b, start + N:start + 2 * N])
            p_oth = start + N
        m["x_oth"] = oth
        m["x_halo"] = halo
        m["mem"] = np.ascontiguousarray(marr[b])
        m["cs_own"] = cs_table(start)
        m["cs_oth"] = cs_table(p_oth)
        m["flags"] = np.ascontiguousarray(np.broadcast_to(np.array([fa, fb, lneg, rneg], f32)[None], (128, 4)))
        maps.append(m)
    return maps


_CACHE = {}


def run(cfg, inputs, debug_outs=(), trace=False, stop=99):
    xp = np.asarray(inputs["x_prompt"], np.float32)
    xs = np.asarray(inputs["x_sample"], np.float32)
    mp = np.asarray(inputs["mem_prompt"], np.float32)
    ms = np.asarray(inputs["mem_sample"], np.float32)
    N = cfg.N
    seqs = [(xp, b, 0, mp) for b in range(4)] + [(xs, b, hh * N, ms) for b in range(2) for hh in range(2)]
    maps = core_inputs(cfg, inputs, seqs)
    key = (cfg.D, cfg.N, cfg.DFF, cfg.T, cfg.F, tuple(debug_outs))
    nc, S = build_nc(cfg, debug_outs, stop)
    res = run_bass_kernel_spmd(nc, maps, core_ids=list(range(8)), **({"trace": True} if trace else {}))
    yp = np.stack([res.results[b]["y"] for b in range(4)], 0)
    ysm = np.stack([np.concatenate([res.results[4 + 2 * b]["y"], res.results[5 + 2 * b]["y"]], 0) for b in range(2)], 0)
    return (yp.astype(np.float32), ysm.astype(np.float32)), res, S


def kernel(**inputs):
    cfg = Cfg()
    (yp, ysm), _, _ = run(cfg, inputs)
    return (yp, ysm)
```
